# Optimizing a Trainium2 kernel written in Bass

```python
import math
import jax, jax.numpy as jnp
from jax import lax
import numpy as np

D_MODEL = 1024
BATCH = 8
SEQ = 4096
DEPTH = 2

N_EVEN = (DEPTH + 1) // 2
N_ODD = DEPTH // 2

DA_HEADS = 4
DA_DIM = 64
DA_VDIM = 2 * DA_DIM
RET_HEADS = 4
RET_QK = 64
RET_V = 128
RET_CHUNK = 128
RET_THETA = 10000.0
GLA_HEADS = 4
GLA_QK = 128
GLA_V = 256
GLA_RANK = 16
GLA_TAU = 16.0
GLA_CHUNK = 64
D_FF = 2816
ROPE_THETA = 500000.0
ROPE_FRAC = 4
Q_BLOCK = 128
EPS = 1e-6

AB_SPLITS = (DA_HEADS * 2 * DA_DIM, DA_HEADS * 2 * DA_DIM, DA_HEADS * DA_VDIM,
             RET_HEADS * RET_QK, RET_HEADS * RET_QK, RET_HEADS * RET_V, RET_HEADS * RET_V)
AB_IN = sum(AB_SPLITS)
AB_OUT = DA_HEADS * DA_VDIM + RET_HEADS * RET_V
C_SPLITS = (GLA_HEADS * GLA_QK, GLA_HEADS * GLA_QK, GLA_HEADS * GLA_V, GLA_HEADS * GLA_V,
            GLA_RANK, GLA_RANK)
C_IN = sum(C_SPLITS)
C_OUT = GLA_HEADS * GLA_V

kernel_name = "hybrid_diffattn_retention_gla_macaron"


def _split(t, widths):
    idx = np.cumsum(widths)[:-1].tolist()
    return jnp.split(t, idx, axis=-1)


def _heads(t, n):
    b, s, _ = t.shape
    return t.reshape(b, s, n, -1).transpose(0, 2, 1, 3)


def _merge_heads(t):
    b, n, s, d = t.shape
    return t.transpose(0, 2, 1, 3).reshape(b, s, n * d)


def rmsnorm(x, g):
    xf = x.astype(jnp.float32)
    y = xf * lax.rsqrt(jnp.mean(xf * xf, axis=-1, keepdims=True) + EPS)
    return (y * g.astype(jnp.float32)).astype(x.dtype)


def rope_tables(positions, dim, theta):
    inv = 1.0 / (theta ** (jnp.arange(0, dim, 2, dtype=jnp.float32) / dim))
    ang = positions.astype(jnp.float32)[:, None] * inv[None, :]
    return jnp.cos(ang), jnp.sin(ang)


def apply_rope(x, cos, sin):
    half = x.shape[-1] // 2
    x1, x2 = x[..., :half], x[..., half:]
    c, s = cos.astype(x.dtype), sin.astype(x.dtype)
    return jnp.concatenate([x1 * c - x2 * s, x1 * s + x2 * c], axis=-1)


def apply_partial_rope(x, cos, sin):
    r = x.shape[-1] // ROPE_FRAC
    return jnp.concatenate([apply_rope(x[..., :r], cos, sin), x[..., r:]], axis=-1)


def swiglu(x, w_gate, w_up, w_down):
    return (jax.nn.silu(x @ w_gate) * (x @ w_up)) @ w_down


def diff_attention(q, k, v, lam):
    b, h, _, s, d = q.shape
    nb = s // Q_BLOCK
    q = q * (d ** -0.5)
    qb = q.reshape(b, h, 2, nb, Q_BLOCK, d).transpose(3, 0, 1, 2, 4, 5)

    def block(qblk):
        sc = jnp.einsum('bhcqd,bhckd->bhcqk', qblk, k).astype(jnp.float32)
        p = jax.nn.softmax(sc, axis=-1)
        a = p[:, :, 0] - lam * p[:, :, 1]
        return jnp.einsum('bhqk,bhkv->bhqv', a.astype(v.dtype), v)

    o = lax.map(block, qb)
    return o.transpose(1, 2, 0, 3, 4).reshape(b, h, s, v.shape[-1])


def retention_dir(q, k, v, log_gamma):
    b, h, s, dk = q.shape
    dv = v.shape[-1]
    c = RET_CHUNK
    n = s // c
    qc = q.reshape(b, h, n, c, dk)
    kc = k.reshape(b, h, n, c, dk)
    vc = v.reshape(b, h, n, c, dv)
    idx = jnp.arange(c, dtype=jnp.float32)
    lg = log_gamma.astype(jnp.float32)
    diff = idx[:, None] - idx[None, :]
    intra_decay = jnp.where(diff >= 0, jnp.exp(lg[:, None, None] * jnp.maximum(diff, 0.0)[None]), 0.0)
    sc = jnp.einsum('bhncd,bhnjd->bhncj', qc, kc) * intra_decay[None, :, None]
    o_intra = jnp.einsum('bhncj,bhnjv->bhncv', sc, vc)
    k_dec = jnp.exp(lg[:, None] * (c - 1.0 - idx)[None, :])
    q_dec = jnp.exp(lg[:, None] * (idx + 1.0)[None, :])
    kv = jnp.einsum('bhnjd,bhnjv->nbhdv', kc * k_dec[None, :, None, :, None], vc)
    chunk_decay = jnp.exp(lg * c)[None, :, None, None]

    def step(state, kv_n):
        return chunk_decay * state + kv_n, state

    _, states = lax.scan(step, jnp.zeros(kv.shape[1:], kv.dtype), kv)
    o_inter = jnp.einsum('bhncd,nbhdv->bhncv', qc * q_dec[None, :, None, :, None], states)
    return (o_intra + o_inter).reshape(b, h, s, dv).astype(v.dtype)


def gla_dir(q, k, v, log_a):
    b, h, s, dk = q.shape
    dv = v.shape[-1]
    c = GLA_CHUNK
    n = s // c
    qc = q.reshape(b, h, n, c, dk).astype(jnp.float32)
    kc = k.reshape(b, h, n, c, dk).astype(jnp.float32)
    vc = v.reshape(b, h, n, c, dv).astype(jnp.float32)
    cum = jnp.cumsum(log_a.reshape(b, h, n, c, dk), axis=3)
    last = cum[:, :, :, -1:]
    q_g = qc * jnp.exp(cum)
    sc = jnp.einsum('bhncd,bhnjd->bhncj', q_g, kc * jnp.exp(-cum))
    mask = jnp.tril(jnp.ones((c, c), dtype=bool))
    o_intra = jnp.einsum('bhncj,bhnjv->bhncv', jnp.where(mask, sc, 0.0), vc)
    kv = jnp.einsum('bhnjd,bhnjv->nbhdv', kc * jnp.exp(last - cum), vc)
    dec = jnp.exp(last[:, :, :, 0]).transpose(2, 0, 1, 3)[..., None]

    def step(state, inp):
        kv_n, dec_n = inp
        return dec_n * state + kv_n, state

    _, states = lax.scan(step, jnp.zeros(kv.shape[1:], jnp.float32), (kv, dec))
    o_inter = jnp.einsum('bhncd,nbhdv->bhncv', q_g, states)
    return (o_intra + o_inter).reshape(b, h, s, dv).astype(v.dtype)


def _flip(t):
    return jnp.flip(t, axis=2)


def mixer_ab(h, w_in, lq1, lk1, lq2, lk2, da_norm, ret_logit_f, ret_logit_b, ret_norm, w_out,
             cos_a, sin_a, cos_r, sin_r, lam_init):
    b, s, _ = h.shape
    qa, ka, va, qr, kr, vr, gr = _split(h @ w_in, AB_SPLITS)
    qa = qa.reshape(b, s, DA_HEADS, 2, DA_DIM).transpose(0, 2, 3, 1, 4)
    ka = ka.reshape(b, s, DA_HEADS, 2, DA_DIM).transpose(0, 2, 3, 1, 4)
    qa = apply_partial_rope(qa, cos_a, sin_a)
    ka = apply_partial_rope(ka, cos_a, sin_a)
    va = _heads(va, DA_HEADS)
    lam = (jnp.exp(jnp.sum(lq1.astype(jnp.float32) * lk1.astype(jnp.float32)))
           - jnp.exp(jnp.sum(lq2.astype(jnp.float32) * lk2.astype(jnp.float32))) + lam_init)
    oa = diff_attention(qa, ka, va, lam)
    oa = rmsnorm(oa, da_norm) * (1.0 - lam_init)
    qr = apply_rope(_heads(qr, RET_HEADS), cos_r, sin_r)
    kr = apply_rope(_heads(kr, RET_HEADS), cos_r, sin_r) * (RET_QK ** -0.5)
    vr = _heads(vr, RET_HEADS)
    lg_f = jax.nn.log_sigmoid(ret_logit_f.astype(jnp.float32))
    lg_b = jax.nn.log_sigmoid(ret_logit_b.astype(jnp.float32))
    orr = retention_dir(qr, kr, vr, lg_f) + _flip(retention_dir(_flip(qr), _flip(kr), _flip(vr), lg_b))
    orr = _merge_heads(rmsnorm(orr, ret_norm)) * jax.nn.silu(gr)
    return jnp.concatenate([_merge_heads(oa), orr], axis=-1) @ w_out


def mixer_c(h, w_in, w2_f, b_f, w2_b, b_b, gla_norm, w_out):
    q, k, v, g, lr_f, lr_b = _split(h @ w_in, C_SPLITS)
    q = _heads(q, GLA_HEADS) * (GLA_QK ** -0.5)
    k = _heads(k, GLA_HEADS)
    v = _heads(v, GLA_HEADS)
    la_f = _heads(jax.nn.log_sigmoid((lr_f @ w2_f + b_f).astype(jnp.float32)) / GLA_TAU, GLA_HEADS)
    la_b = _heads(jax.nn.log_sigmoid((lr_b @ w2_b + b_b).astype(jnp.float32)) / GLA_TAU, GLA_HEADS)
    o = gla_dir(q, k, v, la_f) + _flip(gla_dir(_flip(q), _flip(k), _flip(v), _flip(la_b)))
    o = _merge_heads(rmsnorm(o, gla_norm)) * jax.nn.silu(g)
    return o @ w_out


def setup_inputs(seed: int = 0) -> dict:
    key = jax.random.key(seed)
    ks = iter(jax.random.split(key, 40))

    def nrm(shape, scale):
        return jax.random.normal(next(ks), shape, jnp.float32) * scale

    def gain(shape):
        return 1.0 + nrm(shape, 0.02)

    gammas = 1.0 - 2.0 ** (-5.0 - np.arange(RET_HEADS, dtype=np.float64))
    ret_base = jnp.asarray(np.log(gammas / (1.0 - gammas)).astype(np.float32))
    return {
        "x": nrm((BATCH, SEQ, D_MODEL), 1.0),
        "positions": jnp.arange(SEQ, dtype=jnp.int32),
        "ffn1_norm": gain((DEPTH, D_MODEL)),
        "ffn1_w_gate": nrm((DEPTH, D_MODEL, D_FF), D_MODEL ** -0.5),
        "ffn1_w_up": nrm((DEPTH, D_MODEL, D_FF), D_MODEL ** -0.5),
        "ffn1_w_down": nrm((DEPTH, D_FF, D_MODEL), D_FF ** -0.5),
        "ffn2_norm": gain((DEPTH, D_MODEL)),
        "ffn2_w_gate": nrm((DEPTH, D_MODEL, D_FF), D_MODEL ** -0.5),
        "ffn2_w_up": nrm((DEPTH, D_MODEL, D_FF), D_MODEL ** -0.5),
        "ffn2_w_down": nrm((DEPTH, D_FF, D_MODEL), D_FF ** -0.5),
        "ab_norm": gain((N_EVEN, D_MODEL)),
        "ab_w_in": nrm((N_EVEN, D_MODEL, AB_IN), D_MODEL ** -0.5),
        "da_lq1": nrm((N_EVEN, DA_DIM), 0.1),
        "da_lk1": nrm((N_EVEN, DA_DIM), 0.1),
        "da_lq2": nrm((N_EVEN, DA_DIM), 0.1),
        "da_lk2": nrm((N_EVEN, DA_DIM), 0.1),
        "da_norm": gain((N_EVEN, DA_VDIM)),
        "ret_logit_f": ret_base[None, :] + nrm((N_EVEN, RET_HEADS), 0.1),
        "ret_logit_b": ret_base[None, :] + nrm((N_EVEN, RET_HEADS), 0.1),
        "ret_norm": gain((N_EVEN, RET_V)),
        "ab_w_out": nrm((N_EVEN, AB_OUT, D_MODEL), AB_OUT ** -0.5),
        "c_norm": gain((N_ODD, D_MODEL)),
        "c_w_in": nrm((N_ODD, D_MODEL, C_IN), D_MODEL ** -0.5),
        "gla_w2_f": nrm((N_ODD, GLA_RANK, GLA_HEADS * GLA_QK), GLA_RANK ** -0.5),
        "gla_b_f": nrm((N_ODD, GLA_HEADS * GLA_QK), 0.1),
        "gla_w2_b": nrm((N_ODD, GLA_RANK, GLA_HEADS * GLA_QK), GLA_RANK ** -0.5),
        "gla_b_b": nrm((N_ODD, GLA_HEADS * GLA_QK), 0.1),
        "gla_norm": gain((N_ODD, GLA_V)),
        "c_w_out": nrm((N_ODD, C_OUT, D_MODEL), C_OUT ** -0.5),
        "final_norm": gain((D_MODEL,)),
    }


def reference(x, positions, ffn1_norm, ffn1_w_gate, ffn1_w_up, ffn1_w_down,
              ffn2_norm, ffn2_w_gate, ffn2_w_up, ffn2_w_down,
              ab_norm, ab_w_in, da_lq1, da_lk1, da_lq2, da_lk2, da_norm,
              ret_logit_f, ret_logit_b, ret_norm, ab_w_out,
              c_norm, c_w_in, gla_w2_f, gla_b_f, gla_w2_b, gla_b_b, gla_norm, c_w_out,
              final_norm):
    cos_a, sin_a = rope_tables(positions, DA_DIM // ROPE_FRAC, ROPE_THETA)
    cos_r, sin_r = rope_tables(positions, RET_QK, RET_THETA)
    for layer in range(DEPTH):
        i = layer // 2
        x = x + 0.5 * swiglu(rmsnorm(x, ffn1_norm[layer]), ffn1_w_gate[layer], ffn1_w_up[layer], ffn1_w_down[layer])
        if layer % 2 == 0:
            lam_init = 0.8 - 0.6 * math.exp(-0.3 * layer)
            x = x + mixer_ab(rmsnorm(x, ab_norm[i]), ab_w_in[i], da_lq1[i], da_lk1[i], da_lq2[i], da_lk2[i],
                             da_norm[i], ret_logit_f[i], ret_logit_b[i], ret_norm[i], ab_w_out[i],
                             cos_a, sin_a, cos_r, sin_r, lam_init)
        else:
            x = x + mixer_c(rmsnorm(x, c_norm[i]), c_w_in[i], gla_w2_f[i], gla_b_f[i], gla_w2_b[i], gla_b_b[i],
                            gla_norm[i], c_w_out[i])
        x = x + 0.5 * swiglu(rmsnorm(x, ffn2_norm[layer]), ffn2_w_gate[layer], ffn2_w_up[layer], ffn2_w_down[layer])
    return rmsnorm(x, final_norm)
```

```python
import math
from contextlib import ExitStack

import numpy as np
import ml_dtypes

import concourse.bass as bass
import concourse.mybir as mybir
from concourse.bass_utils import run_bass_kernel_spmd

F32 = mybir.dt.float32
BF16 = mybir.dt.bfloat16
I32 = mybir.dt.int32
AF = mybir.ActivationFunctionType
ALU = mybir.AluOpType

D = 1024
DFF = 2816
NF = DFF // 128
EPS = 1e-6


class Op:
    __slots__ = ("eng", "fn", "reads", "writes", "dma", "deps", "signal", "sigidx", "grp", "cum", "idx")

    def __init__(self, eng, fn, reads, writes, dma, grp):
        self.eng = eng
        self.fn = fn
        self.reads = reads
        self.writes = writes
        self.dma = dma
        self.grp = grp
        self.deps = []
        self.signal = False
        self.sigidx = 0
        self.cum = 0


class Prog:
    ENGS = ("pe", "act", "dve", "pool", "sp")

    def __init__(self, nc, tag=0):
        self.nc = nc
        self.tag = tag
        self.ops = []
        self.last_writer = {}
        self.readers = {}
        self.grp_cum = {}

    def _add(self, eng, fn, reads, writes, dma=False, grp=None):
        op = Op(eng, fn, tuple(reads), tuple(writes), dma, grp)
        op.idx = len(self.ops)
        deps = {}
        for r in op.reads:
            w = self.last_writer.get(r)
            if w is not None:
                deps[id(w)] = (w, "raw")
        for wkey in op.writes:
            w = self.last_writer.get(wkey)
            if w is not None and id(w) not in deps:
                deps[id(w)] = (w, "waw")
            for rd in self.readers.get(wkey, ()):
                if id(rd) not in deps:
                    deps[id(rd)] = (rd, "war")
        best = {}
        for d, kind in deps.values():
            if d is op:
                continue
            if not d.dma and d.eng == eng and not dma:
                if eng == "pe" or kind != "raw":
                    continue
            key = ("g", d.grp) if d.dma else ("e", d.eng)
            cur = best.get(key)
            if cur is None or d.idx > cur.idx:
                best[key] = d
        for d in best.values():
            op.deps.append(d)
            if not d.dma:
                d.signal = True
        if dma:
            c = self.grp_cum.get(grp, 0) + 16
            self.grp_cum[grp] = c
            op.cum = c
        for r in op.reads:
            self.readers.setdefault(r, []).append(op)
        for wkey in op.writes:
            self.last_writer[wkey] = op
            self.readers[wkey] = []
        self.ops.append(op)
        return op

    def pe(self, fn, reads, writes):
        return self._add("pe", fn, reads, writes)

    def act(self, fn, reads, writes):
        return self._add("act", fn, reads, writes)

    def dve(self, fn, reads, writes):
        return self._add("dve", fn, reads, writes)

    def pool(self, fn, reads, writes):
        return self._add("pool", fn, reads, writes)

    def dma(self, q, grp, fn, reads, writes):
        if grp == "const":
            writes = list(writes) + [("constchain",)]
        return self._add(q, fn, reads, writes, dma=True, grp=grp)

    def emit(self, stack, shared):
        nc = self.nc
        esem = shared["esem"]
        ecount = shared["ecount"]
        gslots = shared["gsem"]
        gcount = shared["gcount"]
        assert len(self.grp_cum) <= len(gslots), len(self.grp_cum)
        gidx = {g: i for i, g in enumerate(self.grp_cum)}
        gsem = {g: gslots[i] for g, i in gidx.items()}
        gbase = {g: gcount[i] for g, i in gidx.items()}
        cnt = dict(ecount)
        for op in self.ops:
            if not op.dma and op.signal:
                cnt[op.eng] += 1
                op.sigidx = cnt[op.eng]
            if op.dma:
                op.cum += gbase[op.grp]
        for e in self.ENGS:
            ecount[e] = cnt[e]
        for g, i in gidx.items():
            gcount[i] += self.grp_cum[g]
        per = {e: [] for e in self.ENGS}
        for op in self.ops:
            per[op.eng].append(op)
        final = [(gsem[g], gbase[g] + c) for g, c in self.grp_cum.items()]
        block = stack.enter_context(nc.Block())

        def body(ename):
            def f(eng):
                waited = {}
                for op in per[ename]:
                    need = []
                    for d in op.deps:
                        if d.dma:
                            key = ("g", d.grp)
                            val = d.cum
                            sem = gsem[d.grp]
                        else:
                            key = ("e", d.eng)
                            val = d.sigidx
                            sem = esem[d.eng]
                        if waited.get(key, 0) >= val:
                            continue
                        waited[key] = val
                        need.append((sem, val))
                    for sem, val in need[:-1]:
                        eng.wait_ge(sem, val)
                    ins = op.fn(eng)
                    if need:
                        ins._wait_ge(*need[-1])
                    if op.dma:
                        ins.then_inc(gsem[op.grp], 16)
                    elif op.signal:
                        ins.then_inc(esem[op.eng], 1)
                if ename == "sp":
                    for sem, c in final:
                        eng.wait_ge(sem, c)
            return f

        block.tensor(body("pe"))
        block.scalar(body("act"))
        block.vector(body("dve"))
        block.gpsimd(body("pool"))
        block.sync(body("sp"))


class K:
    def __init__(self, S):
        self.S = S
        self.nc = bass.Bass("TRN2", target_bir_lowering=False)
        self.P = None
        self.stack = None
        self.bank_rr = 0
        self.uid = 0
        self.nphase = 0
        self.gstack = ExitStack()
        nc = self.nc
        self.shared = {
            "esem": {e: self.gstack.enter_context(nc.semaphore("sem_" + e)) for e in Prog.ENGS},
            "ecount": {e: 0 for e in Prog.ENGS},
            "gsem": [self.gstack.enter_context(nc.semaphore("gsem%d" % i)) for i in range(28)],
            "gcount": [0] * 28,
        }

    def phase(self):
        return _Phase(self)

    def sb(self, name, shape, dt):
        return self.stack.enter_context(self.nc.sbuf_tensor("%s_p%d" % (name, self.nphase), shape, dt))

    def psum(self, name, shape, dt):
        return self.stack.enter_context(self.nc.psum_tensor("%s_p%d" % (name, self.nphase), shape, dt))

    def dram(self, name, shape, dt, kind="Internal"):
        import os
        kind = os.environ.get("SCRATCH_KIND", kind)
        return self.nc.dram_tensor(name, shape, dt, kind=kind).ap()

    def carve(self, shape):
        n = int(np.prod(shape))
        a = self.arena[self.arena_off:self.arena_off + n]
        self.arena_off += n
        if len(shape) == 3:
            return a.rearrange("(j p s) -> j p s", p=shape[1], s=shape[2])
        return a.rearrange("(p s) -> p s", s=shape[1])

    def bank(self, n=8, base=0):
        b = base + self.bank_rr % n
        self.bank_rr += 1
        return b


class _Phase:
    def __init__(self, k):
        self.k = k

    def __enter__(self):
        k = self.k
        k.nphase += 1
        k.P = Prog(k.nc, k.nphase)
        k.stack = ExitStack()
        k.stack.__enter__()
        k.stack.enter_context(k.nc.named_scope("ph%d" % k.nphase))
        k.bank_rr = 0
        k.uid = 0
        return k

    def __exit__(self, et, ev, tb):
        k = self.k
        if et is None:
            k.P.emit(k.stack, k.shared)
        k.stack.__exit__(et, ev, tb)
        return False


def mm(k, out, lhsT, rhs, start, stop, reads, writes):
    k.P.pe(lambda e, out=out, lhsT=lhsT, rhs=rhs, start=start, stop=stop:
           e.matmul(out, lhsT, rhs, start=start, stop=stop), reads, writes)


def emit_norm(k, x_sb, xn, T, gain_col, vecs, sqbuf, sqkey, rstd, ones_bf, ps, tmp_key="nrm"):
    P = k.P
    nsub = T // 512
    for kc in range(8):
        P.act(lambda e, kc=kc: e.activation(sqbuf[:, kc, :T], x_sb[:, kc, :T], AF.Square),
              [("x", kc)], [(sqkey, kc, s) for s in range(nsub)])
    for s in range(nsub):
        b = k.bank()
        sl = slice(s * 512, (s + 1) * 512)
        for kc in range(8):
            mm(k, ps[:, b, :], ones_bf[:, :], sqbuf[:, kc, sl], kc == 0, kc == 7,
               [(sqkey, kc, s), ("ones",)], [("ps", b)])
        P.act(lambda e, b=b, sl=sl: e.activation(rstd[:, sl], ps[:, b, :], AF.Ln,
                                                 bias=k.eps_col[:, 0:1], scale=1.0 / D),
              [("ps", b), ("eps",)], [("rstd", s)])
        P.act(lambda e, sl=sl: e.activation(rstd[:, sl], rstd[:, sl], AF.Exp, scale=-0.5), [("rstd", s)], [("rstd", s)])
        for kc in range(8):
            P.dve(lambda e, kc=kc, sl=sl: e.scalar_tensor_tensor(
                xn[:, kc, sl], x_sb[:, kc, sl], vecs[:, gain_col + kc:gain_col + kc + 1], rstd[:, sl],
                op0=ALU.mult, op1=ALU.mult),
                [("x", kc), ("rstd", s), ("vecs",)], [("xn", kc, s)])


def emit_ffn(k, li, fi, x_src, x_dst, wgu_d, wd_d, gain_col, final_gain_col=None, out_dst=None, pre=None):
    P = k.P
    S = k.S
    T = min(1024, S)
    nsub = T // 512
    B = k.bufs
    x_sb, xn, h, rstd, stmp, ps, vecs, ones_bf = (B["x"], B["xn"], B["h"], B["rstd"], B["stmp"],
                                                  B["ps"], B["vecs"], B["ones"])
    wgs, wgb, wds, wdb = B["wgs"], B["wgb"], B["wds"], B["wdb"]
    NG = S // T

    def load_wg(n):
        if n >= NG * NF:
            return
        f_, ss = n % NF, n % 2
        P.dma("sp", ("wgs", ss), lambda e, f_=f_, ss=ss: e.dma_start(out=wgs[:, ss, :], in_=wgu_d[f_]),
              [], [("wgs", ss)])

    def load_wd(m):
        if m >= NG * 8:
            return
        dm_, ss = m % 8, m % 2
        P.dma("sp", ("wds", ss), lambda e, dm_=dm_, ss=ss: e.dma_start(out=wds[:, ss, :], in_=wd_d[dm_]),
              [], [("wds", ss)])

    def load_xk(g_, kc):
        if g_ >= NG:
            return
        src = x_src[kc * 128:(kc + 1) * 128, g_ * T:(g_ + 1) * T]
        P.dma("sp", ("xld", kc), lambda e, src=src, kc=kc: e.dma_start(out=x_sb[:, kc, :T], in_=src),
              [("xdram", id(x_src), g_)], [("x", kc)])

    for kc in range(8):
        load_xk(0, kc)
    sqb_ = B["sq"]
    for g in range(S // T):
        tsl = slice(g * T, (g + 1) * T)
        if pre is not None:
            ocatT, woutb = pre
            oc_v = ocatT[:, :, tsl].rearrange("kc p t -> p kc t")
            P.dma("sp", "ocld", lambda e, oc_v=oc_v: e.dma_start(out=xn[:, :, :T], in_=oc_v),
                  [("ocdram", g)], [("xn", kc, s) for kc in range(8) for s in range(nsub)])
            for dm in range(8):
                for s in range(nsub):
                    sl = slice(s * 512, (s + 1) * 512)
                    b = k.bank()
                    for kc in range(8):
                        mm(k, ps[:, b, :], woutb[:, kc, dm * 128:(dm + 1) * 128], xn[:, kc, sl], kc == 0, kc == 7,
                           [("woutb",), ("xn", kc, s)], [("ps", b)])
                    P.dve(lambda e, b=b, dm=dm, sl=sl: e.tensor_tensor(
                        x_sb[:, dm, sl], x_sb[:, dm, sl], ps[:, b, :], op=ALU.add),
                        [("ps", b), ("x", dm)], [("x", dm)])
        emit_norm(k, x_sb, xn, T, gain_col, vecs, sqb_, "sq", rstd, ones_bf, ps)
        for f in range(NF):
            n = g * NF + f
            slot_s = n % 2
            slot_b = n % 3
            if n == 0:
                load_wg(0)
                load_wg(1)
                load_wd(0)
                load_wd(1)
            P.pool(lambda e, slot_s=slot_s, slot_b=slot_b: e.tensor_copy(wgb[:, slot_b, :], wgs[:, slot_s, :]),
                   [("wgs", slot_s)], [("wgb", slot_b)])
            load_wg(n + 2)
            for s in range(nsub):
                sl = slice(s * 512, (s + 1) * 512)
                bg = k.bank()
                bu = k.bank()
                for gu, b in ((0, bg), (1, bu)):
                    for kc in range(8):
                        o = (gu * 8 + kc) * 128
                        mm(k, ps[:, b, :], wgb[:, slot_b, o:o + 128], xn[:, kc, sl], kc == 0, kc == 7,
                           [("wgb", slot_b), ("xn", kc, s)], [("ps", b)])
                st = k.uid % 2
                k.uid += 1
                P.act(lambda e, bg=bg, st=st: e.activation(stmp[:, st, :], ps[:, bg, :], AF.Silu),
                      [("ps", bg)], [("stmp", st)])
                P.dve(lambda e, bu=bu, st=st, f=f, sl=sl: e.tensor_tensor(
                    h[:, f, sl], stmp[:, st, :], ps[:, bu, :], op=ALU.mult),
                    [("ps", bu), ("stmp", st)], [("h", f, s)])
        for dm in range(8):
            m = g * 8 + dm
            slot = m % 2
            P.act(lambda e, slot=slot: e.copy(wdb[:, slot, :], wds[:, slot, :]),
                  [("wds", slot)], [("wdb", slot)])
            load_wd(m + 2)
            for s in range(nsub):
                sl = slice(s * 512, (s + 1) * 512)
                b = k.bank()
                for f in range(NF):
                    mm(k, ps[:, b, :], wdb[:, slot, f * 128:(f + 1) * 128], h[:, f, sl], f == 0, f == NF - 1,
                       [("wdb", slot), ("h", f, s)], [("ps", b)])
                P.dve(lambda e, b=b, dm=dm, sl=sl: e.scalar_tensor_tensor(
                    x_sb[:, dm, sl], ps[:, b, :], 0.5, x_sb[:, dm, sl], op0=ALU.mult, op1=ALU.add),
                    [("ps", b), ("x", dm)], [("x", dm)])
            if final_gain_col is None:
                dst = x_dst[dm * 128:(dm + 1) * 128, tsl]
                P.dma("sp", ("xst", dm), lambda e, dst=dst, dm=dm: e.dma_start(out=dst, in_=x_sb[:, dm, :T]),
                      [("x", dm)], [("xdram", id(x_dst), g)])
                load_xk(g + 1, dm)
        if final_gain_col is None:
            pass
        else:
            P2 = k.P
            for kc in range(8):
                P2.act(lambda e, kc=kc: e.activation(sqb_[:, kc, :T], x_sb[:, kc, :T], AF.Square),
                       [("x", kc)], [("sq", kc, s) for s in range(nsub)])
            for s in range(nsub):
                b = k.bank()
                sl = slice(s * 512, (s + 1) * 512)
                for kc in range(8):
                    mm(k, ps[:, b, :], ones_bf[:, :], sqb_[:, kc, sl], kc == 0, kc == 7, [("sq", kc, s), ("ones",)], [("ps", b)])
                P2.act(lambda e, b=b, sl=sl: e.activation(rstd[:, sl], ps[:, b, :], AF.Ln,
                                                          bias=k.eps_col[:, 0:1], scale=1.0 / D),
                       [("ps", b), ("eps",)], [("rstd", s)])
                P2.act(lambda e, sl=sl: e.activation(rstd[:, sl], rstd[:, sl], AF.Exp, scale=-0.5),
                       [("rstd", s)], [("rstd", s)])
                for kc in range(8):
                    P2.dve(lambda e, kc=kc, sl=sl: e.scalar_tensor_tensor(
                        x_sb[:, kc, sl], x_sb[:, kc, sl], vecs[:, final_gain_col + kc:final_gain_col + kc + 1],
                        rstd[:, sl], op0=ALU.mult, op1=ALU.mult),
                        [("x", kc), ("rstd", s), ("vecs",)], [("x", kc)])
            od_v = out_dst[:, tsl].rearrange("(kc p) t -> p kc t", p=128)
            P2.dma("sp", "xst", lambda e, od_v=od_v: e.dma_start(out=od_v, in_=x_sb[:, :, :T]),
                   [("x", kc) for kc in range(8)], [("odram", g)])
            for kc in range(8):
                load_xk(g + 1, kc)


def load_wout(k, wout_d):
    B = k.bufs
    woutb = k.sb("woutb", [128, 8, 1024], BF16)
    for kc in range(8):
        slot = k.wd_ctr % 2
        k.wd_ctr += 1
        k.P.dma("sp", ("wds", slot), lambda e, kc=kc, slot=slot: e.dma_start(
            out=B["wds"][:, slot, 0:1024], in_=wout_d[:, kc, :]), [], [("wds", slot)])
        k.P.act(lambda e, kc=kc, slot=slot: e.copy(woutb[:, kc, :], B["wds"][:, slot, 0:1024]),
                [("wds", slot)], [("woutb",)])
    return woutb


def alloc_common(k, vecs_d, NV):
    S = k.S
    T = min(1024, S)
    B = {}
    B["x"] = k.sb("x_sb", [128, 8, T], F32)
    B["xn"] = k.sb("xn", [128, 8, T], BF16)
    B["h"] = k.sb("h", [128, NF, T], BF16)
    B["sq"] = k.sb("sq", [128, 8, T], BF16)
    B["rstd"] = k.sb("rstd", [128, T], F32)
    B["stmp"] = k.sb("stmp", [128, 2, 512], F32)
    B["wgs"] = k.sb("wgs", [128, 2, 2048], F32)
    B["wgb"] = k.sb("wgb", [128, 3, 2048], BF16)
    B["wds"] = k.sb("wds", [128, 2, NF * 128], F32)
    B["wdb"] = k.sb("wdb", [128, 2, NF * 128], BF16)
    B["vecs"] = k.sb("vecs", [128, NV], F32)
    B["ones"] = k.sb("ones", [128, 128], BF16)
    B["ps"] = k.psum("ps", [128, 8, 512], F32)
    k.eps_col = k.sb("epsc", [128, 1], F32)
    k.bufs = B
    k.wg_ctr = 0
    k.wd_ctr = 0
    P = k.P
    P.dma("sp", "const", lambda e: e.dma_start(out=B["vecs"][:, :], in_=vecs_d), [], [("vecs",)])
    P.dve(lambda e: e.memset(B["ones"][:, :], 1.0), [], [("ones",)])
    P.dve(lambda e: e.memset(k.eps_col[:, :], EPS), [], [("eps",)])
    return B


TWO_PI = 2.0 * math.pi
CW1 = 6.28125
CW2 = float(np.float32(TWO_PI - 6.28125))

C_INVA, C_SGNA, C_INVR, C_SGNR, C_127MP, C_P = 0, 1, 2, 3, 4, 5
C_DPOS, C_DNEG, C_MGE, C_MLE, C_IOTA1, C_IOTAR, C_IDENT = 8, 136, 264, 392, 520, 648, 776
C_TRI_INC, C_TRI_SUF, C_MASKLO, C_MASKUP, C_TRI_SSUF, C_TRI_SPRE = 904, 1032, 1160, 1288, 1416, 1544
NCST = 1672


def emit_rope_tables(k, posf, T, cst, inv_col, sgn_col, Ctab, Stab, tmp, key):
    P = k.P
    ang, yi, kf, th, w = tmp["ang"], tmp["yi"], tmp["kf"], tmp["th"], tmp["w"]
    R = [("posf",), ("cst",)]
    P.dve(lambda e: e.tensor_scalar(ang[:, :T], posf[:, :T], cst[:, inv_col:inv_col + 1], None, op0=ALU.mult),
           R, [("t_ang",)])
    P.dve(lambda e: e.tensor_scalar(yi[:, :T], ang[:, :T], 1.0 / TWO_PI, None, op0=ALU.mult),
           [("t_ang",)], [("t_yi",)])
    P.dve(lambda e: e.tensor_copy(kf[:, :T], yi[:, :T]), [("t_yi",)], [("t_kf",)])
    P.dve(lambda e: e.tensor_scalar(th[:, :T], kf[:, :T], -CW1, None, op0=ALU.mult), [("t_kf",)], [("t_th",)])
    P.dve(lambda e: e.tensor_tensor(th[:, :T], th[:, :T], ang[:, :T], op=ALU.add), [("t_th",), ("t_ang",)], [("t_th",)])
    P.dve(lambda e: e.tensor_scalar(kf[:, :T], kf[:, :T], -CW2, None, op0=ALU.mult), [("t_kf",)], [("t_kf",)])
    P.dve(lambda e: e.tensor_tensor(th[:, :T], th[:, :T], kf[:, :T], op=ALU.add), [("t_th",), ("t_kf",)], [("t_th",)])
    P.dve(lambda e: e.tensor_scalar(w[:, :T], th[:, :T], math.pi, TWO_PI, op0=ALU.is_gt, op1=ALU.mult),
           [("t_th",)], [("t_w",)])
    P.dve(lambda e: e.tensor_tensor(w[:, :T], th[:, :T], w[:, :T], op=ALU.subtract), [("t_th",), ("t_w",)], [("t_w",)])
    P.act(lambda e: e.activation(Stab[:, :T], w[:, :T], AF.Sin, scale=cst[:, sgn_col:sgn_col + 1]),
          [("t_w",), ("cst",)], [(key, "S")])
    P.dve(lambda e: e.tensor_scalar(w[:, :T], th[:, :T], math.pi / 2, TWO_PI, op0=ALU.is_gt, op1=ALU.mult),
           [("t_th",)], [("t_w",)])
    P.dve(lambda e: e.tensor_tensor(w[:, :T], th[:, :T], w[:, :T], op=ALU.subtract), [("t_th",), ("t_w",)], [("t_w",)])
    P.act(lambda e: e.activation(Ctab[:, :T], w[:, :T], AF.Sin, bias=k.halfpi_col[:, 0:1]),
          [("t_w",), ("hpi",)], [(key, "C")])


def emit_inproj(k, x_src, gain_col, win_d, pairs, wtok_d, ntokc, tok_dsts, pos_d, cst_d, vecs_d, NV):
    P = k.P
    S = k.S
    T = min(1024, S)
    nsub = T // 512
    x_sb = k.sb("x", [128, 8, T], F32)
    xn = k.sb("xn", [128, 8, T], BF16)
    sq = k.sb("sq", [128, 8, T], BF16)
    rstd = k.sb("rstd", [128, T], F32)
    wgs = k.sb("wgs", [128, 2, 2048], F32)
    wgb = k.sb("wgb", [128, 3, 2048], BF16)
    wts = k.sb("wts", [128, 2, ntokc], F32)
    wtok = k.sb("wtok", [128, 8, ntokc], BF16)
    vecs = k.sb("vecs", [128, NV], F32)
    cst = k.sb("cst", [128, NCST], F32)
    ones = k.sb("ones", [128, 128], BF16)
    ost = k.sb("ost", [128, 8, 512], BF16)
    vst = k.sb("vst", [128, 8, 512], BF16)
    t12 = k.sb("t12", [128, 4, 512], F32)
    ps = k.psum("ps", [128, 8, 512], F32)
    k.eps_col = k.sb("epsc", [128, 1], F32)
    k.halfpi_col = k.sb("hpic", [128, 1], F32)
    rope = any(p[0].startswith("rope") for p in pairs)
    P.dma("sp", "const", lambda e: e.dma_start(out=vecs[:, :], in_=vecs_d), [], [("vecs",)])
    P.dma("sp", "const", lambda e: e.dma_start(out=cst[:, :], in_=cst_d), [], [("cst",)])
    P.dve(lambda e: e.memset(ones[:, :], 1.0), [], [("ones",)])
    P.dve(lambda e: e.memset(k.eps_col[:, :], EPS), [], [("eps",)])
    P.dve(lambda e: e.memset(k.halfpi_col[:, :], math.pi / 2), [], [("hpi",)])
    if rope:
        posi = k.sb("posi", [128, T], I32)
        posf = k.sb("posf", [128, T], F32)
        tabs = {n: k.sb("tab" + n, [128, T], F32) for n in ("CA", "SA", "CR", "SR")}
        tmp = {"ang": k.sb("ang", [128, T], F32), "yi": k.sb("yi", [128, T], I32),
               "kf": k.sb("kf", [128, T], F32), "th": k.sb("th", [128, T], F32), "w": k.sb("w", [128, T], F32)}
    for kc in range(8):
        sl = kc % 2
        P.dma("sp", ("wts", sl), lambda e, kc=kc, sl=sl: e.dma_start(out=wts[:, sl, :], in_=wtok_d[:, kc, :]),
              [], [("wts", sl)])
        P.act(lambda e, kc=kc, sl=sl: e.copy(wtok[:, kc, :], wts[:, sl, :]), [("wts", sl)], [("wtok", kc)])
    wg_ctr = 0
    n_w = (S // T) * len(pairs)

    def load_w(n):
        if n >= n_w:
            return
        pi_, ss = n % len(pairs), n % 2
        P.dma("sp", ("wgs", ss), lambda e, pi_=pi_, ss=ss: e.dma_start(out=wgs[:, ss, :], in_=win_d[pi_]),
              [], [("wgs", ss)])

    def load_x(g_):
        if g_ >= S // T:
            return
        tsl_ = slice(g_ * T, (g_ + 1) * T)
        xs_v = x_src[:, tsl_].rearrange("(kc p) t -> p kc t", p=128)
        P.dma("sp", "xld", lambda e, xs_v=xs_v: e.dma_start(out=x_sb[:, :, :], in_=xs_v),
              [("xdram", g_)], [("x", kc) for kc in range(8)])
        if rope:
            P.dma("sp", "pos", lambda e, tsl_=tsl_: e.dma_start(out=posi[:, :], in_=pos_d[tsl_].partition_broadcast(128)),
                  [], [("posi",)])

    load_x(0)
    for g in range(S // T):
        tsl = slice(g * T, (g + 1) * T)
        if rope:
            P.dve(lambda e: e.tensor_copy(posf[:, :], posi[:, :]), [("posi",)], [("posf",)])
            emit_rope_tables(k, posf, T, cst, C_INVA, C_SGNA, tabs["CA"], tabs["SA"], tmp, "tabA")
            emit_rope_tables(k, posf, T, cst, C_INVR, C_SGNR, tabs["CR"], tabs["SR"], tmp, "tabR")
        emit_norm(k, x_sb, xn, T, gain_col, vecs, sq, "sq", rstd, ones, ps)
        load_x(g + 1)
        for pi, (kind, d0, d1) in enumerate(pairs):
            slot_s = wg_ctr % 2
            slot_b = wg_ctr % 3
            wg_ctr += 1
            if wg_ctr == 1:
                load_w(0)
                load_w(1)
            P.act(lambda e, slot_s=slot_s, slot_b=slot_b: e.copy(wgb[:, slot_b, :], wgs[:, slot_s, :]),
                  [("wgs", slot_s)], [("wgb", slot_b)])
            load_w(wg_ctr + 1)
            for s in range(nsub):
                sl = slice(s * 512, (s + 1) * 512)
                dsl = slice(g * T + s * 512, g * T + (s + 1) * 512)
                b0 = k.bank()
                b1 = k.bank()
                for half, b in ((0, b0), (1, b1)):
                    for kc in range(8):
                        o = (half * 8 + kc) * 128
                        mm(k, ps[:, b, :], wgb[:, slot_b, o:o + 128], xn[:, kc, sl], kc == 0, kc == 7,
                           [("wgb", slot_b), ("xn", kc, s)], [("ps", b)])
                if kind.startswith("rope"):
                    tk = "tab" + kind[-1]
                    Ct, St = tabs["C" + kind[-1]], tabs["S" + kind[-1]]
                    u = k.uid % 2
                    k.uid += 1
                    os_ = k.uid % 8
                    P.dve(lambda e, b0=b0, u=u, sl=sl, Ct=Ct: e.tensor_tensor(
                        t12[:, 2 * u, :], ps[:, b0, :], Ct[:, sl], op=ALU.mult),
                        [("ps", b0), (tk, "C")], [("t12", 2 * u)])
                    P.dve(lambda e, b1=b1, u=u, sl=sl, St=St: e.tensor_tensor(
                        t12[:, 2 * u + 1, :], ps[:, b1, :], St[:, sl], op=ALU.mult),
                        [("ps", b1), (tk, "S")], [("t12", 2 * u + 1)])
                    P.pool(lambda e, u=u, os_=os_: e.tensor_tensor(
                        ost[:, os_, :], t12[:, 2 * u, :], t12[:, 2 * u + 1, :], op=ALU.add),
                        [("t12", 2 * u), ("t12", 2 * u + 1)], [("ost", os_)])
                    P.dma("sp", ("ost", os_), lambda e, os_=os_, d0=d0, dsl=dsl: e.dma_start(
                        out=d0[:, dsl], in_=ost[:, os_, :]), [("ost", os_)], [("sc_fm", id(d0), g)])
                else:
                    fn = AF.Silu if kind == "silu" else AF.Copy
                    for b, dd in ((b0, d0), (b1, d1)):
                        if dd is None:
                            continue
                        k.uid += 1
                        os_ = k.uid % 8
                        P.act(lambda e, b=b, os_=os_, fn=fn: e.activation(ost[:, os_, :], ps[:, b, :], fn),
                              [("ps", b)], [("ost", os_)])
                        P.dma("sp", ("ost", os_), lambda e, os_=os_, dd=dd, dsl=dsl: e.dma_start(
                            out=dd[:, dsl], in_=ost[:, os_, :]), [("ost", os_)], [("sc_fm", id(dd), g)])
        for tt in range(T // 128):
            t0 = g * T + tt * 128
            for (c0, ncol, dd) in tok_dsts:
                b = k.bank()
                for kc in range(8):
                    mm(k, ps[:, b, :ncol], xn[:, kc, tt * 128:(tt + 1) * 128], wtok[:, kc, c0:c0 + ncol],
                       kc == 0, kc == 7, [("wtok", kc), ("xn", kc, tt // 4)], [("ps", b)])
                k.uid += 1
                vs = k.uid % 8
                P.act(lambda e, b=b, vs=vs, ncol=ncol: e.copy(vst[:, vs, :ncol], ps[:, b, :ncol]),
                      [("ps", b)], [("vst", vs)])
                P.dma("sp", ("vst", vs), lambda e, vs=vs, dd=dd, t0=t0, ncol=ncol: e.dma_start(
                    out=dd[t0:t0 + 128, :], in_=vst[:, vs, :ncol]), [("vst", vs)], [("sc_tm", id(dd), g)])


def dma_tiles(k, grp, dst, src, nt, key):
    for t0 in range(0, nt, 8):
        t1 = min(nt, t0 + 8)
        sv = src[t0 * 128:t1 * 128, :].rearrange("(t p) c -> p t c", p=128)
        k.P.dma("sp", grp, lambda e, sv=sv, t0=t0, t1=t1: e.dma_start(out=dst[:, t0:t1, :], in_=sv), [], [key])


def emit_pnorm(k, O, T, ps, ones, sqb, rs, ncp, n_feat, eps_col, pk=None):
    P = k.P
    b = k.bank(4, 0)
    pkeys = [("ps", b)] if pk is None else pk(b)
    for i, (Oap, Okey) in enumerate(O):
        P.act(lambda e, Oap=Oap, i=i: e.activation(sqb[:, i, :T], Oap, AF.Square), [Okey], [("pn_sq", i)])
        mm(k, ps[:, b, :T], ones[:, :], sqb[:, i, :T], i == 0, i == len(O) - 1, [("pn_sq", i), ("ones",)], pkeys)
    P.act(lambda e, b=b: e.activation(rs[:, :T], ps[:, b, :T], AF.Ln, bias=eps_col[:, 0:1], scale=1.0 / n_feat),
          pkeys + [("eps",)], [("pn_rs",)])
    P.act(lambda e: e.activation(rs[:, :T], rs[:, :T], AF.Exp, scale=-0.5), [("pn_rs",)], [("pn_rs",)])


PE_L = (1,)


def emit_diffattn(k, qkT, vtok_a, ocatT, vecs_d, NV, V_LQ, V_DANORM, lam_init):
    P = k.P
    S = k.S
    NQ = S // 512
    NK = S // 128
    qT = k.sb("qT", [128, 2, S], BF16)
    kT = k.sb("kT", [128, 2, S], BF16)
    V = k.sb("V", [128, 2, NK, 128], BF16)
    pb = k.sb("pb", [128, 4, 512], BF16)
    accL = k.sb("accL", [128, 2, 512], F32)
    Oc = k.sb("Oc", [128, 2, 512], F32)
    onesf = k.sb("onesf", [128, 128], F32)
    negone = k.sb("negone", [128, 512], F32)
    vecs = k.sb("vecs", [128, NV], F32)
    ones = k.sb("ones", [128, 128], BF16)
    eps_col = k.sb("epsc", [128, 1], F32)
    sc = k.sb("scal", [128, 8], F32)
    junk = k.sb("junk", [128, 2, 64], F32)
    r12 = k.sb("r12", [128, 2, 512], F32)
    t12 = k.sb("t12", [128, 2, 512], F32)
    Of = k.sb("Of", [128, 512], F32)
    sqb = k.sb("sqb", [128, 1, 512], BF16)
    rs = k.sb("rs", [128, 512], F32)
    ob = k.sb("ob", [128, 2, 512], BF16)
    ps = k.psum("ps", [128, 8, 512], F32)
    P.dma("sp", "const", lambda e: e.dma_start(out=vecs[:, :], in_=vecs_d), [], [("vecs",)])
    P.dve(lambda e: e.memset(ones[:, :], 1.0), [], [("ones",)])
    P.dve(lambda e: e.memset(onesf[:, :], 1.0), [], [("onesf",)])
    P.dve(lambda e: e.memset(negone[:, :], -1.0), [], [("negone",)])
    P.dve(lambda e: e.memset(eps_col[:, :], EPS), [], [("eps",)])
    for i in range(2):
        a = V_LQ + i * 128
        P.dve(lambda e, a=a, i=i: e.scalar_tensor_tensor(junk[:, i, :], vecs[:, a:a + 64], 1.0, vecs[:, a + 64:a + 128],
                                                         op0=ALU.mult, op1=ALU.mult, accum_out=sc[:, i:i + 1]),
              [("vecs",)], [("sc", i), ("junk", i)])
        P.act(lambda e, i=i: e.activation(sc[:, i:i + 1], sc[:, i:i + 1], AF.Exp), [("sc", i)], [("sc", i)])
    P.dve(lambda e: e.tensor_tensor(sc[:, 2:3], sc[:, 1:2], sc[:, 0:1], op=ALU.subtract), [("sc", 0), ("sc", 1)], [("sc", 2)])
    P.dve(lambda e: e.tensor_scalar(sc[:, 2:3], sc[:, 2:3], -lam_init, None, op0=ALU.add), [("sc", 2)], [("sc", 2)])
    P.dve(lambda e: e.tensor_scalar(sc[:, 3:4], vecs[:, V_DANORM:V_DANORM + 1], 1.0 - lam_init, None, op0=ALU.mult),
          [("vecs",)], [("sc", 3)])
    for h in range(4):
        sl = h % 2
        P.dma("sp", ("hq", sl), lambda e, h=h, sl=sl: e.dma_start(out=qT[:, sl, :], in_=qkT[h]), [], [("qT", sl)])
        P.dma("sp", ("hk", sl), lambda e, h=h, sl=sl: e.dma_start(out=kT[:, sl, :], in_=qkT[4 + h]), [], [("kT", sl)])
        dma_tiles(k, ("hv", sl), V[:, sl, :, :], vtok_a[:, h * 128:(h + 1) * 128], NK, ("V", sl))
        for qg in range(NQ):
            qs = slice(qg * 512, (qg + 1) * 512)
            bO = (4, 5)
            bL = (6, 7)
            sbanks = {}

            def scores(kc):
                ks = slice(kc * 128, (kc + 1) * 128)
                bb = []
                for c in range(2):
                    b = k.bank(4, 0)
                    bb.append(b)
                    pr = slice(c * 64, (c + 1) * 64)
                    mm(k, ps[:, b, :], kT[pr, sl, ks], qT[pr, sl, qs], True, True,
                       [("kT", sl), ("qT", sl)], [("ps", b)])
                sbanks[kc] = bb

            scores(0)
            for kc in range(NK):
                if kc + 1 < NK:
                    scores(kc + 1)
                for c in range(2):
                    b = sbanks[kc][c]
                    k.uid += 1
                    pp = k.uid % 4
                    P.act(lambda e, b=b, pp=pp: e.activation(pb[:, pp, :], ps[:, b, :], AF.Exp, scale=0.125),
                          [("ps", b)], [("pb", pp)])
                    mm(k, ps[:, bO[c], :], V[:, sl, kc, :], pb[:, pp, :], kc == 0, kc == NK - 1,
                       [("V", sl), ("pb", pp)], [("ps", bO[c])])
                    if c in PE_L:
                        mm(k, ps[:, bL[c], :], ones[:, :], pb[:, pp, :], kc == 0, kc == NK - 1,
                           [("ones",), ("pb", pp)], [("ps", bL[c])])
                    elif kc == 0:
                        P.dve(lambda e, c=c, pp=pp: e.tensor_copy(accL[:, c, :], pb[:, pp, :]),
                              [("pb", pp)], [("accL", c)])
                    else:
                        P.dve(lambda e, c=c, pp=pp: e.tensor_tensor(accL[:, c, :], accL[:, c, :], pb[:, pp, :], op=ALU.add),
                              [("pb", pp), ("accL", c)], [("accL", c)])
            for c in range(2):
                P.dve(lambda e, c=c: e.tensor_copy(Oc[:, c, :], ps[:, bO[c], :]), [("ps", bO[c])], [("Oc", c)])
            for c in range(2):
                if c in PE_L:
                    continue
                mm(k, ps[:, bL[c], :], onesf[:, :], accL[:, c, :], True, True, [("onesf",), ("accL", c)], [("ps", bL[c])])
            for c in range(2):
                P.act(lambda e, c=c: e.activation(r12[:, c, :], ps[:, bL[c], :], AF.Ln), [("ps", bL[c])], [("r12", c)])
                P.act(lambda e, c=c: e.activation(r12[:, c, :], r12[:, c, :], AF.Exp, scale=-1.0),
                      [("r12", c)], [("r12", c)])
                P.dve(lambda e, c=c: e.tensor_tensor(t12[:, c, :], Oc[:, c, :], r12[:, c, :], op=ALU.mult),
                      [("Oc", c), ("r12", c)], [("t12", c)])
            P.dve(lambda e: e.scalar_tensor_tensor(Of[:, :], t12[:, 1, :], sc[:, 2:3], t12[:, 0, :],
                                                   op0=ALU.mult, op1=ALU.add),
                  [("t12", 0), ("t12", 1), ("sc", 2)], [("Of",)])
            emit_pnorm(k, [(Of[:, :], ("Of",))], 512, ps, ones, sqb, rs, 1, 128.0, eps_col)
            k.uid += 1
            oo = k.uid % 2
            P.dve(lambda e, oo=oo: e.scalar_tensor_tensor(ob[:, oo, :], Of[:, :], sc[:, 3:4], rs[:, :],
                                                          op0=ALU.mult, op1=ALU.mult),
                  [("Of",), ("sc", 3), ("pn_rs",)], [("ob", oo)])
            P.dma("sp", ("ob", oo), lambda e, oo=oo, h=h, qs=qs: e.dma_start(out=ocatT[h][:, qs], in_=ob[:, oo, :]),
                  [("ob", oo)], [("ocat", h, qg)])


def emit_retention(k, qkT, vtok_r, sgT, ocatT, vecs_d, NV, cst_d, V_RLF, V_RLB, V_RETNORM):
    P = k.P
    S = k.S
    NK = S // 128
    NQ = S // 512
    q64 = k.sb("q64", [64, S], BF16)
    k64 = k.sb("k64", [64, S], BF16)
    qf = k.sb("qf", [64, S], BF16)
    qb = k.sb("qb", [64, S], BF16)
    V = k.sb("V", [128, NK, 128], BF16)
    sg = k.sb("sg", [128, S], BF16)
    AT = k.sb("AT", [128, NK, 128], BF16)
    kfb = k.sb("kfb", [128, 4, 128], BF16)
    kvs = k.sb("kvs", [64, NK, 256], F32)
    SF = k.sb("SF", [64, NK, 128], F32)
    SB = k.sb("SB", [64, NK, 128], F32)
    SFb = k.sb("SFb", [64, NK, 128], BF16)
    SBb = k.sb("SBb", [64, NK, 128], BF16)
    vecs = k.sb("vecs", [128, NV], F32)
    cst = k.sb("cst", [128, NCST], F32)
    identb = k.sb("identb", [128, 128], BF16)
    ones = k.sb("ones", [128, 128], BF16)
    eps_col = k.sb("epsc", [128, 1], F32)
    one_col = k.sb("onec", [128, 1], F32)
    sc = k.sb("scal", [128, 8], F32)
    E12 = k.sb("E12", [128, 2, 128], F32)
    Dc = k.sb("Dc", [128, 128], F32)
    qd = k.sb("qd", [128, 2, 512], F32)
    Of = k.sb("Of", [128, 512], F32)
    sqb = k.sb("sqb", [128, 1, 512], BF16)
    rs = k.sb("rs", [128, 512], F32)
    tt = k.sb("tt", [128, 512], F32)
    ob = k.sb("ob", [128, 2, 512], BF16)
    ps = k.psum("ps", [128, 7, 512], F32)
    P.dma("sp", "const", lambda e: e.dma_start(out=vecs[:, :], in_=vecs_d), [], [("vecs",)])
    P.dma("sp", "const", lambda e: e.dma_start(out=cst[:, :], in_=cst_d), [], [("cst",)])
    P.dve(lambda e: e.memset(ones[:, :], 1.0), [], [("ones",)])
    P.dve(lambda e: e.memset(eps_col[:, :], EPS), [], [("eps",)])
    P.dve(lambda e: e.memset(one_col[:, :], 1.0), [], [("onec",)])
    P.dve(lambda e: e.tensor_copy(identb[:, :], cst[:, C_IDENT:C_IDENT + 128]), [("cst",)], [("identb",)])
    for h in range(4):
        pr = slice((h % 2) * 64, (h % 2) * 64 + 64)
        P.dma("sp", "hq", lambda e, h=h, pr=pr: e.dma_start(out=q64[:, :], in_=qkT[8 + h // 2][pr, :]), [], [("q64",)])
        P.dma("sp", "hk", lambda e, h=h, pr=pr: e.dma_start(out=k64[:, :], in_=qkT[10 + h // 2][pr, :]), [], [("k64",)])
        dma_tiles(k, "hv", V, vtok_r[:, h * 128:(h + 1) * 128], NK, ("V",))
        P.dma("sp", "hg", lambda e, h=h: e.dma_start(out=sg[:, :], in_=sgT[h]), [], [("sg",)])
        for d, col in ((0, V_RLF + h), (1, V_RLB + h)):
            P.act(lambda e, d=d, col=col: e.activation(sc[:, d:d + 1], vecs[:, col:col + 1], AF.Exp, scale=-1.0),
                  [("vecs",)], [("sc", d)])
            P.act(lambda e, d=d: e.activation(sc[:, d:d + 1], sc[:, d:d + 1], AF.Ln, bias=one_col[:, 0:1]),
                  [("sc", d), ("onec",)], [("sc", d)])
            P.dve(lambda e, d=d: e.tensor_scalar(sc[:, d:d + 1], sc[:, d:d + 1], -1.0, None, op0=ALU.mult),
                  [("sc", d)], [("sc", d)])
            ccol = C_127MP if d == 0 else C_P
            P.act(lambda e, d=d, ccol=ccol: e.activation(sc[:, 2 + d:3 + d], cst[:, ccol:ccol + 1], AF.Exp,
                                                         scale=sc[:, d:d + 1]),
                  [("sc", d), ("cst",)], [("sc", 2 + d)])
            P.act(lambda e, d=d: e.activation(sc[:, 4 + d:5 + d], sc[:, d:d + 1], AF.Exp, scale=128.0),
                  [("sc", d)], [("sc", 4 + d)])
            dcol, mcol = (C_DPOS, C_MGE) if d == 0 else (C_DNEG, C_MLE)
            P.act(lambda e, d=d, dcol=dcol: e.activation(E12[:, d, :], cst[:, dcol:dcol + 128], AF.Exp,
                                                         scale=sc[:, d:d + 1]),
                  [("sc", d), ("cst",)], [("E12", d)])
            P.dve(lambda e, d=d, mcol=mcol: e.tensor_tensor(E12[:, d, :], E12[:, d, :], cst[:, mcol:mcol + 128],
                                                            op=ALU.mult), [("E12", d), ("cst",)], [("E12", d)])
            icol = C_IOTA1 if d == 0 else C_IOTAR
            P.act(lambda e, d=d, icol=icol: e.activation(qd[:, d, 0:128], cst[:, icol:icol + 128], AF.Exp,
                                                         scale=sc[:, d:d + 1]),
                  [("sc", d), ("cst",)], [("qd", d)])
            P.dve(lambda e, d=d: e.tensor_scalar(qd[:, d, 0:128], qd[:, d, 0:128], 0.125, None, op0=ALU.mult),
                  [("qd", d)], [("qd", d)])
            for j in range(1, 4):
                P.dve(lambda e, d=d, j=j: e.tensor_copy(qd[:, d, j * 128:(j + 1) * 128], qd[:, d, 0:128]),
                      [("qd", d)], [("qd", d)])
        P.dve(lambda e: e.tensor_tensor(Dc[:, :], E12[:, 0, :], E12[:, 1, :], op=ALU.add),
              [("E12", 0), ("E12", 1)], [("Dc",)])
        P.dve(lambda e: e.tensor_scalar(Dc[:, :], Dc[:, :], 0.125, None, op0=ALU.mult), [("Dc",)], [("Dc",)])
        for g in range(NQ):
            gs = slice(g * 512, (g + 1) * 512)
            P.dve(lambda e, gs=gs: e.tensor_tensor(qf[:, gs], q64[:, gs], qd[0:64, 0, :], op=ALU.mult),
                  [("q64",), ("qd", 0)], [("qf", g)])
            P.pool(lambda e, gs=gs: e.tensor_tensor(qb[:, gs], q64[:, gs], qd[0:64, 1, :], op=ALU.mult),
                   [("q64",), ("qd", 1)], [("qb", g)])
        P.dve(lambda e: e.memset(SF[:, 0, :], 0.0), [], [("SF", 0)])
        P.dve(lambda e: e.memset(SB[:, NK - 1, :], 0.0), [], [("SB", NK - 1)])
        import os
        rstop = int(os.environ.get("RET_STOP", "9"))
        if rstop < 1:
            continue
        def retA(c):
            cs = slice(c * 128, (c + 1) * 128)
            ks = c % 4
            b = k.bank(7, 0)
            mm(k, ps[:, b, 0:128], k64[:, cs], q64[:, cs], True, True, [("k64",), ("q64",)], [("ps", b)])
            P.dve(lambda e, b=b, c=c: e.tensor_tensor(AT[:, c, :], ps[:, b, 0:128], Dc[:, :], op=ALU.mult),
                  [("ps", b), ("Dc",)], [("AT", c)])
            bt = k.bank(7, 0)
            mm(k, ps[:, bt, 0:64], k64[:, cs], identb[0:64, 0:64], True, True, [("k64",), ("identb",)], [("ps", bt)])
            P.dve(lambda e, bt=bt, ks=ks: e.tensor_scalar(kfb[:, ks, 0:64], ps[:, bt, 0:64], sc[:, 2:3], None,
                                                          op0=ALU.mult),
                  [("ps", bt), ("sc", 2)], [("kfb", ks, 0)])
            P.dve(lambda e, bt=bt, ks=ks: e.tensor_scalar(kfb[:, ks, 64:128], ps[:, bt, 0:64], sc[:, 3:4], None,
                                                          op0=ALU.mult),
                  [("ps", bt), ("sc", 3)], [("kfb", ks, 1)])

        def retB(c):
            ks = c % 4
            b2 = k.bank(7, 0)
            mm(k, ps[0:64, b2, 0:128], kfb[:, ks, 0:64], V[:, c, :], True, True, [("kfb", ks, 0), ("V",)], [("ps", b2)])
            mm(k, ps[0:64, b2, 128:256], kfb[:, ks, 64:128], V[:, c, :], True, True, [("kfb", ks, 1), ("V",)], [("ps", b2)])
            P.act(lambda e, b2=b2, c=c: e.copy(kvs[:, c, :], ps[0:64, b2, 0:256]), [("ps", b2)], [("kvs", c)])

        for step in range(NK + 2):
            if step < NK:
                retA(step)
            if step >= 2:
                retB(step - 2)
        if rstop < 2:
            continue
        for c in range(NK - 1):
            P.dve(lambda e, c=c: e.scalar_tensor_tensor(SF[:, c + 1, :], SF[:, c, :], sc[0:64, 4:5], kvs[:, c, 0:128],
                                                        op0=ALU.mult, op1=ALU.add),
                  [("SF", c), ("kvs", c), ("sc", 4)], [("SF", c + 1)])
        for c in range(NK - 1, 0, -1):
            P.dve(lambda e, c=c: e.scalar_tensor_tensor(SB[:, c - 1, :], SB[:, c, :], sc[0:64, 5:6], kvs[:, c, 128:256],
                                                        op0=ALU.mult, op1=ALU.add),
                  [("SB", c), ("kvs", c), ("sc", 5)], [("SB", c - 1)])
        P.act(lambda e: e.copy(SFb[:, :, :], SF[:, :, :]), [("SF", c) for c in range(NK)], [("SFb",)])
        P.pool(lambda e: e.tensor_copy(SBb[:, :, :], SB[:, :, :]), [("SB", c) for c in range(NK)], [("SBb",)])
        if rstop < 3:
            continue
        for g in range(NQ):
            gs = slice(g * 512, (g + 1) * 512)
            b = k.bank(7, 0)
            for j in range(4):
                c = g * 4 + j
                cs = slice(c * 128, (c + 1) * 128)
                js = slice(j * 128, (j + 1) * 128)
                mm(k, ps[:, b, js], V[:, c, :], AT[:, c, :], True, False, [("V",), ("AT", c)], [("ps", b)])
                mm(k, ps[:, b, js], SFb[:, c, :], qf[:, cs], False, False, [("SFb",), ("qf", g)], [("ps", b)])
                mm(k, ps[:, b, js], SBb[:, c, :], qb[:, cs], False, True, [("SBb",), ("qb", g)], [("ps", b)])
            P.dve(lambda e, b=b: e.tensor_copy(Of[:, :], ps[:, b, :]), [("ps", b)], [("Of",)])
            emit_pnorm(k, [(Of[:, :], ("Of",))], 512, ps, ones, sqb, rs, 1, 128.0, eps_col)
            P.dve(lambda e: e.scalar_tensor_tensor(tt[:, :], Of[:, :], vecs[:, V_RETNORM:V_RETNORM + 1], rs[:, :],
                                                   op0=ALU.mult, op1=ALU.mult),
                  [("Of",), ("vecs",), ("pn_rs",)], [("tt",)])
            k.uid += 1
            oo = k.uid % 2
            P.dve(lambda e, oo=oo, gs=gs: e.tensor_tensor(ob[:, oo, :], tt[:, :], sg[:, gs], op=ALU.mult),
                  [("tt",), ("sg",)], [("ob", oo)])
            P.dma("sp", ("ob", oo), lambda e, oo=oo, h=h, gs=gs: e.dma_start(out=ocatT[4 + h][:, gs], in_=ob[:, oo, :]),
                  [("ob", oo)], [("ocat", h, g)])


V_FFN1 = (0, 16)
V_FFN2 = (8, 24)
V_AB, V_C, V_FINAL, V_DANORM, V_RETNORM, V_GLANORM = 32, 40, 48, 56, 57, 58
V_RLF, V_RLB, V_LQ = 60, 64, 68
NV = 68 + 256


def lay_vec(v):
    return np.ascontiguousarray(np.asarray(v, np.float32).reshape(-1, 128).T)


def lay_wgu(Wg, Wu):
    a = np.stack([Wg, Wu], 0).reshape(2, 8, 128, -1, 128)
    nf = a.shape[3]
    return np.ascontiguousarray(a.transpose(3, 2, 0, 1, 4)).reshape(nf, 128, 2048)


def lay_wd(Wd):
    a = Wd.reshape(NF, 128, 8, 128)
    return np.ascontiguousarray(a.transpose(2, 1, 0, 3)).reshape(8, 128, NF * 128)


def lay_kmajor(W):
    return np.ascontiguousarray(W.reshape(8, 128, -1).transpose(1, 0, 2))


def build_cst():
    c = np.zeros((128, NCST), np.float32)
    p = np.arange(128)
    d = p % 64
    inva = np.where(d < 16, 500000.0 ** (-(2.0 * (d % 8)) / 16.0), 0.0)
    ia = (1.0 / (np.float32(500000.0) ** (np.arange(0, 16, 2, dtype=np.float32) / np.float32(16)))).astype(np.float32)
    ir = (1.0 / (np.float32(10000.0) ** (np.arange(0, 64, 2, dtype=np.float32) / np.float32(64)))).astype(np.float32)
    c[:, C_INVA] = np.where(d < 16, ia[d % 8], 0.0)
    c[:, C_SGNA] = np.where(d < 8, -1.0, 1.0)
    c[:, C_INVR] = ir[d % 32]
    c[:, C_SGNR] = np.where(d < 32, -1.0, 1.0)
    c[:, C_127MP] = 127 - p
    c[:, C_P] = p
    diff = (p[None, :] - p[:, None]).astype(np.float32)
    c[:, C_DPOS:C_DPOS + 128] = np.maximum(diff, 0)
    c[:, C_DNEG:C_DNEG + 128] = np.maximum(-diff, 0)
    c[:, C_MGE:C_MGE + 128] = (diff >= 0)
    c[:, C_MLE:C_MLE + 128] = (diff <= 0)
    c[:, C_IOTA1:C_IOTA1 + 128] = (p + 1)[None, :]
    c[:, C_IOTAR:C_IOTAR + 128] = (128 - p)[None, :]
    c[:, C_IDENT:C_IDENT + 128] = np.eye(128)
    same = (p[:, None] // 64) == (p[None, :] // 64)
    sd = p[:, None]
    td = p[None, :]
    c[:, C_TRI_INC:C_TRI_INC + 128] = same & (sd <= td)
    c[:, C_TRI_SUF:C_TRI_SUF + 128] = same & (sd >= td)
    c[:, C_MASKLO:C_MASKLO + 128] = same & (sd <= td)
    c[:, C_MASKUP:C_MASKUP + 128] = same & (sd >= td)
    c[:, C_TRI_SSUF:C_TRI_SSUF + 128] = same & (sd > td)
    c[:, C_TRI_SPRE:C_TRI_SPRE + 128] = same & (sd < td)
    return c


def swap_perm_da():
    idx = np.arange(512)
    d = idx % 64
    return np.where(d < 8, idx + 8, np.where(d < 16, idx - 8, idx))


def swap_perm_ret():
    idx = np.arange(256)
    d = idx % 64
    return np.where(d < 32, idx + 32, idx - 32)


def lay_win_ab(W):
    qa, ka, va, qr, kr, vr, gr = (W[:, 0:512], W[:, 512:1024], W[:, 1024:1536], W[:, 1536:1792],
                                  W[:, 1792:2048], W[:, 2048:2560], W[:, 2560:3072])
    pa, prr = swap_perm_da(), swap_perm_ret()
    pairs = []
    for X in (qa, ka):
        Xs = X[:, pa]
        for j in range(4):
            pairs.append((X[:, j * 128:(j + 1) * 128], Xs[:, j * 128:(j + 1) * 128]))
    for X in (qr, kr):
        Xs = X[:, prr]
        for j in range(2):
            pairs.append((X[:, j * 128:(j + 1) * 128], Xs[:, j * 128:(j + 1) * 128]))
    pairs.append((gr[:, 0:128], gr[:, 128:256]))
    pairs.append((gr[:, 256:384], gr[:, 384:512]))
    win = np.concatenate([lay_wgu(a, b) for a, b in pairs], 0)
    wtok = lay_kmajor(np.concatenate([va, vr], 1))
    return np.ascontiguousarray(win), wtok


def build_program(S, stages):
    k = K(S)
    nc = k.nc
    inp = lambda n, shp, dt=F32: nc.dram_tensor(n, list(shp), dt, kind="ExternalInput").ap()
    xT = inp("xT", [D, S])
    pos = inp("pos", [S], I32)
    vecs_d = inp("vecs", [128, NV])
    cst_d = inp("cst", [128, NCST])
    wgu = {}
    wd = {}
    for l in range(2):
        for f in range(2):
            if ("ffn%d_%d" % (f + 1, l)) in stages:
                wgu[(l, f)] = inp("wgu_%d_%d" % (l, f), [NF, 128, 2048])
                wd[(l, f)] = inp("wd_%d_%d" % (l, f), [8, 128, NF * 128])
    out = nc.dram_tensor("outT", [D, S], F32, kind="ExternalOutput").ap()
    xs = out
    k.arena = k.dram("arena", [40 * 128 * S], BF16)
    k.arena_off = 0
    ocatT = k.carve([8, 128, S])
    k.arena_base = k.arena_off
    if "ab" in stages:
        win_ab = inp("win_ab", [14, 128, 2048])
        wtok_ab = inp("wtok_ab", [128, 8, 1024])
        wout_ab = inp("wout_ab", [128, 8, 1024])
        k.arena_off = k.arena_base
        qkT = k.carve([12, 128, S])
        sgT = k.carve([4, 128, S])
        vtok = k.carve([S, 1024])
    if "c" in stages:
        win_c = inp("win_c", [NPAIR_C, 128, 2048])
        wtok_c = inp("wtok_c", [128, 8, 1536])
        wout_c = inp("wout_c", [128, 8, 1024])
        w2aug = inp("w2aug", [17, 1024])
    last = stages[-1]
    cur = xT
    for st in stages:
        dst = out if st == last else xs
        if st.startswith("ffn"):
            f = int(st[3]) - 1
            l = int(st[5])
            with k.phase():
                alloc_common(k, vecs_d, NV)
                pre = None
                if f == 1 and l == 0 and "ab" in stages:
                    pre = (ocatT, load_wout(k, wout_ab))
                if f == 1 and l == 1 and "c" in stages:
                    pre = (ocatT, load_wout(k, wout_c))
                gcol = (V_FFN1, V_FFN2)[f][l]
                if st == "ffn2_1":
                    emit_ffn(k, l, f, cur, None, wgu[(l, f)], wd[(l, f)], gcol, final_gain_col=V_FINAL,
                             out_dst=out, pre=pre)
                else:
                    emit_ffn(k, l, f, cur, dst, wgu[(l, f)], wd[(l, f)], gcol, pre=pre)
            cur = dst
        elif st == "ab":
            with k.phase():
                pairs = [("ropeA", qkT[j], None) for j in range(8)] + [("ropeR", qkT[8 + j], None) for j in range(4)]
                pairs += [("silu", sgT[0], sgT[1]), ("silu", sgT[2], sgT[3])]
                emit_inproj(k, cur, V_AB, win_ab, pairs, wtok_ab, 1024,
                            [(0, 512, vtok[:, 0:512]), (512, 512, vtok[:, 512:1024])], pos, cst_d, vecs_d, NV)
            import os
            sub = os.environ.get("AB_SUB", "da,ret")
            if "da" in sub:
              with k.phase():
                emit_diffattn(k, qkT, vtok[:, 0:512], ocatT, vecs_d, NV, V_LQ, V_DANORM, 0.2)
            if "ret" in sub:
              with k.phase():
                emit_retention(k, qkT, vtok[:, 512:1024], sgT, ocatT, vecs_d, NV, cst_d, V_RLF, V_RLB, V_RETNORM)
            if st == last:
                raise ValueError("mixer cannot be last stage")
        elif st == "c":
            emit_gla_all(k, cur, ocatT, win_c, wtok_c, w2aug, pos, cst_d, vecs_d)
    k.gstack.close()
    return k


NPAIR_C = 9


def host_inputs(inputs, b, stages, S):
    g = lambda n: np.asarray(inputs[n], np.float32)
    m = {}
    m["xT"] = np.ascontiguousarray(g("x")[b, :S].T)
    m["pos"] = np.ascontiguousarray(np.asarray(inputs["positions"], np.int32)[:S])
    return m


def host_shared(inputs, stages):
    g = lambda n: np.asarray(inputs[n], np.float32)
    m = {}
    vecs = np.zeros((128, NV), np.float32)
    for l in range(2):
        vecs[:, V_FFN1[l]:V_FFN1[l] + 8] = lay_vec(g("ffn1_norm")[l])
        vecs[:, V_FFN2[l]:V_FFN2[l] + 8] = lay_vec(g("ffn2_norm")[l])
    vecs[:, V_AB:V_AB + 8] = lay_vec(g("ab_norm")[0])
    vecs[:, V_C:V_C + 8] = lay_vec(g("c_norm")[0])
    vecs[:, V_FINAL:V_FINAL + 8] = lay_vec(g("final_norm"))
    vecs[:, V_DANORM] = g("da_norm")[0]
    vecs[:, V_RETNORM] = g("ret_norm")[0]
    vecs[:, V_GLANORM:V_GLANORM + 2] = lay_vec(g("gla_norm")[0])
    vecs[:, V_RLF:V_RLF + 4] = g("ret_logit_f")[0][None, :]
    vecs[:, V_RLB:V_RLB + 4] = g("ret_logit_b")[0][None, :]
    for i, n in enumerate(("da_lq1", "da_lk1", "da_lq2", "da_lk2")):
        vecs[:, V_LQ + 64 * i:V_LQ + 64 * (i + 1)] = g(n)[0][None, :]
    m["vecs"] = vecs
    m["cst"] = build_cst()
    for l in range(2):
        for f in range(2):
            if ("ffn%d_%d" % (f + 1, l)) in stages:
                names = (("ffn1_w_gate", "ffn1_w_up", "ffn1_w_down"), ("ffn2_w_gate", "ffn2_w_up", "ffn2_w_down"))[f]
                m["wgu_%d_%d" % (l, f)] = lay_wgu(g(names[0])[l], g(names[1])[l])
                m["wd_%d_%d" % (l, f)] = lay_wd(g(names[2])[l])
    if "ab" in stages:
        m["win_ab"], m["wtok_ab"] = lay_win_ab(g("ab_w_in")[0])
        m["wout_ab"] = lay_kmajor(g("ab_w_out")[0])
    if "c" in stages:
        m.update(lay_gla(inputs))
    return m


ALL_STAGES = ["ffn1_0", "ab", "ffn2_0", "ffn1_1", "c", "ffn2_1"]


def run(inputs, stages, S, ncores, trace=False):
    k = build_program(S, stages)
    shared = host_shared(inputs, stages)
    in_maps = []
    for b in range(ncores):
        m = dict(shared)
        m.update(host_inputs(inputs, b, stages, S))
        in_maps.append(m)
    res = run_bass_kernel_spmd(k.nc, in_maps, core_ids=list(range(ncores)), trace=trace)
    outs = [np.ascontiguousarray(r["outT"].T) for r in res.results]
    return np.stack(outs, 0), res


def lay_gla(inputs):
    g = lambda n: np.asarray(inputs[n], np.float32)
    W = g("c_w_in")[0]
    q, kk, v, gg = W[:, 0:512], W[:, 512:1024], W[:, 1024:2048], W[:, 2048:3072]
    lr = np.zeros((1024, 128), np.float32)
    lr[:, 0:32] = W[:, 3072:3104]
    zero = np.zeros((1024, 128), np.float32)
    pairs = [(q[:, 0:128], q[:, 128:256]), (q[:, 256:384], q[:, 384:512]),
             (kk[:, 0:128], kk[:, 128:256]), (kk[:, 256:384], kk[:, 384:512])]
    for j in range(4):
        pairs.append((gg[:, j * 256:j * 256 + 128], gg[:, j * 256 + 128:j * 256 + 256]))
    pairs.append((lr, zero))
    win = np.concatenate([lay_wgu(a, b) for a, b in pairs], 0)
    wtok = lay_kmajor(np.concatenate([v, kk], 1))
    w2aug = np.zeros((17, 1024), np.float32)
    w2aug[0:16, 0:512] = g("gla_w2_f")[0]
    w2aug[16, 0:512] = g("gla_b_f")[0]
    w2aug[0:16, 512:1024] = g("gla_w2_b")[0]
    w2aug[16, 512:1024] = g("gla_b_b")[0]
    return {"win_c": np.ascontiguousarray(win), "wtok_c": wtok, "w2aug": w2aug,
            "wout_c": lay_kmajor(g("c_w_out")[0])}


def emit_gla_all(k, x_src, ocatT, win_c, wtok_c, w2aug_d, pos, cst_d, vecs_d):
    S = k.S
    k.arena_off = k.arena_base
    gqT = k.carve([4, 128, S])
    gkT = k.carve([4, 128, S])
    gsgT = k.carve([8, 128, S])
    lrT = k.carve([128, S])
    vtokc = k.carve([S, 1024])
    ktokc = k.carve([S, 512])
    with k.phase():
        pairs = [("copy", gqT[0], gqT[1]), ("copy", gqT[2], gqT[3]), ("copy", gkT[0], gkT[1]), ("copy", gkT[2], gkT[3])]
        pairs += [("silu", gsgT[2 * j], gsgT[2 * j + 1]) for j in range(4)]
        pairs += [("copy", lrT, None)]
        emit_inproj(k, x_src, V_C, win_c, pairs, wtok_c, 1536,
                    [(0, 512, vtokc[:, 0:512]), (512, 512, vtokc[:, 512:1024]), (1024, 512, ktokc)],
                    pos, cst_d, vecs_d, NV)
    import os
    if os.environ.get("C_SUB", "gla") == "gla":
      with k.phase():
        emit_gla(k, gqT, gkT, gsgT, lrT, vtokc, ktokc, ocatT, w2aug_d, cst_d, vecs_d)


def emit_gla(k, gqT, gkT, gsgT, lrT, vtokc, ktokc, ocatT, w2aug_d, cst_d, vecs_d):
    P = k.P
    S = k.S
    NT = S // 128
    NC = S // 64
    NQ = S // 512
    QS = 128 ** -0.5
    qT = k.sb("qT", [128, S], BF16)
    kT = k.sb("kT", [128, S], BF16)
    V = k.sb("V", [128, NT, 256], BF16)
    ktok = k.sb("ktok", [128, NT, 128], BF16)
    sg = k.sb("sg", [128, 2, S], BF16)
    lrA = k.sb("lrA", [17, 2, S], BF16)
    w2s = k.sb("w2s", [17, 1024], F32)
    w2b = k.sb("w2b", [17, 1024], BF16)
    qg = k.sb("qg", [128, S], BF16)
    AT = k.sb("AT", [128, NT, 128], BF16)
    Sb = k.sb("Sb", [128, NC, 256], BF16)
    Sf = k.sb("Sf", [128, 2, 256], F32)
    OF = k.sb("OF", [128, 2, S], F32)
    vecs = k.sb("vecs", [128, NV], F32)
    cst = k.sb("cst", [128, NCST], F32)
    cb = k.sb("cb", [128, 6, 128], BF16)
    ones = k.sb("ones", [128, 128], BF16)
    eps_col = k.sb("epsc", [128, 1], F32)
    one_col = k.sb("onec", [128, 1], F32)
    ez = k.sb("ez", [128, 8, 128], F32)
    la = k.sb("la", [128, 8, 128], BF16)
    EcEw = k.sb("EcEw", [128, 8, 256], F32)
    Ec = EcEw[:, :, 0:128]
    En = k.sb("En", [128, 8, 128], F32)
    Ew = EcEw[:, :, 128:256]
    kneg = k.sb("kneg", [128, 8, 128], BF16)
    klast = k.sb("klast", [128, 8, 128], BF16)
    Oe = k.sb("Oe", [128, 2, 512], F32)
    sqb = k.sb("sqb", [128, 2, 512], BF16)
    rs = k.sb("rs", [128, 512], F32)
    tt_ = k.sb("tt", [128, 512], F32)
    ob = k.sb("ob", [128, 2, 512], BF16)
    ps = k.psum("ps", [128, 8, 512], F32)
    P.dma("sp", "const", lambda e: e.dma_start(out=vecs[:, :], in_=vecs_d), [], [("vecs",)])
    P.dma("sp", "const", lambda e: e.dma_start(out=cst[:, :], in_=cst_d), [], [("cst",)])
    P.dma("sp", "const", lambda e: e.dma_start(out=w2s[:, :], in_=w2aug_d), [], [("w2s",)])
    P.dve(lambda e: e.tensor_copy(w2b[:, :], w2s[:, :]), [("w2s",)], [("w2b",)])
    P.dve(lambda e: e.memset(ones[:, :], 1.0), [], [("ones",)])
    P.dve(lambda e: e.memset(eps_col[:, :], EPS), [], [("eps",)])
    P.dve(lambda e: e.memset(one_col[:, :], 1.0), [], [("onec",)])
    for i, col in enumerate((C_TRI_INC, C_TRI_SUF, C_TRI_SSUF, C_TRI_SPRE)):
        P.dve(lambda e, i=i, col=col: e.tensor_copy(cb[:, i, :], cst[:, col:col + 128]), [("cst",)], [("cb",)])
    P.dve(lambda e: e.memset(lrA[:, :, :], 1.0), [], [("lrA",)])
    for d in range(2):
        P.dma("sp", "const", lambda e, d=d: e.dma_start(out=lrA[0:16, d, :], in_=lrT[d * 16:(d + 1) * 16, :]),
              [], [("lrA",)])
    qrr = [0]

    def qalloc(n):
        i = ((qrr[0] + n - 1) // n) * n % 32
        qrr[0] = i + n
        return i // 4, (i % 4) * 128, [("psq", i + j) for j in range(n)]

    def pk(b):
        return [("psq", b * 4 + q) for q in range(4)]

    for h in range(4):
        P.dma("sp", "hq", lambda e, h=h: e.dma_start(out=qT[:, :], in_=gqT[h]), [], [("qT",)])
        P.dma("sp", "hk", lambda e, h=h: e.dma_start(out=kT[:, :], in_=gkT[h]), [], [("kT",)])
        dma_tiles(k, "hv", V, vtokc[:, h * 256:(h + 1) * 256], NT, ("V",))
        dma_tiles(k, "hkt", ktok, ktokc[:, h * 128:(h + 1) * 128], NT, ("ktok",))
        for e2 in range(2):
            P.dma("sp", "hg", lambda e, h=h, e2=e2: e.dma_start(out=sg[:, e2, :], in_=gsgT[2 * h + e2]), [], [("sg", e2)])
        for d in range(2):
            tri_c, tri_w = (0, 2) if d == 0 else (1, 3)
            mcol = C_MASKLO if d == 0 else C_MASKUP
            tiles = list(range(NT)) if d == 0 else list(range(NT - 1, -1, -1))
            c_init = 0 if d == 0 else NC - 1
            P.dve(lambda e, c_init=c_init: e.memset(Sf[:, 0, :], 0.0), [], [("Sf", 0)])
            P.pool(lambda e, c_init=c_init: e.memset(Sb[:, c_init, :], 0.0), [], [("Sb", c_init)])
            sslot = 0
            def stageA(i, tt):
                u = i % 8
                ts = slice(tt * 128, (tt + 1) * 128)
                bz, cz, kz = qalloc(4)
                mm(k, ps[:, bz, cz:cz + 128], lrA[:, d, ts], w2b[:, d * 512 + h * 128:d * 512 + (h + 1) * 128], True, True,
                   [("lrA",), ("w2b",)], kz)
                P.act(lambda e, bz=bz, cz=cz, u=u: e.activation(ez[:, u, :], ps[:, bz, cz:cz + 128], AF.Exp, scale=-1.0),
                      kz, [("ez", u)])
                P.act(lambda e, u=u: e.activation(ez[:, u, :], ez[:, u, :], AF.Ln, bias=one_col[:, 0:1]),
                      [("ez", u), ("onec",)], [("ez", u)])
                P.dve(lambda e, u=u: e.tensor_scalar(la[:, u, :], ez[:, u, :], -1.0 / 16.0, None, op0=ALU.mult),
                      [("ez", u)], [("la", u)])

            def stageB(i, tt):
                u = i % 8
                ts = slice(tt * 128, (tt + 1) * 128)
                bc, cc_, kc_ = qalloc(4)
                mm(k, ps[:, bc, cc_:cc_ + 128], la[:, u, :], cb[:, tri_c, :], True, True, [("la", u), ("cb",)], kc_)
                mm(k, ps[:, bc, cc_ + 128:cc_ + 256], cb[:, tri_w, :], la[:, u, :], True, True, [("la", u), ("cb",)], kc_)
                P.act(lambda e, bc=bc, cc_=cc_, u=u: e.activation(EcEw[:, u, :], ps[:, bc, cc_:cc_ + 256], AF.Exp),
                      kc_, [("Ec", u), ("Ew", u)])
                P.act(lambda e, bc=bc, cc_=cc_, u=u: e.activation(En[:, u, :], ps[:, bc, cc_:cc_ + 128], AF.Exp, scale=-1.0),
                      kc_, [("En", u)])
                P.dve(lambda e, u=u, ts=ts: e.scalar_tensor_tensor(qg[:, ts], Ec[:, u, :], QS, qT[:, ts],
                                                                   op0=ALU.mult, op1=ALU.mult),
                      [("Ec", u), ("qT",)], [("qg", tt)])
                P.pool(lambda e, u=u, ts=ts: e.tensor_tensor(kneg[:, u, :], En[:, u, :], kT[:, ts], op=ALU.mult),
                       [("En", u), ("kT",)], [("kneg", u)])
                P.pool(lambda e, u=u, tt=tt: e.tensor_tensor(klast[:, u, :], Ew[:, u, :], ktok[:, tt, :], op=ALU.mult),
                       [("Ew", u), ("ktok",)], [("klast", u)])

            def stageC(i, tt):
                nonlocal sslot
                u = i % 8
                ts = slice(tt * 128, (tt + 1) * 128)
                bs, cs_, ks_ = qalloc(4)
                mm(k, ps[:, bs, cs_:cs_ + 128], kneg[:, u, :], qg[:, ts], True, True, [("kneg", u), ("qg", tt)], ks_)
                P.dve(lambda e, bs=bs, cs_=cs_, tt=tt, mcol=mcol: e.tensor_tensor(AT[:, tt, :], ps[:, bs, cs_:cs_ + 128],
                                                                                  cst[:, mcol:mcol + 128], op=ALU.mult),
                      ks_ + [("cst",)], [("AT", tt)])
                chunks = (0, 1) if d == 0 else (1, 0)
                for cc in chunks:
                    c = tt * 2 + cc
                    cn = c + 1 if d == 0 else c - 1
                    if cn < 0 or cn >= NC:
                        continue
                    pr = slice(cc * 64, cc * 64 + 64)
                    bk, ck_, kk_ = qalloc(4)
                    mm(k, ps[:, bk, ck_:ck_ + 256], klast[pr, u, :], V[pr, tt, :], True, True, [("klast", u), ("V",)], kk_)
                    dcol = (cc * 64 + 63) if d == 0 else (cc * 64)
                    so, sn = sslot % 2, (sslot + 1) % 2
                    sslot += 1
                    P.dve(lambda e, bk=bk, ck_=ck_, u=u, dcol=dcol, so=so, sn=sn: e.scalar_tensor_tensor(
                        Sf[:, sn, :], Sf[:, so, :], Ec[:, u, dcol:dcol + 1], ps[:, bk, ck_:ck_ + 256],
                        op0=ALU.mult, op1=ALU.add),
                        [("Sf", so), ("Ec", u)] + kk_, [("Sf", sn)])
                    if cc == 0:
                        P.act(lambda e, sn=sn, cn=cn: e.copy(Sb[:, cn, :], Sf[:, sn, :]), [("Sf", sn)], [("Sb", cn)])
                    else:
                        P.dve(lambda e, sn=sn, cn=cn: e.tensor_copy(Sb[:, cn, :], Sf[:, sn, :]), [("Sf", sn)], [("Sb", cn)])

            nt_ = len(tiles)
            for step in range(nt_ + 4):
                if step < nt_:
                    stageA(step, tiles[step])
                if 2 <= step < nt_ + 2:
                    stageB(step - 2, tiles[step - 2])
                if step >= 4:
                    stageC(step - 4, tiles[step - 4])
            for g in range(NQ):
                gs = slice(g * 512, (g + 1) * 512)
                be = (k.bank(), k.bank())
                for j in range(4):
                    tt = g * 4 + j
                    for e2 in range(2):
                        js = slice(j * 128, (j + 1) * 128)
                        mm(k, ps[:, be[e2], js], V[:, tt, e2 * 128:(e2 + 1) * 128], AT[:, tt, :], True, False,
                           [("V",), ("AT", tt)], pk(be[e2]))
                        for cc in range(2):
                            c = tt * 2 + cc
                            cs = slice(c * 64, (c + 1) * 64)
                            jc = slice(j * 128 + cc * 64, j * 128 + cc * 64 + 64)
                            mm(k, ps[:, be[e2], jc], Sb[:, c, e2 * 128:(e2 + 1) * 128], qg[:, cs], False, cc == 1,
                               [("Sb", c), ("qg", tt)], pk(be[e2]))
                if d == 0:
                    for e2 in range(2):
                        P.act(lambda e, e2=e2, gs=gs, b=be[e2]: e.copy(OF[:, e2, gs], ps[:, b, :]),
                              pk(be[e2]), [("OF", e2, g)])
                else:
                    for e2 in range(2):
                        P.dve(lambda e, e2=e2, gs=gs, b=be[e2]: e.tensor_tensor(Oe[:, e2, :], OF[:, e2, gs], ps[:, b, :],
                                                                                op=ALU.add),
                              pk(be[e2]) + [("OF", e2, g)], [("Oe", e2)])
                    emit_pnorm(k, [(Oe[:, 0, :], ("Oe", 0)), (Oe[:, 1, :], ("Oe", 1))], 512, ps, ones, sqb, rs, 2,
                               256.0, eps_col, pk=pk)
                    for e2 in range(2):
                        P.dve(lambda e, e2=e2: e.scalar_tensor_tensor(
                            tt_[:, :], Oe[:, e2, :], vecs[:, V_GLANORM + e2:V_GLANORM + e2 + 1], rs[:, :],
                            op0=ALU.mult, op1=ALU.mult), [("Oe", e2), ("vecs",), ("pn_rs",)], [("tt",)])
                        k.uid += 1
                        oo = k.uid % 2
                        P.dve(lambda e, oo=oo, e2=e2, gs=gs: e.tensor_tensor(ob[:, oo, :], tt_[:, :], sg[:, e2, gs],
                                                                            op=ALU.mult),
                              [("tt",), ("sg", e2)], [("ob", oo)])
                        P.dma("sp", ("ob", oo), lambda e, oo=oo, h=h, e2=e2, gs=gs: e.dma_start(
                            out=ocatT[2 * h + e2][:, gs], in_=ob[:, oo, :]), [("ob", oo)], [("ocat", h, e2, g)])


def run_core_inputs(inputs, stages, S, ncores, x_override=None, trace=False):
    k = build_program(S, stages)
    shared = host_shared(inputs, stages)
    in_maps = []
    for b in range(ncores):
        m = dict(shared)
        m.update(host_inputs(inputs, b, stages, S))
        if x_override is not None:
            m["xT"] = x_override[b]
        in_maps.append(m)
    res = run_bass_kernel_spmd(k.nc, in_maps, core_ids=list(range(ncores)), trace=trace)
    return [r["outT"] for r in res.results], res


LAUNCHES = [ALL_STAGES]


def kernel(**inputs):
    xo = None
    for stages in LAUNCHES:
        xo, _ = run_core_inputs(inputs, stages, 4096, 8, x_override=xo)
    return np.stack([np.ascontiguousarray(o.T) for o in xo], 0).astype(np.float32)
```

```python
import math
from contextlib import ExitStack

import numpy as np
import ml_dtypes

import concourse.bass as bass
import concourse.mybir as mybir
from concourse.bass_utils import run_bass_kernel_spmd

F32 = mybir.dt.float32
BF16 = mybir.dt.bfloat16
I32 = mybir.dt.int32
AF = mybir.ActivationFunctionType
ALU = mybir.AluOpType

D = 1024
DFF = 2816
NF = DFF // 128
EPS = 1e-6


class Op:
    __slots__ = ("eng", "fn", "reads", "writes", "dma", "deps", "signal", "sigidx", "grp", "cum", "idx")

    def __init__(self, eng, fn, reads, writes, dma, grp):
        self.eng = eng
        self.fn = fn
        self.reads = reads
        self.writes = writes
        self.dma = dma
        self.grp = grp
        self.deps = []
        self.signal = False
        self.sigidx = 0
        self.cum = 0


class Prog:
    ENGS = ("pe", "act", "dve", "pool", "sp")

    def __init__(self, nc, tag=0):
        self.nc = nc
        self.tag = tag
        self.ops = []
        self.last_writer = {}
        self.readers = {}
        self.grp_cum = {}

    def _add(self, eng, fn, reads, writes, dma=False, grp=None):
        op = Op(eng, fn, tuple(reads), tuple(writes), dma, grp)
        op.idx = len(self.ops)
        deps = {}
        for r in op.reads:
            w = self.last_writer.get(r)
            if w is not None:
                deps[id(w)] = (w, "raw")
        for wkey in op.writes:
            w = self.last_writer.get(wkey)
            if w is not None and id(w) not in deps:
                deps[id(w)] = (w, "waw")
            for rd in self.readers.get(wkey, ()):
                if id(rd) not in deps:
                    deps[id(rd)] = (rd, "war")
        best = {}
        for d, kind in deps.values():
            if d is op:
                continue
            if not d.dma and d.eng == eng and not dma:
                if eng == "pe" or kind != "raw":
                    continue
            key = ("g", d.grp) if d.dma else ("e", d.eng)
            cur = best.get(key)
            if cur is None or d.idx > cur.idx:
                best[key] = d
        for d in best.values():
            op.deps.append(d)
            if not d.dma:
                d.signal = True
        if dma:
            c = self.grp_cum.get(grp, 0) + 16
            self.grp_cum[grp] = c
            op.cum = c
        for r in op.reads:
            self.readers.setdefault(r, []).append(op)
        for wkey in op.writes:
            self.last_writer[wkey] = op
            self.readers[wkey] = []
        self.ops.append(op)
        return op

    def pe(self, fn, reads, writes):
        return self._add("pe", fn, reads, writes)

    def act(self, fn, reads, writes):
        return self._add("act", fn, reads, writes)

    def dve(self, fn, reads, writes):
        return self._add("dve", fn, reads, writes)

    def pool(self, fn, reads, writes):
        return self._add("pool", fn, reads, writes)

    def dma(self, q, grp, fn, reads, writes):
        if grp == "const":
            writes = list(writes) + [("constchain",)]
        return self._add(q, fn, reads, writes, dma=True, grp=grp)

    def emit(self, stack, shared):
        nc = self.nc
        esem = shared["esem"]
        ecount = shared["ecount"]
        gslots = shared["gsem"]
        gcount = shared["gcount"]
        assert len(self.grp_cum) <= len(gslots), len(self.grp_cum)
        gidx = {g: i for i, g in enumerate(self.grp_cum)}
        gsem = {g: gslots[i] for g, i in gidx.items()}
        gbase = {g: gcount[i] for g, i in gidx.items()}
        cnt = dict(ecount)
        for op in self.ops:
            if not op.dma and op.signal:
                cnt[op.eng] += 1
                op.sigidx = cnt[op.eng]
            if op.dma:
                op.cum += gbase[op.grp]
        for e in self.ENGS:
            ecount[e] = cnt[e]
        for g, i in gidx.items():
            gcount[i] += self.grp_cum[g]
        per = {e: [] for e in self.ENGS}
        for op in self.ops:
            per[op.eng].append(op)
        final = [(gsem[g], gbase[g] + c) for g, c in self.grp_cum.items()]
        block = stack.enter_context(nc.Block())

        def body(ename):
            def f(eng):
                waited = {}
                for op in per[ename]:
                    need = []
                    for d in op.deps:
                        if d.dma:
                            key = ("g", d.grp)
                            val = d.cum
                            sem = gsem[d.grp]
                        else:
                            key = ("e", d.eng)
                            val = d.sigidx
                            sem = esem[d.eng]
                        if waited.get(key, 0) >= val:
                            continue
                        waited[key] = val
                        need.append((sem, val))
                    for sem, val in need[:-1]:
                        eng.wait_ge(sem, val)
                    ins = op.fn(eng)
                    if need:
                        ins._wait_ge(*need[-1])
                    if op.dma:
                        ins.then_inc(gsem[op.grp], 16)
                    elif op.signal:
                        ins.then_inc(esem[op.eng], 1)
                if ename == "sp":
                    for sem, c in final:
                        eng.wait_ge(sem, c)
            return f

        block.tensor(body("pe"))
        block.scalar(body("act"))
        block.vector(body("dve"))
        block.gpsimd(body("pool"))
        block.sync(body("sp"))


class K:
    def __init__(self, S):
        self.S = S
        self.nc = bass.Bass("TRN2", target_bir_lowering=False)
        self.P = None
        self.stack = None
        self.bank_rr = 0
        self.uid = 0
        self.nphase = 0
        self.gstack = ExitStack()
        nc = self.nc
        self.shared = {
            "esem": {e: self.gstack.enter_context(nc.semaphore("sem_" + e)) for e in Prog.ENGS},
            "ecount": {e: 0 for e in Prog.ENGS},
            "gsem": [self.gstack.enter_context(nc.semaphore("gsem%d" % i)) for i in range(28)],
            "gcount": [0] * 28,
        }

    def phase(self):
        return _Phase(self)

    def sb(self, name, shape, dt):
        return self.stack.enter_context(self.nc.sbuf_tensor("%s_p%d" % (name, self.nphase), shape, dt))

    def psum(self, name, shape, dt):
        return self.stack.enter_context(self.nc.psum_tensor("%s_p%d" % (name, self.nphase), shape, dt))

    def dram(self, name, shape, dt, kind="Internal"):
        import os
        kind = os.environ.get("SCRATCH_KIND", kind)
        return self.nc.dram_tensor(name, shape, dt, kind=kind).ap()

    def carve(self, shape):
        n = int(np.prod(shape))
        a = self.arena[self.arena_off:self.arena_off + n]
        self.arena_off += n
        if len(shape) == 3:
            return a.rearrange("(j p s) -> j p s", p=shape[1], s=shape[2])
        return a.rearrange("(p s) -> p s", s=shape[1])

    def bank(self, n=8, base=0):
        b = base + self.bank_rr % n
        self.bank_rr += 1
        return b


class _Phase:
    def __init__(self, k):
        self.k = k

    def __enter__(self):
        k = self.k
        k.nphase += 1
        k.P = Prog(k.nc, k.nphase)
        k.stack = ExitStack()
        k.stack.__enter__()
        k.stack.enter_context(k.nc.named_scope("ph%d" % k.nphase))
        k.bank_rr = 0
        k.uid = 0
        return k

    def __exit__(self, et, ev, tb):
        k = self.k
        if et is None:
            k.P.emit(k.stack, k.shared)
        k.stack.__exit__(et, ev, tb)
        return False


def mm(k, out, lhsT, rhs, start, stop, reads, writes):
    k.P.pe(lambda e, out=out, lhsT=lhsT, rhs=rhs, start=start, stop=stop:
           e.matmul(out, lhsT, rhs, start=start, stop=stop), reads, writes)


def emit_norm(k, x_sb, xn, T, gain_col, vecs, sqbuf, sqkey, rstd, ones_bf, ps, tmp_key="nrm"):
    P = k.P
    nsub = T // 512
    for kc in range(8):
        P.act(lambda e, kc=kc: e.activation(sqbuf[:, kc, :T], x_sb[:, kc, :T], AF.Square),
              [("x", kc)], [(sqkey, kc, s) for s in range(nsub)])
    for s in range(nsub):
        b = k.bank()
        sl = slice(s * 512, (s + 1) * 512)
        for kc in range(8):
            mm(k, ps[:, b, :], ones_bf[:, :], sqbuf[:, kc, sl], kc == 0, kc == 7,
               [(sqkey, kc, s), ("ones",)], [("ps", b)])
        P.act(lambda e, b=b, sl=sl: e.activation(rstd[:, sl], ps[:, b, :], AF.Ln,
                                                 bias=k.eps_col[:, 0:1], scale=1.0 / D),
              [("ps", b), ("eps",)], [("rstd", s)])
        P.act(lambda e, sl=sl: e.activation(rstd[:, sl], rstd[:, sl], AF.Exp, scale=-0.5), [("rstd", s)], [("rstd", s)])
        for kc in range(8):
            P.dve(lambda e, kc=kc, sl=sl: e.scalar_tensor_tensor(
                xn[:, kc, sl], x_sb[:, kc, sl], vecs[:, gain_col + kc:gain_col + kc + 1], rstd[:, sl],
                op0=ALU.mult, op1=ALU.mult),
                [("x", kc), ("rstd", s), ("vecs",)], [("xn", kc, s)])


def emit_ffn(k, li, fi, x_src, x_dst, wgu_d, wd_d, gain_col, final_gain_col=None, out_dst=None, pre=None):
    P = k.P
    S = k.S
    T = min(1024, S)
    nsub = T // 512
    B = k.bufs
    x_sb, xn, h, rstd, stmp, ps, vecs, ones_bf = (B["x"], B["xn"], B["h"], B["rstd"], B["stmp"],
                                                  B["ps"], B["vecs"], B["ones"])
    wgs, wgb, wds, wdb = B["wgs"], B["wgb"], B["wds"], B["wdb"]
    NG = S // T

    def load_wg(n):
        if n >= NG * NF:
            return
        f_, ss = n % NF, n % 2
        P.dma("sp", ("wgs", ss), lambda e, f_=f_, ss=ss: e.dma_start(out=wgs[:, ss, :], in_=wgu_d[f_]),
              [], [("wgs", ss)])

    def load_wd(m):
        if m >= NG * 8:
            return
        dm_, ss = m % 8, m % 2
        P.dma("sp", ("wds", ss), lambda e, dm_=dm_, ss=ss: e.dma_start(out=wds[:, ss, :], in_=wd_d[dm_]),
              [], [("wds", ss)])

    def load_xk(g_, kc):
        if g_ >= NG:
            return
        src = x_src[kc * 128:(kc + 1) * 128, g_ * T:(g_ + 1) * T]
        P.dma("sp", ("xld", kc), lambda e, src=src, kc=kc: e.dma_start(out=x_sb[:, kc, :T], in_=src),
              [("xdram", id(x_src), g_)], [("x", kc)])

    for kc in range(8):
        load_xk(0, kc)
    sqb_ = B["sq"]
    for g in range(S // T):
        tsl = slice(g * T, (g + 1) * T)
        if pre is not None:
            ocatT, woutb = pre
            oc_v = ocatT[:, :, tsl].rearrange("kc p t -> p kc t")
            P.dma("sp", "ocld", lambda e, oc_v=oc_v: e.dma_start(out=xn[:, :, :T], in_=oc_v),
                  [("ocdram", g)], [("xn", kc, s) for kc in range(8) for s in range(nsub)])
            for dm in range(8):
                for s in range(nsub):
                    sl = slice(s * 512, (s + 1) * 512)
                    b = k.bank()
                    for kc in range(8):
                        mm(k, ps[:, b, :], woutb[:, kc, dm * 128:(dm + 1) * 128], xn[:, kc, sl], kc == 0, kc == 7,
                           [("woutb",), ("xn", kc, s)], [("ps", b)])
                    P.dve(lambda e, b=b, dm=dm, sl=sl: e.tensor_tensor(
                        x_sb[:, dm, sl], x_sb[:, dm, sl], ps[:, b, :], op=ALU.add),
                        [("ps", b), ("x", dm)], [("x", dm)])
        emit_norm(k, x_sb, xn, T, gain_col, vecs, sqb_, "sq", rstd, ones_bf, ps)
        for f in range(NF):
            n = g * NF + f
            slot_s = n % 2
            slot_b = n % 3
            if n == 0:
                load_wg(0)
                load_wg(1)
                load_wd(0)
                load_wd(1)
            P.pool(lambda e, slot_s=slot_s, slot_b=slot_b: e.tensor_copy(wgb[:, slot_b, :], wgs[:, slot_s, :]),
                   [("wgs", slot_s)], [("wgb", slot_b)])
            load_wg(n + 2)
            for s in range(nsub):
                sl = slice(s * 512, (s + 1) * 512)
                bg = k.bank()
                bu = k.bank()
                for gu, b in ((0, bg), (1, bu)):
                    for kc in range(8):
                        o = (gu * 8 + kc) * 128
                        mm(k, ps[:, b, :], wgb[:, slot_b, o:o + 128], xn[:, kc, sl], kc == 0, kc == 7,
                           [("wgb", slot_b), ("xn", kc, s)], [("ps", b)])
                st = k.uid % 2
                k.uid += 1
                P.act(lambda e, bg=bg, st=st: e.activation(stmp[:, st, :], ps[:, bg, :], AF.Silu),
                      [("ps", bg)], [("stmp", st)])
                P.dve(lambda e, bu=bu, st=st, f=f, sl=sl: e.tensor_tensor(
                    h[:, f, sl], stmp[:, st, :], ps[:, bu, :], op=ALU.mult),
                    [("ps", bu), ("stmp", st)], [("h", f, s)])
        for dm in range(8):
            m = g * 8 + dm
            slot = m % 2
            P.act(lambda e, slot=slot: e.copy(wdb[:, slot, :], wds[:, slot, :]),
                  [("wds", slot)], [("wdb", slot)])
            load_wd(m + 2)
            for s in range(nsub):
                sl = slice(s * 512, (s + 1) * 512)
                b = k.bank()
                for f in range(NF):
                    mm(k, ps[:, b, :], wdb[:, slot, f * 128:(f + 1) * 128], h[:, f, sl], f == 0, f == NF - 1,
                       [("wdb", slot), ("h", f, s)], [("ps", b)])
                P.dve(lambda e, b=b, dm=dm, sl=sl: e.scalar_tensor_tensor(
                    x_sb[:, dm, sl], ps[:, b, :], 0.5, x_sb[:, dm, sl], op0=ALU.mult, op1=ALU.add),
                    [("ps", b), ("x", dm)], [("x", dm)])
            if final_gain_col is None:
                dst = x_dst[dm * 128:(dm + 1) * 128, tsl]
                P.dma("sp", ("xst", dm), lambda e, dst=dst, dm=dm: e.dma_start(out=dst, in_=x_sb[:, dm, :T]),
                      [("x", dm)], [("xdram", id(x_dst), g)])
                load_xk(g + 1, dm)
        if final_gain_col is None:
            pass
        else:
            P2 = k.P
            for kc in range(8):
                P2.act(lambda e, kc=kc: e.activation(sqb_[:, kc, :T], x_sb[:, kc, :T], AF.Square),
                       [("x", kc)], [("sq", kc, s) for s in range(nsub)])
            for s in range(nsub):
                b = k.bank()
                sl = slice(s * 512, (s + 1) * 512)
                for kc in range(8):
                    mm(k, ps[:, b, :], ones_bf[:, :], sqb_[:, kc, sl], kc == 0, kc == 7, [("sq", kc, s), ("ones",)], [("ps", b)])
                P2.act(lambda e, b=b, sl=sl: e.activation(rstd[:, sl], ps[:, b, :], AF.Ln,
                                                          bias=k.eps_col[:, 0:1], scale=1.0 / D),
                       [("ps", b), ("eps",)], [("rstd", s)])
                P2.act(lambda e, sl=sl: e.activation(rstd[:, sl], rstd[:, sl], AF.Exp, scale=-0.5),
                       [("rstd", s)], [("rstd", s)])
            for s in range(nsub):
                sl = slice(s * 512, (s + 1) * 512)
                for kc in range(8):
                    P2.dve(lambda e, kc=kc, sl=sl: e.scalar_tensor_tensor(
                        x_sb[:, kc, sl], x_sb[:, kc, sl], vecs[:, final_gain_col + kc:final_gain_col + kc + 1],
                        rstd[:, sl], op0=ALU.mult, op1=ALU.mult),
                        [("x", kc), ("rstd", s), ("vecs",)], [("x", kc)])
            od_v = out_dst[:, tsl].rearrange("(kc p) t -> p kc t", p=128)
            P2.dma("sp", "xst", lambda e, od_v=od_v: e.dma_start(out=od_v, in_=x_sb[:, :, :T]),
                   [("x", kc) for kc in range(8)], [("odram", g)])
            for kc in range(8):
                load_xk(g + 1, kc)


def load_wout(k, wout_d):
    B = k.bufs
    woutb = k.sb("woutb", [128, 8, 1024], BF16)
    for kc in range(8):
        slot = k.wd_ctr % 2
        k.wd_ctr += 1
        k.P.dma("sp", ("wds", slot), lambda e, kc=kc, slot=slot: e.dma_start(
            out=B["wds"][:, slot, 0:1024], in_=wout_d[:, kc, :]), [], [("wds", slot)])
        k.P.act(lambda e, kc=kc, slot=slot: e.copy(woutb[:, kc, :], B["wds"][:, slot, 0:1024]),
                [("wds", slot)], [("woutb",)])
    return woutb


def alloc_common(k, vecs_d, NV):
    S = k.S
    T = min(1024, S)
    B = {}
    B["x"] = k.sb("x_sb", [128, 8, T], F32)
    B["xn"] = k.sb("xn", [128, 8, T], BF16)
    B["h"] = k.sb("h", [128, NF, T], BF16)
    B["sq"] = k.sb("sq", [128, 8, T], BF16)
    B["rstd"] = k.sb("rstd", [128, T], F32)
    B["stmp"] = k.sb("stmp", [128, 2, 512], F32)
    B["wgs"] = k.sb("wgs", [128, 2, 2048], F32)
    B["wgb"] = k.sb("wgb", [128, 3, 2048], BF16)
    B["wds"] = k.sb("wds", [128, 2, NF * 128], F32)
    B["wdb"] = k.sb("wdb", [128, 2, NF * 128], BF16)
    B["vecs"] = k.sb("vecs", [128, NV], F32)
    B["ones"] = k.sb("ones", [128, 128], BF16)
    B["ps"] = k.psum("ps", [128, 8, 512], F32)
    k.eps_col = k.sb("epsc", [128, 1], F32)
    k.bufs = B
    k.wg_ctr = 0
    k.wd_ctr = 0
    P = k.P
    P.dma("sp", "const", lambda e: e.dma_start(out=B["vecs"][:, :], in_=vecs_d), [], [("vecs",)])
    P.dve(lambda e: e.memset(B["ones"][:, :], 1.0), [], [("ones",)])
    P.dve(lambda e: e.memset(k.eps_col[:, :], EPS), [], [("eps",)])
    return B


TWO_PI = 2.0 * math.pi
CW1 = 6.28125
CW2 = float(np.float32(TWO_PI - 6.28125))

C_INVA, C_SGNA, C_INVR, C_SGNR, C_127MP, C_P = 0, 1, 2, 3, 4, 5
C_DPOS, C_DNEG, C_MGE, C_MLE, C_IOTA1, C_IOTAR, C_IDENT = 8, 136, 264, 392, 520, 648, 776
C_TRI_INC, C_TRI_SUF, C_MASKLO, C_MASKUP, C_TRI_SSUF, C_TRI_SPRE = 904, 1032, 1160, 1288, 1416, 1544
NCST = 1672


def emit_rope_tables(k, posf, T, cst, inv_col, sgn_col, Ctab, Stab, tmp, key):
    P = k.P
    ang, yi, kf, th, w = tmp["ang"], tmp["yi"], tmp["kf"], tmp["th"], tmp["w"]
    R = [("posf",), ("cst",)]
    P.dve(lambda e: e.tensor_scalar(ang[:, :T], posf[:, :T], cst[:, inv_col:inv_col + 1], None, op0=ALU.mult),
           R, [("t_ang",)])
    P.dve(lambda e: e.tensor_scalar(yi[:, :T], ang[:, :T], 1.0 / TWO_PI, None, op0=ALU.mult),
           [("t_ang",)], [("t_yi",)])
    P.dve(lambda e: e.tensor_copy(kf[:, :T], yi[:, :T]), [("t_yi",)], [("t_kf",)])
    P.dve(lambda e: e.tensor_scalar(th[:, :T], kf[:, :T], -CW1, None, op0=ALU.mult), [("t_kf",)], [("t_th",)])
    P.dve(lambda e: e.tensor_tensor(th[:, :T], th[:, :T], ang[:, :T], op=ALU.add), [("t_th",), ("t_ang",)], [("t_th",)])
    P.dve(lambda e: e.tensor_scalar(kf[:, :T], kf[:, :T], -CW2, None, op0=ALU.mult), [("t_kf",)], [("t_kf",)])
    P.dve(lambda e: e.tensor_tensor(th[:, :T], th[:, :T], kf[:, :T], op=ALU.add), [("t_th",), ("t_kf",)], [("t_th",)])
    P.dve(lambda e: e.tensor_scalar(w[:, :T], th[:, :T], math.pi, TWO_PI, op0=ALU.is_gt, op1=ALU.mult),
           [("t_th",)], [("t_w",)])
    P.dve(lambda e: e.tensor_tensor(w[:, :T], th[:, :T], w[:, :T], op=ALU.subtract), [("t_th",), ("t_w",)], [("t_w",)])
    P.act(lambda e: e.activation(Stab[:, :T], w[:, :T], AF.Sin, scale=cst[:, sgn_col:sgn_col + 1]),
          [("t_w",), ("cst",)], [(key, "S")])
    P.dve(lambda e: e.tensor_scalar(w[:, :T], th[:, :T], math.pi / 2, TWO_PI, op0=ALU.is_gt, op1=ALU.mult),
           [("t_th",)], [("t_w",)])
    P.dve(lambda e: e.tensor_tensor(w[:, :T], th[:, :T], w[:, :T], op=ALU.subtract), [("t_th",), ("t_w",)], [("t_w",)])
    P.act(lambda e: e.activation(Ctab[:, :T], w[:, :T], AF.Sin, bias=k.halfpi_col[:, 0:1]),
          [("t_w",), ("hpi",)], [(key, "C")])


def emit_inproj(k, x_src, gain_col, win_d, pairs, wtok_d, ntokc, tok_dsts, pos_d, cst_d, vecs_d, NV):
    P = k.P
    S = k.S
    T = min(1024, S)
    nsub = T // 512
    x_sb = k.sb("x", [128, 8, T], F32)
    xn = k.sb("xn", [128, 8, T], BF16)
    sq = k.sb("sq", [128, 8, T], BF16)
    rstd = k.sb("rstd", [128, T], F32)
    wgs = k.sb("wgs", [128, 2, 2048], F32)
    wgb = k.sb("wgb", [128, 3, 2048], BF16)
    wts = k.sb("wts", [128, 2, ntokc], F32)
    wtok = k.sb("wtok", [128, 8, ntokc], BF16)
    vecs = k.sb("vecs", [128, NV], F32)
    cst = k.sb("cst", [128, NCST], F32)
    ones = k.sb("ones", [128, 128], BF16)
    ost = k.sb("ost", [128, 8, 512], BF16)
    vst = k.sb("vst", [128, 8, 512], BF16)
    t12 = k.sb("t12", [128, 4, 512], F32)
    ps = k.psum("ps", [128, 8, 512], F32)
    k.eps_col = k.sb("epsc", [128, 1], F32)
    k.halfpi_col = k.sb("hpic", [128, 1], F32)
    rope = any(p[0].startswith("rope") for p in pairs)
    P.dma("sp", "const", lambda e: e.dma_start(out=vecs[:, :], in_=vecs_d), [], [("vecs",)])
    P.dma("sp", "const", lambda e: e.dma_start(out=cst[:, :], in_=cst_d), [], [("cst",)])
    P.dve(lambda e: e.memset(ones[:, :], 1.0), [], [("ones",)])
    P.dve(lambda e: e.memset(k.eps_col[:, :], EPS), [], [("eps",)])
    P.dve(lambda e: e.memset(k.halfpi_col[:, :], math.pi / 2), [], [("hpi",)])
    if rope:
        posi = k.sb("posi", [128, T], I32)
        posf = k.sb("posf", [128, T], F32)
        tabs = {n: k.sb("tab" + n, [128, T], F32) for n in ("CA", "SA", "CR", "SR")}
        tmp = {"ang": k.sb("ang", [128, T], F32), "yi": k.sb("yi", [128, T], I32),
               "kf": k.sb("kf", [128, T], F32), "th": k.sb("th", [128, T], F32), "w": k.sb("w", [128, T], F32)}
    for kc in range(8):
        sl = kc % 2
        P.dma("sp", ("wts", sl), lambda e, kc=kc, sl=sl: e.dma_start(out=wts[:, sl, :], in_=wtok_d[:, kc, :]),
              [], [("wts", sl)])
        P.act(lambda e, kc=kc, sl=sl: e.copy(wtok[:, kc, :], wts[:, sl, :]), [("wts", sl)], [("wtok", kc)])
    wg_ctr = 0
    n_w = (S // T) * len(pairs)

    def load_w(n):
        if n >= n_w:
            return
        pi_, ss = n % len(pairs), n % 2
        P.dma("sp", ("wgs", ss), lambda e, pi_=pi_, ss=ss: e.dma_start(out=wgs[:, ss, :], in_=win_d[pi_]),
              [], [("wgs", ss)])

    def load_x(g_):
        if g_ >= S // T:
            return
        tsl_ = slice(g_ * T, (g_ + 1) * T)
        xs_v = x_src[:, tsl_].rearrange("(kc p) t -> p kc t", p=128)
        P.dma("sp", "xld", lambda e, xs_v=xs_v: e.dma_start(out=x_sb[:, :, :], in_=xs_v),
              [("xdram", g_)], [("x", kc) for kc in range(8)])
        if rope:
            P.dma("sp", "pos", lambda e, tsl_=tsl_: e.dma_start(out=posi[:, :], in_=pos_d[tsl_].partition_broadcast(128)),
                  [], [("posi",)])

    load_x(0)
    for g in range(S // T):
        tsl = slice(g * T, (g + 1) * T)
        if rope:
            P.dve(lambda e: e.tensor_copy(posf[:, :], posi[:, :]), [("posi",)], [("posf",)])
            emit_rope_tables(k, posf, T, cst, C_INVA, C_SGNA, tabs["CA"], tabs["SA"], tmp, "tabA")
            emit_rope_tables(k, posf, T, cst, C_INVR, C_SGNR, tabs["CR"], tabs["SR"], tmp, "tabR")
        emit_norm(k, x_sb, xn, T, gain_col, vecs, sq, "sq", rstd, ones, ps)
        load_x(g + 1)
        for pi, (kind, d0, d1) in enumerate(pairs):
            slot_s = wg_ctr % 2
            slot_b = wg_ctr % 3
            wg_ctr += 1
            if wg_ctr == 1:
                load_w(0)
                load_w(1)
            P.act(lambda e, slot_s=slot_s, slot_b=slot_b: e.copy(wgb[:, slot_b, :], wgs[:, slot_s, :]),
                  [("wgs", slot_s)], [("wgb", slot_b)])
            load_w(wg_ctr + 1)
            for s in range(nsub):
                sl = slice(s * 512, (s + 1) * 512)
                dsl = slice(g * T + s * 512, g * T + (s + 1) * 512)
                b0 = k.bank()
                b1 = k.bank()
                for half, b in ((0, b0), (1, b1)):
                    for kc in range(8):
                        o = (half * 8 + kc) * 128
                        mm(k, ps[:, b, :], wgb[:, slot_b, o:o + 128], xn[:, kc, sl], kc == 0, kc == 7,
                           [("wgb", slot_b), ("xn", kc, s)], [("ps", b)])
                if kind.startswith("rope"):
                    tk = "tab" + kind[-1]
                    Ct, St = tabs["C" + kind[-1]], tabs["S" + kind[-1]]
                    u = k.uid % 2
                    k.uid += 1
                    os_ = k.uid % 8
                    P.dve(lambda e, b0=b0, u=u, sl=sl, Ct=Ct: e.tensor_tensor(
                        t12[:, 2 * u, :], ps[:, b0, :], Ct[:, sl], op=ALU.mult),
                        [("ps", b0), (tk, "C")], [("t12", 2 * u)])
                    P.dve(lambda e, b1=b1, u=u, sl=sl, St=St: e.tensor_tensor(
                        t12[:, 2 * u + 1, :], ps[:, b1, :], St[:, sl], op=ALU.mult),
                        [("ps", b1), (tk, "S")], [("t12", 2 * u + 1)])
                    P.pool(lambda e, u=u, os_=os_: e.tensor_tensor(
                        ost[:, os_, :], t12[:, 2 * u, :], t12[:, 2 * u + 1, :], op=ALU.add),
                        [("t12", 2 * u), ("t12", 2 * u + 1)], [("ost", os_)])
                    P.dma("sp", ("ost", os_), lambda e, os_=os_, d0=d0, dsl=dsl: e.dma_start(
                        out=d0[:, dsl], in_=ost[:, os_, :]), [("ost", os_)], [("sc_fm", id(d0), g)])
                else:
                    fn = AF.Silu if kind == "silu" else AF.Copy
                    for b, dd in ((b0, d0), (b1, d1)):
                        if dd is None:
                            continue
                        k.uid += 1
                        os_ = k.uid % 8
                        P.act(lambda e, b=b, os_=os_, fn=fn: e.activation(ost[:, os_, :], ps[:, b, :], fn),
                              [("ps", b)], [("ost", os_)])
                        P.dma("sp", ("ost", os_), lambda e, os_=os_, dd=dd, dsl=dsl: e.dma_start(
                            out=dd[:, dsl], in_=ost[:, os_, :]), [("ost", os_)], [("sc_fm", id(dd), g)])
        for tt in range(T // 128):
            t0 = g * T + tt * 128
            for (c0, ncol, dd) in tok_dsts:
                b = k.bank()
                for kc in range(8):
                    mm(k, ps[:, b, :ncol], xn[:, kc, tt * 128:(tt + 1) * 128], wtok[:, kc, c0:c0 + ncol],
                       kc == 0, kc == 7, [("wtok", kc), ("xn", kc, tt // 4)], [("ps", b)])
                k.uid += 1
                vs = k.uid % 8
                P.act(lambda e, b=b, vs=vs, ncol=ncol: e.copy(vst[:, vs, :ncol], ps[:, b, :ncol]),
                      [("ps", b)], [("vst", vs)])
                P.dma("sp", ("vst", vs), lambda e, vs=vs, dd=dd, t0=t0, ncol=ncol: e.dma_start(
                    out=dd[t0:t0 + 128, :], in_=vst[:, vs, :ncol]), [("vst", vs)], [("sc_tm", id(dd), g)])


def dma_tiles(k, grp, dst, src, nt, key):
    for t0 in range(0, nt, 8):
        t1 = min(nt, t0 + 8)
        sv = src[t0 * 128:t1 * 128, :].rearrange("(t p) c -> p t c", p=128)
        k.P.dma("sp", grp, lambda e, sv=sv, t0=t0, t1=t1: e.dma_start(out=dst[:, t0:t1, :], in_=sv), [], [key])


def emit_pnorm(k, O, T, ps, ones, sqb, rs, ncp, n_feat, eps_col, pk=None):
    P = k.P
    b = k.bank(4, 0)
    pkeys = [("ps", b)] if pk is None else pk(b)
    for i, (Oap, Okey) in enumerate(O):
        P.act(lambda e, Oap=Oap, i=i: e.activation(sqb[:, i, :T], Oap, AF.Square), [Okey], [("pn_sq", i)])
        mm(k, ps[:, b, :T], ones[:, :], sqb[:, i, :T], i == 0, i == len(O) - 1, [("pn_sq", i), ("ones",)], pkeys)
    P.act(lambda e, b=b: e.activation(rs[:, :T], ps[:, b, :T], AF.Ln, bias=eps_col[:, 0:1], scale=1.0 / n_feat),
          pkeys + [("eps",)], [("pn_rs",)])
    P.act(lambda e: e.activation(rs[:, :T], rs[:, :T], AF.Exp, scale=-0.5), [("pn_rs",)], [("pn_rs",)])


PE_L = (1,)


def emit_diffattn(k, qkT, vtok_a, ocatT, vecs_d, NV, V_LQ, V_DANORM, lam_init):
    P = k.P
    S = k.S
    NQ = S // 512
    NK = S // 128
    qT = k.sb("qT", [128, 2, S], BF16)
    kT = k.sb("kT", [128, 2, S], BF16)
    V = k.sb("V", [128, 2, NK, 128], BF16)
    pb = k.sb("pb", [128, 4, 512], BF16)
    accL = k.sb("accL", [128, 2, 512], F32)
    Oc = k.sb("Oc", [128, 2, 512], F32)
    onesf = k.sb("onesf", [128, 128], F32)
    negone = k.sb("negone", [128, 512], F32)
    vecs = k.sb("vecs", [128, NV], F32)
    ones = k.sb("ones", [128, 128], BF16)
    eps_col = k.sb("epsc", [128, 1], F32)
    sc = k.sb("scal", [128, 8], F32)
    junk = k.sb("junk", [128, 2, 64], F32)
    r12 = k.sb("r12", [128, 2, 512], F32)
    t12 = k.sb("t12", [128, 2, 512], F32)
    Of = k.sb("Of", [128, 512], F32)
    sqb = k.sb("sqb", [128, 1, 512], BF16)
    rs = k.sb("rs", [128, 512], F32)
    ob = k.sb("ob", [128, 2, 512], BF16)
    ps = k.psum("ps", [128, 8, 512], F32)
    P.dma("sp", "const", lambda e: e.dma_start(out=vecs[:, :], in_=vecs_d), [], [("vecs",)])
    P.dve(lambda e: e.memset(ones[:, :], 1.0), [], [("ones",)])
    P.dve(lambda e: e.memset(onesf[:, :], 1.0), [], [("onesf",)])
    P.dve(lambda e: e.memset(negone[:, :], -1.0), [], [("negone",)])
    P.dve(lambda e: e.memset(eps_col[:, :], EPS), [], [("eps",)])
    for i in range(2):
        a = V_LQ + i * 128
        P.dve(lambda e, a=a, i=i: e.scalar_tensor_tensor(junk[:, i, :], vecs[:, a:a + 64], 1.0, vecs[:, a + 64:a + 128],
                                                         op0=ALU.mult, op1=ALU.mult, accum_out=sc[:, i:i + 1]),
              [("vecs",)], [("sc", i), ("junk", i)])
        P.act(lambda e, i=i: e.activation(sc[:, i:i + 1], sc[:, i:i + 1], AF.Exp), [("sc", i)], [("sc", i)])
    P.dve(lambda e: e.tensor_tensor(sc[:, 2:3], sc[:, 1:2], sc[:, 0:1], op=ALU.subtract), [("sc", 0), ("sc", 1)], [("sc", 2)])
    P.dve(lambda e: e.tensor_scalar(sc[:, 2:3], sc[:, 2:3], -lam_init, None, op0=ALU.add), [("sc", 2)], [("sc", 2)])
    P.dve(lambda e: e.tensor_scalar(sc[:, 3:4], vecs[:, V_DANORM:V_DANORM + 1], 1.0 - lam_init, None, op0=ALU.mult),
          [("vecs",)], [("sc", 3)])
    def load_head(h_):
        if h_ >= 4:
            return
        sl_ = h_ % 2
        P.dma("sp", ("hq", sl_), lambda e: e.dma_start(out=qT[:, sl_, :], in_=qkT[h_]), [], [("qT", sl_)])
        P.dma("sp", ("hk", sl_), lambda e: e.dma_start(out=kT[:, sl_, :], in_=qkT[4 + h_]), [], [("kT", sl_)])
        dma_tiles(k, ("hv", sl_), V[:, sl_, :, :], vtok_a[:, h_ * 128:(h_ + 1) * 128], NK, ("V", sl_))

    load_head(0)
    for h in range(4):
        sl = h % 2
        load_head(h + 1)
        for qg in range(NQ):
            qs = slice(qg * 512, (qg + 1) * 512)
            bO = (4, 5)
            bL = (6, 7)
            sbanks = {}

            def scores(kc):
                ks = slice(kc * 128, (kc + 1) * 128)
                bb = []
                for c in range(2):
                    b = k.bank(4, 0)
                    bb.append(b)
                    pr = slice(c * 64, (c + 1) * 64)
                    mm(k, ps[:, b, :], kT[pr, sl, ks], qT[pr, sl, qs], True, True,
                       [("kT", sl), ("qT", sl)], [("ps", b)])
                sbanks[kc] = bb

            scores(0)
            for kc in range(NK):
                if kc + 1 < NK:
                    scores(kc + 1)
                for c in range(2):
                    b = sbanks[kc][c]
                    k.uid += 1
                    pp = k.uid % 4
                    P.act(lambda e, b=b, pp=pp: e.activation(pb[:, pp, :], ps[:, b, :], AF.Exp, scale=0.125),
                          [("ps", b)], [("pb", pp)])
                    mm(k, ps[:, bO[c], :], V[:, sl, kc, :], pb[:, pp, :], kc == 0, kc == NK - 1,
                       [("V", sl), ("pb", pp)], [("ps", bO[c])])
                    if c in PE_L:
                        mm(k, ps[:, bL[c], :], ones[:, :], pb[:, pp, :], kc == 0, kc == NK - 1,
                           [("ones",), ("pb", pp)], [("ps", bL[c])])
                    elif kc == 0:
                        P.dve(lambda e, c=c, pp=pp: e.tensor_copy(accL[:, c, :], pb[:, pp, :]),
                              [("pb", pp)], [("accL", c)])
                    else:
                        P.dve(lambda e, c=c, pp=pp: e.tensor_tensor(accL[:, c, :], accL[:, c, :], pb[:, pp, :], op=ALU.add),
                              [("pb", pp), ("accL", c)], [("accL", c)])
            for c in range(2):
                P.dve(lambda e, c=c: e.tensor_copy(Oc[:, c, :], ps[:, bO[c], :]), [("ps", bO[c])], [("Oc", c)])
            for c in range(2):
                if c in PE_L:
                    continue
                mm(k, ps[:, bL[c], :], onesf[:, :], accL[:, c, :], True, True, [("onesf",), ("accL", c)], [("ps", bL[c])])
            for c in range(2):
                P.act(lambda e, c=c: e.activation(r12[:, c, :], ps[:, bL[c], :], AF.Ln), [("ps", bL[c])], [("r12", c)])
                P.act(lambda e, c=c: e.activation(r12[:, c, :], r12[:, c, :], AF.Exp, scale=-1.0),
                      [("r12", c)], [("r12", c)])
                P.dve(lambda e, c=c: e.tensor_tensor(t12[:, c, :], Oc[:, c, :], r12[:, c, :], op=ALU.mult),
                      [("Oc", c), ("r12", c)], [("t12", c)])
            P.dve(lambda e: e.scalar_tensor_tensor(Of[:, :], t12[:, 1, :], sc[:, 2:3], t12[:, 0, :],
                                                   op0=ALU.mult, op1=ALU.add),
                  [("t12", 0), ("t12", 1), ("sc", 2)], [("Of",)])
            emit_pnorm(k, [(Of[:, :], ("Of",))], 512, ps, ones, sqb, rs, 1, 128.0, eps_col)
            k.uid += 1
            oo = k.uid % 2
            P.dve(lambda e, oo=oo: e.scalar_tensor_tensor(ob[:, oo, :], Of[:, :], sc[:, 3:4], rs[:, :],
                                                          op0=ALU.mult, op1=ALU.mult),
                  [("Of",), ("sc", 3), ("pn_rs",)], [("ob", oo)])
            P.dma("sp", ("ob", oo), lambda e, oo=oo, h=h, qs=qs: e.dma_start(out=ocatT[h][:, qs], in_=ob[:, oo, :]),
                  [("ob", oo)], [("ocat", h, qg)])


def emit_retention(k, qkT, vtok_r, sgT, ocatT, vecs_d, NV, cst_d, V_RLF, V_RLB, V_RETNORM):
    P = k.P
    S = k.S
    NK = S // 128
    NQ = S // 512
    q64 = k.sb("q64", [64, S], BF16)
    k64 = k.sb("k64", [64, S], BF16)
    qf = k.sb("qf", [64, S], BF16)
    qb = k.sb("qb", [64, S], BF16)
    V = k.sb("V", [128, NK, 128], BF16)
    sg = k.sb("sg", [128, S], BF16)
    AT = k.sb("AT", [128, NK, 128], BF16)
    kfb = k.sb("kfb", [128, 4, 128], BF16)
    kvs = k.sb("kvs", [64, NK, 256], F32)
    SF = k.sb("SF", [64, NK, 128], F32)
    SB = k.sb("SB", [64, NK, 128], F32)
    SFb = k.sb("SFb", [64, NK, 128], BF16)
    SBb = k.sb("SBb", [64, NK, 128], BF16)
    vecs = k.sb("vecs", [128, NV], F32)
    cst = k.sb("cst", [128, NCST], F32)
    identb = k.sb("identb", [128, 128], BF16)
    ones = k.sb("ones", [128, 128], BF16)
    eps_col = k.sb("epsc", [128, 1], F32)
    one_col = k.sb("onec", [128, 1], F32)
    sc = k.sb("scal", [128, 8], F32)
    E12 = k.sb("E12", [128, 2, 128], F32)
    Dc = k.sb("Dc", [128, 128], F32)
    qd = k.sb("qd", [128, 2, 512], F32)
    Of = k.sb("Of", [128, 512], F32)
    sqb = k.sb("sqb", [128, 1, 512], BF16)
    rs = k.sb("rs", [128, 512], F32)
    tt = k.sb("tt", [128, 512], F32)
    ob = k.sb("ob", [128, 2, 512], BF16)
    ps = k.psum("ps", [128, 7, 512], F32)
    P.dma("sp", "const", lambda e: e.dma_start(out=vecs[:, :], in_=vecs_d), [], [("vecs",)])
    P.dma("sp", "const", lambda e: e.dma_start(out=cst[:, :], in_=cst_d), [], [("cst",)])
    P.dve(lambda e: e.memset(ones[:, :], 1.0), [], [("ones",)])
    P.dve(lambda e: e.memset(eps_col[:, :], EPS), [], [("eps",)])
    P.dve(lambda e: e.memset(one_col[:, :], 1.0), [], [("onec",)])
    P.dve(lambda e: e.tensor_copy(identb[:, :], cst[:, C_IDENT:C_IDENT + 128]), [("cst",)], [("identb",)])
    for h in range(4):
        pr = slice((h % 2) * 64, (h % 2) * 64 + 64)
        P.dma("sp", "hq", lambda e, h=h, pr=pr: e.dma_start(out=q64[:, :], in_=qkT[8 + h // 2][pr, :]), [], [("q64",)])
        P.dma("sp", "hk", lambda e, h=h, pr=pr: e.dma_start(out=k64[:, :], in_=qkT[10 + h // 2][pr, :]), [], [("k64",)])
        dma_tiles(k, "hv", V, vtok_r[:, h * 128:(h + 1) * 128], NK, ("V",))
        P.dma("sp", "hg", lambda e, h=h: e.dma_start(out=sg[:, :], in_=sgT[h]), [], [("sg",)])
        for d, col in ((0, V_RLF + h), (1, V_RLB + h)):
            P.act(lambda e, d=d, col=col: e.activation(sc[:, d:d + 1], vecs[:, col:col + 1], AF.Exp, scale=-1.0),
                  [("vecs",)], [("sc", d)])
            P.act(lambda e, d=d: e.activation(sc[:, d:d + 1], sc[:, d:d + 1], AF.Ln, bias=one_col[:, 0:1]),
                  [("sc", d), ("onec",)], [("sc", d)])
            P.dve(lambda e, d=d: e.tensor_scalar(sc[:, d:d + 1], sc[:, d:d + 1], -1.0, None, op0=ALU.mult),
                  [("sc", d)], [("sc", d)])
            ccol = C_127MP if d == 0 else C_P
            P.act(lambda e, d=d, ccol=ccol: e.activation(sc[:, 2 + d:3 + d], cst[:, ccol:ccol + 1], AF.Exp,
                                                         scale=sc[:, d:d + 1]),
                  [("sc", d), ("cst",)], [("sc", 2 + d)])
            P.act(lambda e, d=d: e.activation(sc[:, 4 + d:5 + d], sc[:, d:d + 1], AF.Exp, scale=128.0),
                  [("sc", d)], [("sc", 4 + d)])
            dcol, mcol = (C_DPOS, C_MGE) if d == 0 else (C_DNEG, C_MLE)
            P.act(lambda e, d=d, dcol=dcol: e.activation(E12[:, d, :], cst[:, dcol:dcol + 128], AF.Exp,
                                                         scale=sc[:, d:d + 1]),
                  [("sc", d), ("cst",)], [("E12", d)])
            P.dve(lambda e, d=d, mcol=mcol: e.tensor_tensor(E12[:, d, :], E12[:, d, :], cst[:, mcol:mcol + 128],
                                                            op=ALU.mult), [("E12", d), ("cst",)], [("E12", d)])
            icol = C_IOTA1 if d == 0 else C_IOTAR
            P.act(lambda e, d=d, icol=icol: e.activation(qd[:, d, 0:128], cst[:, icol:icol + 128], AF.Exp,
                                                         scale=sc[:, d:d + 1]),
                  [("sc", d), ("cst",)], [("qd", d)])
            P.dve(lambda e, d=d: e.tensor_scalar(qd[:, d, 0:128], qd[:, d, 0:128], 0.125, None, op0=ALU.mult),
                  [("qd", d)], [("qd", d)])
            for j in range(1, 4):
                P.dve(lambda e, d=d, j=j: e.tensor_copy(qd[:, d, j * 128:(j + 1) * 128], qd[:, d, 0:128]),
                      [("qd", d)], [("qd", d)])
        P.dve(lambda e: e.tensor_tensor(Dc[:, :], E12[:, 0, :], E12[:, 1, :], op=ALU.add),
              [("E12", 0), ("E12", 1)], [("Dc",)])
        P.dve(lambda e: e.tensor_scalar(Dc[:, :], Dc[:, :], 0.125, None, op0=ALU.mult), [("Dc",)], [("Dc",)])
        for g in range(NQ):
            gs = slice(g * 512, (g + 1) * 512)
            P.dve(lambda e, gs=gs: e.tensor_tensor(qf[:, gs], q64[:, gs], qd[0:64, 0, :], op=ALU.mult),
                  [("q64",), ("qd", 0)], [("qf", g)])
            P.pool(lambda e, gs=gs: e.tensor_tensor(qb[:, gs], q64[:, gs], qd[0:64, 1, :], op=ALU.mult),
                   [("q64",), ("qd", 1)], [("qb", g)])
        P.dve(lambda e: e.memset(SF[:, 0, :], 0.0), [], [("SF", 0)])
        P.dve(lambda e: e.memset(SB[:, NK - 1, :], 0.0), [], [("SB", NK - 1)])
        import os
        rstop = int(os.environ.get("RET_STOP", "9"))
        if rstop < 1:
            continue
        def retA(c):
            cs = slice(c * 128, (c + 1) * 128)
            ks = c % 4
            b = k.bank(7, 0)
            mm(k, ps[:, b, 0:128], k64[:, cs], q64[:, cs], True, True, [("k64",), ("q64",)], [("ps", b)])
            P.dve(lambda e, b=b, c=c: e.tensor_tensor(AT[:, c, :], ps[:, b, 0:128], Dc[:, :], op=ALU.mult),
                  [("ps", b), ("Dc",)], [("AT", c)])
            bt = k.bank(7, 0)
            mm(k, ps[:, bt, 0:64], k64[:, cs], identb[0:64, 0:64], True, True, [("k64",), ("identb",)], [("ps", bt)])
            P.dve(lambda e, bt=bt, ks=ks: e.tensor_scalar(kfb[:, ks, 0:64], ps[:, bt, 0:64], sc[:, 2:3], None,
                                                          op0=ALU.mult),
                  [("ps", bt), ("sc", 2)], [("kfb", ks, 0)])
            P.dve(lambda e, bt=bt, ks=ks: e.tensor_scalar(kfb[:, ks, 64:128], ps[:, bt, 0:64], sc[:, 3:4], None,
                                                          op0=ALU.mult),
                  [("ps", bt), ("sc", 3)], [("kfb", ks, 1)])

        def retB(c):
            ks = c % 4
            b2 = k.bank(7, 0)
            mm(k, ps[0:64, b2, 0:128], kfb[:, ks, 0:64], V[:, c, :], True, True, [("kfb", ks, 0), ("V",)], [("ps", b2)])
            mm(k, ps[0:64, b2, 128:256], kfb[:, ks, 64:128], V[:, c, :], True, True, [("kfb", ks, 1), ("V",)], [("ps", b2)])
            P.act(lambda e, b2=b2, c=c: e.copy(kvs[:, c, :], ps[0:64, b2, 0:256]), [("ps", b2)], [("kvs", c)])

        for step in range(NK + 2):
            if step < NK:
                retA(step)
            if step >= 2:
                retB(step - 2)
        if rstop < 2:
            continue
        for c in range(NK - 1):
            P.dve(lambda e, c=c: e.scalar_tensor_tensor(SF[:, c + 1, :], SF[:, c, :], sc[0:64, 4:5], kvs[:, c, 0:128],
                                                        op0=ALU.mult, op1=ALU.add),
                  [("SF", c), ("kvs", c), ("sc", 4)], [("SF", c + 1)])
        for c in range(NK - 1, 0, -1):
            P.dve(lambda e, c=c: e.scalar_tensor_tensor(SB[:, c - 1, :], SB[:, c, :], sc[0:64, 5:6], kvs[:, c, 128:256],
                                                        op0=ALU.mult, op1=ALU.add),
                  [("SB", c), ("kvs", c), ("sc", 5)], [("SB", c - 1)])
        P.act(lambda e: e.copy(SFb[:, :, :], SF[:, :, :]), [("SF", c) for c in range(NK)], [("SFb",)])
        P.pool(lambda e: e.tensor_copy(SBb[:, :, :], SB[:, :, :]), [("SB", c) for c in range(NK)], [("SBb",)])
        if rstop < 3:
            continue
        for g in range(NQ):
            gs = slice(g * 512, (g + 1) * 512)
            b = k.bank(7, 0)
            for j in range(4):
                c = g * 4 + j
                cs = slice(c * 128, (c + 1) * 128)
                js = slice(j * 128, (j + 1) * 128)
                mm(k, ps[:, b, js], V[:, c, :], AT[:, c, :], True, False, [("V",), ("AT", c)], [("ps", b)])
                mm(k, ps[:, b, js], SFb[:, c, :], qf[:, cs], False, False, [("SFb",), ("qf", g)], [("ps", b)])
                mm(k, ps[:, b, js], SBb[:, c, :], qb[:, cs], False, True, [("SBb",), ("qb", g)], [("ps", b)])
            P.dve(lambda e, b=b: e.tensor_copy(Of[:, :], ps[:, b, :]), [("ps", b)], [("Of",)])
            emit_pnorm(k, [(Of[:, :], ("Of",))], 512, ps, ones, sqb, rs, 1, 128.0, eps_col)
            P.dve(lambda e: e.scalar_tensor_tensor(tt[:, :], Of[:, :], vecs[:, V_RETNORM:V_RETNORM + 1], rs[:, :],
                                                   op0=ALU.mult, op1=ALU.mult),
                  [("Of",), ("vecs",), ("pn_rs",)], [("tt",)])
            k.uid += 1
            oo = k.uid % 2
            P.dve(lambda e, oo=oo, gs=gs: e.tensor_tensor(ob[:, oo, :], tt[:, :], sg[:, gs], op=ALU.mult),
                  [("tt",), ("sg",)], [("ob", oo)])
            P.dma("sp", ("ob", oo), lambda e, oo=oo, h=h, gs=gs: e.dma_start(out=ocatT[4 + h][:, gs], in_=ob[:, oo, :]),
                  [("ob", oo)], [("ocat", h, g)])


V_FFN1 = (0, 16)
V_FFN2 = (8, 24)
V_AB, V_C, V_FINAL, V_DANORM, V_RETNORM, V_GLANORM = 32, 40, 48, 56, 57, 58
V_RLF, V_RLB, V_LQ = 60, 64, 68
NV = 68 + 256


def lay_vec(v):
    return np.ascontiguousarray(np.asarray(v, np.float32).reshape(-1, 128).T)


def lay_wgu(Wg, Wu):
    a = np.stack([Wg, Wu], 0).reshape(2, 8, 128, -1, 128)
    nf = a.shape[3]
    return np.ascontiguousarray(a.transpose(3, 2, 0, 1, 4)).reshape(nf, 128, 2048)


def lay_wd(Wd):
    a = Wd.reshape(NF, 128, 8, 128)
    return np.ascontiguousarray(a.transpose(2, 1, 0, 3)).reshape(8, 128, NF * 128)


def lay_kmajor(W):
    return np.ascontiguousarray(W.reshape(8, 128, -1).transpose(1, 0, 2))


def build_cst():
    c = np.zeros((128, NCST), np.float32)
    p = np.arange(128)
    d = p % 64
    inva = np.where(d < 16, 500000.0 ** (-(2.0 * (d % 8)) / 16.0), 0.0)
    ia = (1.0 / (np.float32(500000.0) ** (np.arange(0, 16, 2, dtype=np.float32) / np.float32(16)))).astype(np.float32)
    ir = (1.0 / (np.float32(10000.0) ** (np.arange(0, 64, 2, dtype=np.float32) / np.float32(64)))).astype(np.float32)
    c[:, C_INVA] = np.where(d < 16, ia[d % 8], 0.0)
    c[:, C_SGNA] = np.where(d < 8, -1.0, 1.0)
    c[:, C_INVR] = ir[d % 32]
    c[:, C_SGNR] = np.where(d < 32, -1.0, 1.0)
    c[:, C_127MP] = 127 - p
    c[:, C_P] = p
    diff = (p[None, :] - p[:, None]).astype(np.float32)
    c[:, C_DPOS:C_DPOS + 128] = np.maximum(diff, 0)
    c[:, C_DNEG:C_DNEG + 128] = np.maximum(-diff, 0)
    c[:, C_MGE:C_MGE + 128] = (diff >= 0)
    c[:, C_MLE:C_MLE + 128] = (diff <= 0)
    c[:, C_IOTA1:C_IOTA1 + 128] = (p + 1)[None, :]
    c[:, C_IOTAR:C_IOTAR + 128] = (128 - p)[None, :]
    c[:, C_IDENT:C_IDENT + 128] = np.eye(128)
    same = (p[:, None] // 64) == (p[None, :] // 64)
    sd = p[:, None]
    td = p[None, :]
    c[:, C_TRI_INC:C_TRI_INC + 128] = same & (sd <= td)
    c[:, C_TRI_SUF:C_TRI_SUF + 128] = same & (sd >= td)
    c[:, C_MASKLO:C_MASKLO + 128] = same & (sd <= td)
    c[:, C_MASKUP:C_MASKUP + 128] = same & (sd >= td)
    c[:, C_TRI_SSUF:C_TRI_SSUF + 128] = same & (sd > td)
    c[:, C_TRI_SPRE:C_TRI_SPRE + 128] = same & (sd < td)
    return c


def swap_perm_da():
    idx = np.arange(512)
    d = idx % 64
    return np.where(d < 8, idx + 8, np.where(d < 16, idx - 8, idx))


def swap_perm_ret():
    idx = np.arange(256)
    d = idx % 64
    return np.where(d < 32, idx + 32, idx - 32)


def lay_win_ab(W):
    qa, ka, va, qr, kr, vr, gr = (W[:, 0:512], W[:, 512:1024], W[:, 1024:1536], W[:, 1536:1792],
                                  W[:, 1792:2048], W[:, 2048:2560], W[:, 2560:3072])
    pa, prr = swap_perm_da(), swap_perm_ret()
    pairs = []
    for X in (qa, ka):
        Xs = X[:, pa]
        for j in range(4):
            pairs.append((X[:, j * 128:(j + 1) * 128], Xs[:, j * 128:(j + 1) * 128]))
    for X in (qr, kr):
        Xs = X[:, prr]
        for j in range(2):
            pairs.append((X[:, j * 128:(j + 1) * 128], Xs[:, j * 128:(j + 1) * 128]))
    pairs.append((gr[:, 0:128], gr[:, 128:256]))
    pairs.append((gr[:, 256:384], gr[:, 384:512]))
    win = np.concatenate([lay_wgu(a, b) for a, b in pairs], 0)
    wtok = lay_kmajor(np.concatenate([va, vr], 1))
    return np.ascontiguousarray(win), wtok


def build_program(S, stages):
    k = K(S)
    nc = k.nc
    inp = lambda n, shp, dt=F32: nc.dram_tensor(n, list(shp), dt, kind="ExternalInput").ap()
    xT = inp("xT", [D, S])
    pos = inp("pos", [S], I32)
    vecs_d = inp("vecs", [128, NV])
    cst_d = inp("cst", [128, NCST])
    wgu = {}
    wd = {}
    for l in range(2):
        for f in range(2):
            if ("ffn%d_%d" % (f + 1, l)) in stages:
                wgu[(l, f)] = inp("wgu_%d_%d" % (l, f), [NF, 128, 2048])
                wd[(l, f)] = inp("wd_%d_%d" % (l, f), [8, 128, NF * 128])
    out = nc.dram_tensor("outT", [D, S], F32, kind="ExternalOutput").ap()
    xs = out
    k.arena = k.dram("arena", [40 * 128 * S], BF16)
    k.arena_off = 0
    ocatT = k.carve([8, 128, S])
    k.arena_base = k.arena_off
    if "ab" in stages:
        win_ab = inp("win_ab", [14, 128, 2048])
        wtok_ab = inp("wtok_ab", [128, 8, 1024])
        wout_ab = inp("wout_ab", [128, 8, 1024])
        k.arena_off = k.arena_base
        qkT = k.carve([12, 128, S])
        sgT = k.carve([4, 128, S])
        vtok = k.carve([S, 1024])
    if "c" in stages:
        win_c = inp("win_c", [NPAIR_C, 128, 2048])
        wtok_c = inp("wtok_c", [128, 8, 1536])
        wout_c = inp("wout_c", [128, 8, 1024])
        w2aug = inp("w2aug", [17, 1024])
    last = stages[-1]
    cur = xT
    for st in stages:
        dst = out if st == last else xs
        if st.startswith("ffn"):
            f = int(st[3]) - 1
            l = int(st[5])
            with k.phase():
                alloc_common(k, vecs_d, NV)
                pre = None
                if f == 1 and l == 0 and "ab" in stages:
                    pre = (ocatT, load_wout(k, wout_ab))
                if f == 1 and l == 1 and "c" in stages:
                    pre = (ocatT, load_wout(k, wout_c))
                gcol = (V_FFN1, V_FFN2)[f][l]
                if st == "ffn2_1":
                    emit_ffn(k, l, f, cur, None, wgu[(l, f)], wd[(l, f)], gcol, final_gain_col=V_FINAL,
                             out_dst=out, pre=pre)
                else:
                    emit_ffn(k, l, f, cur, dst, wgu[(l, f)], wd[(l, f)], gcol, pre=pre)
            cur = dst
        elif st == "ab":
            with k.phase():
                pairs = [("ropeA", qkT[j], None) for j in range(8)] + [("ropeR", qkT[8 + j], None) for j in range(4)]
                pairs += [("silu", sgT[0], sgT[1]), ("silu", sgT[2], sgT[3])]
                emit_inproj(k, cur, V_AB, win_ab, pairs, wtok_ab, 1024,
                            [(0, 512, vtok[:, 0:512]), (512, 512, vtok[:, 512:1024])], pos, cst_d, vecs_d, NV)
            import os
            sub = os.environ.get("AB_SUB", "da,ret")
            if "da" in sub:
              with k.phase():
                emit_diffattn(k, qkT, vtok[:, 0:512], ocatT, vecs_d, NV, V_LQ, V_DANORM, 0.2)
            if "ret" in sub:
              with k.phase():
                emit_retention(k, qkT, vtok[:, 512:1024], sgT, ocatT, vecs_d, NV, cst_d, V_RLF, V_RLB, V_RETNORM)
            if st == last:
                raise ValueError("mixer cannot be last stage")
        elif st == "c":
            emit_gla_all(k, cur, ocatT, win_c, wtok_c, w2aug, pos, cst_d, vecs_d)
    k.gstack.close()
    return k


NPAIR_C = 9


def host_inputs(inputs, b, stages, S):
    g = lambda n: np.asarray(inputs[n], np.float32)
    m = {}
    m["xT"] = np.ascontiguousarray(g("x")[b, :S].T)
    m["pos"] = np.ascontiguousarray(np.asarray(inputs["positions"], np.int32)[:S])
    return m


def host_shared(inputs, stages):
    g = lambda n: np.asarray(inputs[n], np.float32)
    m = {}
    vecs = np.zeros((128, NV), np.float32)
    for l in range(2):
        vecs[:, V_FFN1[l]:V_FFN1[l] + 8] = lay_vec(g("ffn1_norm")[l])
        vecs[:, V_FFN2[l]:V_FFN2[l] + 8] = lay_vec(g("ffn2_norm")[l])
    vecs[:, V_AB:V_AB + 8] = lay_vec(g("ab_norm")[0])
    vecs[:, V_C:V_C + 8] = lay_vec(g("c_norm")[0])
    vecs[:, V_FINAL:V_FINAL + 8] = lay_vec(g("final_norm"))
    vecs[:, V_DANORM] = g("da_norm")[0]
    vecs[:, V_RETNORM] = g("ret_norm")[0]
    vecs[:, V_GLANORM:V_GLANORM + 2] = lay_vec(g("gla_norm")[0])
    vecs[:, V_RLF:V_RLF + 4] = g("ret_logit_f")[0][None, :]
    vecs[:, V_RLB:V_RLB + 4] = g("ret_logit_b")[0][None, :]
    for i, n in enumerate(("da_lq1", "da_lk1", "da_lq2", "da_lk2")):
        vecs[:, V_LQ + 64 * i:V_LQ + 64 * (i + 1)] = g(n)[0][None, :]
    m["vecs"] = vecs
    m["cst"] = build_cst()
    for l in range(2):
        for f in range(2):
            if ("ffn%d_%d" % (f + 1, l)) in stages:
                names = (("ffn1_w_gate", "ffn1_w_up", "ffn1_w_down"), ("ffn2_w_gate", "ffn2_w_up", "ffn2_w_down"))[f]
                m["wgu_%d_%d" % (l, f)] = lay_wgu(g(names[0])[l], g(names[1])[l])
                m["wd_%d_%d" % (l, f)] = lay_wd(g(names[2])[l])
    if "ab" in stages:
        m["win_ab"], m["wtok_ab"] = lay_win_ab(g("ab_w_in")[0])
        m["wout_ab"] = lay_kmajor(g("ab_w_out")[0])
    if "c" in stages:
        m.update(lay_gla(inputs))
    return m


ALL_STAGES = ["ffn1_0", "ab", "ffn2_0", "ffn1_1", "c", "ffn2_1"]


def run(inputs, stages, S, ncores, trace=False):
    k = build_program(S, stages)
    shared = host_shared(inputs, stages)
    in_maps = []
    for b in range(ncores):
        m = dict(shared)
        m.update(host_inputs(inputs, b, stages, S))
        in_maps.append(m)
    res = run_bass_kernel_spmd(k.nc, in_maps, core_ids=list(range(ncores)), trace=trace)
    outs = [np.ascontiguousarray(r["outT"].T) for r in res.results]
    return np.stack(outs, 0), res


def lay_gla(inputs):
    g = lambda n: np.asarray(inputs[n], np.float32)
    W = g("c_w_in")[0]
    q, kk, v, gg = W[:, 0:512], W[:, 512:1024], W[:, 1024:2048], W[:, 2048:3072]
    lr = np.zeros((1024, 128), np.float32)
    lr[:, 0:32] = W[:, 3072:3104]
    zero = np.zeros((1024, 128), np.float32)
    pairs = [(q[:, 0:128], q[:, 128:256]), (q[:, 256:384], q[:, 384:512]),
             (kk[:, 0:128], kk[:, 128:256]), (kk[:, 256:384], kk[:, 384:512])]
    for j in range(4):
        pairs.append((gg[:, j * 256:j * 256 + 128], gg[:, j * 256 + 128:j * 256 + 256]))
    pairs.append((lr, zero))
    win = np.concatenate([lay_wgu(a, b) for a, b in pairs], 0)
    wtok = lay_kmajor(np.concatenate([v, kk], 1))
    w2aug = np.zeros((17, 1024), np.float32)
    w2aug[0:16, 0:512] = g("gla_w2_f")[0]
    w2aug[16, 0:512] = g("gla_b_f")[0]
    w2aug[0:16, 512:1024] = g("gla_w2_b")[0]
    w2aug[16, 512:1024] = g("gla_b_b")[0]
    return {"win_c": np.ascontiguousarray(win), "wtok_c": wtok, "w2aug": w2aug,
            "wout_c": lay_kmajor(g("c_w_out")[0])}


def emit_gla_all(k, x_src, ocatT, win_c, wtok_c, w2aug_d, pos, cst_d, vecs_d):
    S = k.S
    k.arena_off = k.arena_base
    gqT = k.carve([4, 128, S])
    gkT = k.carve([4, 128, S])
    gsgT = k.carve([8, 128, S])
    lrT = k.carve([128, S])
    vtokc = k.carve([S, 1024])
    ktokc = k.carve([S, 512])
    with k.phase():
        pairs = [("copy", gqT[0], gqT[1]), ("copy", gqT[2], gqT[3]), ("copy", gkT[0], gkT[1]), ("copy", gkT[2], gkT[3])]
        pairs += [("silu", gsgT[2 * j], gsgT[2 * j + 1]) for j in range(4)]
        pairs += [("copy", lrT, None)]
        emit_inproj(k, x_src, V_C, win_c, pairs, wtok_c, 1536,
                    [(0, 512, vtokc[:, 0:512]), (512, 512, vtokc[:, 512:1024]), (1024, 512, ktokc)],
                    pos, cst_d, vecs_d, NV)
    import os
    if os.environ.get("C_SUB", "gla") == "gla":
      with k.phase():
        emit_gla(k, gqT, gkT, gsgT, lrT, vtokc, ktokc, ocatT, w2aug_d, cst_d, vecs_d)


def emit_gla(k, gqT, gkT, gsgT, lrT, vtokc, ktokc, ocatT, w2aug_d, cst_d, vecs_d):
    P = k.P
    S = k.S
    NT = S // 128
    NC = S // 64
    NQ = S // 512
    QS = 128 ** -0.5
    qT = k.sb("qT", [128, S], BF16)
    kT = k.sb("kT", [128, S], BF16)
    V = k.sb("V", [128, NT, 256], BF16)
    ktok = k.sb("ktok", [128, NT, 128], BF16)
    sg = k.sb("sg", [128, 2, S], BF16)
    lrA = k.sb("lrA", [17, 2, S], BF16)
    w2s = k.sb("w2s", [17, 1024], F32)
    w2b = k.sb("w2b", [17, 1024], BF16)
    qg = k.sb("qg", [128, S], BF16)
    AT = k.sb("AT", [128, NT, 128], BF16)
    Sb = k.sb("Sb", [128, NC, 256], BF16)
    Sf = k.sb("Sf", [128, 2, 256], F32)
    OF = k.sb("OF", [128, 2, S], F32)
    vecs = k.sb("vecs", [128, NV], F32)
    cst = k.sb("cst", [128, NCST], F32)
    cb = k.sb("cb", [128, 6, 128], BF16)
    ones = k.sb("ones", [128, 128], BF16)
    eps_col = k.sb("epsc", [128, 1], F32)
    one_col = k.sb("onec", [128, 1], F32)
    ez = k.sb("ez", [128, 8, 128], F32)
    la = k.sb("la", [128, 8, 128], BF16)
    EcEw = k.sb("EcEw", [128, 8, 256], F32)
    Ec = EcEw[:, :, 0:128]
    En = k.sb("En", [128, 8, 128], F32)
    Ew = EcEw[:, :, 128:256]
    kneg = k.sb("kneg", [128, 8, 128], BF16)
    klast = k.sb("klast", [128, 8, 128], BF16)
    Oe = k.sb("Oe", [128, 2, 512], F32)
    sqb = k.sb("sqb", [128, 2, 512], BF16)
    rs = k.sb("rs", [128, 512], F32)
    tt_ = k.sb("tt", [128, 512], F32)
    ob = k.sb("ob", [128, 2, 512], BF16)
    ps = k.psum("ps", [128, 8, 512], F32)
    P.dma("sp", "const", lambda e: e.dma_start(out=vecs[:, :], in_=vecs_d), [], [("vecs",)])
    P.dma("sp", "const", lambda e: e.dma_start(out=cst[:, :], in_=cst_d), [], [("cst",)])
    P.dma("sp", "const", lambda e: e.dma_start(out=w2s[:, :], in_=w2aug_d), [], [("w2s",)])
    P.dve(lambda e: e.tensor_copy(w2b[:, :], w2s[:, :]), [("w2s",)], [("w2b",)])
    P.dve(lambda e: e.memset(ones[:, :], 1.0), [], [("ones",)])
    P.dve(lambda e: e.memset(eps_col[:, :], EPS), [], [("eps",)])
    P.dve(lambda e: e.memset(one_col[:, :], 1.0), [], [("onec",)])
    for i, col in enumerate((C_TRI_INC, C_TRI_SUF, C_TRI_SSUF, C_TRI_SPRE)):
        P.dve(lambda e, i=i, col=col: e.tensor_copy(cb[:, i, :], cst[:, col:col + 128]), [("cst",)], [("cb",)])
    P.dve(lambda e: e.memset(lrA[:, :, :], 1.0), [], [("lrA",)])
    for d in range(2):
        P.dma("sp", "const", lambda e, d=d: e.dma_start(out=lrA[0:16, d, :], in_=lrT[d * 16:(d + 1) * 16, :]),
              [], [("lrA",)])
    qrr = [0]

    def qalloc(n):
        i = ((qrr[0] + n - 1) // n) * n % 32
        qrr[0] = i + n
        return i // 4, (i % 4) * 128, [("psq", i + j) for j in range(n)]

    def pk(b):
        return [("psq", b * 4 + q) for q in range(4)]

    for h in range(4):
        P.dma("sp", "hq", lambda e, h=h: e.dma_start(out=qT[:, :], in_=gqT[h]), [], [("qT",)])
        P.dma("sp", "hk", lambda e, h=h: e.dma_start(out=kT[:, :], in_=gkT[h]), [], [("kT",)])
        dma_tiles(k, "hv", V, vtokc[:, h * 256:(h + 1) * 256], NT, ("V",))
        dma_tiles(k, "hkt", ktok, ktokc[:, h * 128:(h + 1) * 128], NT, ("ktok",))
        for e2 in range(2):
            P.dma("sp", "hg", lambda e, h=h, e2=e2: e.dma_start(out=sg[:, e2, :], in_=gsgT[2 * h + e2]), [], [("sg", e2)])
        for d in range(2):
            tri_c, tri_w = (0, 2) if d == 0 else (1, 3)
            mcol = C_MASKLO if d == 0 else C_MASKUP
            tiles = list(range(NT)) if d == 0 else list(range(NT - 1, -1, -1))
            c_init = 0 if d == 0 else NC - 1
            P.dve(lambda e, c_init=c_init: e.memset(Sf[:, 0, :], 0.0), [], [("Sf", 0)])
            P.pool(lambda e, c_init=c_init: e.memset(Sb[:, c_init, :], 0.0), [], [("Sb", c_init)])
            sslot = 0
            def stageA(i, tt):
                u = i % 8
                ts = slice(tt * 128, (tt + 1) * 128)
                bz, cz, kz = qalloc(4)
                mm(k, ps[:, bz, cz:cz + 128], lrA[:, d, ts], w2b[:, d * 512 + h * 128:d * 512 + (h + 1) * 128], True, True,
                   [("lrA",), ("w2b",)], kz)
                P.act(lambda e, bz=bz, cz=cz, u=u: e.activation(ez[:, u, :], ps[:, bz, cz:cz + 128], AF.Exp, scale=-1.0),
                      kz, [("ez", u)])
                P.act(lambda e, u=u: e.activation(ez[:, u, :], ez[:, u, :], AF.Ln, bias=one_col[:, 0:1]),
                      [("ez", u), ("onec",)], [("ez", u)])
                P.dve(lambda e, u=u: e.tensor_scalar(la[:, u, :], ez[:, u, :], -1.0 / 16.0, None, op0=ALU.mult),
                      [("ez", u)], [("la", u)])

            def stageB(i, tt):
                u = i % 8
                ts = slice(tt * 128, (tt + 1) * 128)
                bc, cc_, kc_ = qalloc(4)
                mm(k, ps[:, bc, cc_:cc_ + 128], la[:, u, :], cb[:, tri_c, :], True, True, [("la", u), ("cb",)], kc_)
                mm(k, ps[:, bc, cc_ + 128:cc_ + 256], cb[:, tri_w, :], la[:, u, :], True, True, [("la", u), ("cb",)], kc_)
                P.act(lambda e, bc=bc, cc_=cc_, u=u: e.activation(EcEw[:, u, :], ps[:, bc, cc_:cc_ + 256], AF.Exp),
                      kc_, [("Ec", u), ("Ew", u)])
                P.act(lambda e, bc=bc, cc_=cc_, u=u: e.activation(En[:, u, :], ps[:, bc, cc_:cc_ + 128], AF.Exp, scale=-1.0),
                      kc_, [("En", u)])
                P.dve(lambda e, u=u, ts=ts: e.scalar_tensor_tensor(qg[:, ts], Ec[:, u, :], QS, qT[:, ts],
                                                                   op0=ALU.mult, op1=ALU.mult),
                      [("Ec", u), ("qT",)], [("qg", tt)])
                P.pool(lambda e, u=u, ts=ts: e.tensor_tensor(kneg[:, u, :], En[:, u, :], kT[:, ts], op=ALU.mult),
                       [("En", u), ("kT",)], [("kneg", u)])
                P.pool(lambda e, u=u, tt=tt: e.tensor_tensor(klast[:, u, :], Ew[:, u, :], ktok[:, tt, :], op=ALU.mult),
                       [("Ew", u), ("ktok",)], [("klast", u)])

            def stageC(i, tt):
                nonlocal sslot
                u = i % 8
                ts = slice(tt * 128, (tt + 1) * 128)
                bs, cs_, ks_ = qalloc(4)
                mm(k, ps[:, bs, cs_:cs_ + 128], kneg[:, u, :], qg[:, ts], True, True, [("kneg", u), ("qg", tt)], ks_)
                P.dve(lambda e, bs=bs, cs_=cs_, tt=tt, mcol=mcol: e.tensor_tensor(AT[:, tt, :], ps[:, bs, cs_:cs_ + 128],
                                                                                  cst[:, mcol:mcol + 128], op=ALU.mult),
                      ks_ + [("cst",)], [("AT", tt)])
                chunks = (0, 1) if d == 0 else (1, 0)
                for cc in chunks:
                    c = tt * 2 + cc
                    cn = c + 1 if d == 0 else c - 1
                    if cn < 0 or cn >= NC:
                        continue
                    pr = slice(cc * 64, cc * 64 + 64)
                    bk, ck_, kk_ = qalloc(4)
                    mm(k, ps[:, bk, ck_:ck_ + 256], klast[pr, u, :], V[pr, tt, :], True, True, [("klast", u), ("V",)], kk_)
                    dcol = (cc * 64 + 63) if d == 0 else (cc * 64)
                    so, sn = sslot % 2, (sslot + 1) % 2
                    sslot += 1
                    P.dve(lambda e, bk=bk, ck_=ck_, u=u, dcol=dcol, so=so, sn=sn: e.scalar_tensor_tensor(
                        Sf[:, sn, :], Sf[:, so, :], Ec[:, u, dcol:dcol + 1], ps[:, bk, ck_:ck_ + 256],
                        op0=ALU.mult, op1=ALU.add),
                        [("Sf", so), ("Ec", u)] + kk_, [("Sf", sn)])
                    if cc == 0:
                        P.act(lambda e, sn=sn, cn=cn: e.copy(Sb[:, cn, :], Sf[:, sn, :]), [("Sf", sn)], [("Sb", cn)])
                    else:
                        P.dve(lambda e, sn=sn, cn=cn: e.tensor_copy(Sb[:, cn, :], Sf[:, sn, :]), [("Sf", sn)], [("Sb", cn)])

            nt_ = len(tiles)
            for step in range(nt_ + 4):
                if step < nt_:
                    stageA(step, tiles[step])
                if 2 <= step < nt_ + 2:
                    stageB(step - 2, tiles[step - 2])
                if step >= 4:
                    stageC(step - 4, tiles[step - 4])
            for g in range(NQ):
                gs = slice(g * 512, (g + 1) * 512)
                be = (k.bank(), k.bank())
                for j in range(4):
                    tt = g * 4 + j
                    for e2 in range(2):
                        js = slice(j * 128, (j + 1) * 128)
                        mm(k, ps[:, be[e2], js], V[:, tt, e2 * 128:(e2 + 1) * 128], AT[:, tt, :], True, False,
                           [("V",), ("AT", tt)], pk(be[e2]))
                        for cc in range(2):
                            c = tt * 2 + cc
                            cs = slice(c * 64, (c + 1) * 64)
                            jc = slice(j * 128 + cc * 64, j * 128 + cc * 64 + 64)
                            mm(k, ps[:, be[e2], jc], Sb[:, c, e2 * 128:(e2 + 1) * 128], qg[:, cs], False, cc == 1,
                               [("Sb", c), ("qg", tt)], pk(be[e2]))
                if d == 0:
                    for e2 in range(2):
                        P.act(lambda e, e2=e2, gs=gs, b=be[e2]: e.copy(OF[:, e2, gs], ps[:, b, :]),
                              pk(be[e2]), [("OF", e2, g)])
                else:
                    for e2 in range(2):
                        P.dve(lambda e, e2=e2, gs=gs, b=be[e2]: e.tensor_tensor(Oe[:, e2, :], OF[:, e2, gs], ps[:, b, :],
                                                                                op=ALU.add),
                              pk(be[e2]) + [("OF", e2, g)], [("Oe", e2)])
                    emit_pnorm(k, [(Oe[:, 0, :], ("Oe", 0)), (Oe[:, 1, :], ("Oe", 1))], 512, ps, ones, sqb, rs, 2,
                               256.0, eps_col, pk=pk)
                    for e2 in range(2):
                        P.dve(lambda e, e2=e2: e.scalar_tensor_tensor(
                            tt_[:, :], Oe[:, e2, :], vecs[:, V_GLANORM + e2:V_GLANORM + e2 + 1], rs[:, :],
                            op0=ALU.mult, op1=ALU.mult), [("Oe", e2), ("vecs",), ("pn_rs",)], [("tt",)])
                        k.uid += 1
                        oo = k.uid % 2
                        P.dve(lambda e, oo=oo, e2=e2, gs=gs: e.tensor_tensor(ob[:, oo, :], tt_[:, :], sg[:, e2, gs],
                                                                            op=ALU.mult),
                              [("tt",), ("sg", e2)], [("ob", oo)])
                        P.dma("sp", ("ob", oo), lambda e, oo=oo, h=h, e2=e2, gs=gs: e.dma_start(
                            out=ocatT[2 * h + e2][:, gs], in_=ob[:, oo, :]), [("ob", oo)], [("ocat", h, e2, g)])


def run_core_inputs(inputs, stages, S, ncores, x_override=None, trace=False):
    k = build_program(S, stages)
    shared = host_shared(inputs, stages)
    in_maps = []
    for b in range(ncores):
        m = dict(shared)
        m.update(host_inputs(inputs, b, stages, S))
        if x_override is not None:
            m["xT"] = x_override[b]
        in_maps.append(m)
    res = run_bass_kernel_spmd(k.nc, in_maps, core_ids=list(range(ncores)), trace=trace)
    return [r["outT"] for r in res.results], res


LAUNCHES = [ALL_STAGES]


def kernel(**inputs):
    xo = None
    for stages in LAUNCHES:
        xo, _ = run_core_inputs(inputs, stages, 4096, 8, x_override=xo)
    return np.stack([np.ascontiguousarray(o.T) for o in xo], 0).astype(np.float32)
```

```python
import math
from contextlib import ExitStack

import numpy as np
import ml_dtypes

import concourse.bass as bass
import concourse.mybir as mybir
from concourse.bass_utils import run_bass_kernel_spmd

F32 = mybir.dt.float32
BF16 = mybir.dt.bfloat16
I32 = mybir.dt.int32
AF = mybir.ActivationFunctionType
ALU = mybir.AluOpType

D = 1024
DFF = 2816
NF = DFF // 128
EPS = 1e-6


class Op:
    __slots__ = ("eng", "fn", "reads", "writes", "dma", "deps", "signal", "sigidx", "grp", "cum", "idx")

    def __init__(self, eng, fn, reads, writes, dma, grp):
        self.eng = eng
        self.fn = fn
        self.reads = reads
        self.writes = writes
        self.dma = dma
        self.grp = grp
        self.deps = []
        self.signal = False
        self.sigidx = 0
        self.cum = 0


class Prog:
    ENGS = ("pe", "act", "dve", "pool", "sp")

    def __init__(self, nc, tag=0):
        self.nc = nc
        self.tag = tag
        self.ops = []
        self.last_writer = {}
        self.readers = {}
        self.grp_cum = {}

    def _add(self, eng, fn, reads, writes, dma=False, grp=None):
        op = Op(eng, fn, tuple(reads), tuple(writes), dma, grp)
        op.idx = len(self.ops)
        deps = {}
        for r in op.reads:
            w = self.last_writer.get(r)
            if w is not None:
                deps[id(w)] = (w, "raw")
        for wkey in op.writes:
            w = self.last_writer.get(wkey)
            if w is not None and id(w) not in deps:
                deps[id(w)] = (w, "waw")
            for rd in self.readers.get(wkey, ()):
                if id(rd) not in deps:
                    deps[id(rd)] = (rd, "war")
        best = {}
        for d, kind in deps.values():
            if d is op:
                continue
            if not d.dma and d.eng == eng and not dma:
                if eng == "pe" or kind != "raw":
                    continue
            key = ("g", d.grp) if d.dma else ("e", d.eng)
            cur = best.get(key)
            if cur is None or d.idx > cur.idx:
                best[key] = d
        for d in best.values():
            op.deps.append(d)
            if not d.dma:
                d.signal = True
        if dma:
            c = self.grp_cum.get(grp, 0) + 16
            self.grp_cum[grp] = c
            op.cum = c
        for r in op.reads:
            self.readers.setdefault(r, []).append(op)
        for wkey in op.writes:
            self.last_writer[wkey] = op
            self.readers[wkey] = []
        self.ops.append(op)
        return op

    def pe(self, fn, reads, writes):
        return self._add("pe", fn, reads, writes)

    def act(self, fn, reads, writes):
        return self._add("act", fn, reads, writes)

    def dve(self, fn, reads, writes):
        return self._add("dve", fn, reads, writes)

    def pool(self, fn, reads, writes):
        return self._add("pool", fn, reads, writes)

    def dma(self, q, grp, fn, reads, writes):
        if grp == "const":
            writes = list(writes) + [("constchain",)]
        return self._add(q, fn, reads, writes, dma=True, grp=grp)

    def emit(self, stack, shared):
        nc = self.nc
        esem = shared["esem"]
        ecount = shared["ecount"]
        gslots = shared["gsem"]
        gcount = shared["gcount"]
        assert len(self.grp_cum) <= len(gslots), len(self.grp_cum)
        gidx = {g: i for i, g in enumerate(self.grp_cum)}
        gsem = {g: gslots[i] for g, i in gidx.items()}
        gbase = {g: gcount[i] for g, i in gidx.items()}
        cnt = dict(ecount)
        for op in self.ops:
            if not op.dma and op.signal:
                cnt[op.eng] += 1
                op.sigidx = cnt[op.eng]
            if op.dma:
                op.cum += gbase[op.grp]
        for e in self.ENGS:
            ecount[e] = cnt[e]
        for g, i in gidx.items():
            gcount[i] += self.grp_cum[g]
        per = {e: [] for e in self.ENGS}
        for op in self.ops:
            per[op.eng].append(op)
        final = [(gsem[g], gbase[g] + c) for g, c in self.grp_cum.items()]
        block = stack.enter_context(nc.Block())

        def body(ename):
            def f(eng):
                waited = {}
                for op in per[ename]:
                    need = []
                    for d in op.deps:
                        if d.dma:
                            key = ("g", d.grp)
                            val = d.cum
                            sem = gsem[d.grp]
                        else:
                            key = ("e", d.eng)
                            val = d.sigidx
                            sem = esem[d.eng]
                        if waited.get(key, 0) >= val:
                            continue
                        waited[key] = val
                        need.append((sem, val))
                    for sem, val in need[:-1]:
                        eng.wait_ge(sem, val)
                    ins = op.fn(eng)
                    if need:
                        ins._wait_ge(*need[-1])
                    if op.dma:
                        ins.then_inc(gsem[op.grp], 16)
                    elif op.signal:
                        ins.then_inc(esem[op.eng], 1)
                if ename == "sp":
                    for sem, c in final:
                        eng.wait_ge(sem, c)
            return f

        block.tensor(body("pe"))
        block.scalar(body("act"))
        block.vector(body("dve"))
        block.gpsimd(body("pool"))
        block.sync(body("sp"))


class K:
    def __init__(self, S):
        self.S = S
        self.nc = bass.Bass("TRN2", target_bir_lowering=False)
        self.P = None
        self.stack = None
        self.bank_rr = 0
        self.uid = 0
        self.nphase = 0
        self.gstack = ExitStack()
        nc = self.nc
        self.shared = {
            "esem": {e: self.gstack.enter_context(nc.semaphore("sem_" + e)) for e in Prog.ENGS},
            "ecount": {e: 0 for e in Prog.ENGS},
            "gsem": [self.gstack.enter_context(nc.semaphore("gsem%d" % i)) for i in range(28)],
            "gcount": [0] * 28,
        }

    def phase(self):
        return _Phase(self)

    def sb(self, name, shape, dt):
        return self.stack.enter_context(self.nc.sbuf_tensor("%s_p%d" % (name, self.nphase), shape, dt))

    def psum(self, name, shape, dt):
        return self.stack.enter_context(self.nc.psum_tensor("%s_p%d" % (name, self.nphase), shape, dt))

    def dram(self, name, shape, dt, kind="Internal"):
        import os
        kind = os.environ.get("SCRATCH_KIND", kind)
        return self.nc.dram_tensor(name, shape, dt, kind=kind).ap()

    def carve(self, shape):
        n = int(np.prod(shape))
        a = self.arena[self.arena_off:self.arena_off + n]
        self.arena_off += n
        if len(shape) == 3:
            return a.rearrange("(j p s) -> j p s", p=shape[1], s=shape[2])
        return a.rearrange("(p s) -> p s", s=shape[1])

    def bank(self, n=8, base=0):
        b = base + self.bank_rr % n
        self.bank_rr += 1
        return b


class _Phase:
    def __init__(self, k):
        self.k = k

    def __enter__(self):
        k = self.k
        k.nphase += 1
        k.P = Prog(k.nc, k.nphase)
        k.stack = ExitStack()
        k.stack.__enter__()
        k.stack.enter_context(k.nc.named_scope("ph%d" % k.nphase))
        k.bank_rr = 0
        k.uid = 0
        return k

    def __exit__(self, et, ev, tb):
        k = self.k
        if et is None:
            k.P.emit(k.stack, k.shared)
        k.stack.__exit__(et, ev, tb)
        return False


def mm(k, out, lhsT, rhs, start, stop, reads, writes):
    k.P.pe(lambda e, out=out, lhsT=lhsT, rhs=rhs, start=start, stop=stop:
           e.matmul(out, lhsT, rhs, start=start, stop=stop), reads, writes)


def emit_norm(k, x_sb, xn, T, gain_col, vecs, sqbuf, sqkey, rstd, ones_bf, ps, tmp_key="nrm"):
    P = k.P
    nsub = T // 512
    for kc in range(8):
        P.act(lambda e, kc=kc: e.activation(sqbuf[:, kc, :T], x_sb[:, kc, :T], AF.Square),
              [("x", kc)], [(sqkey, kc, s) for s in range(nsub)])
    for s in range(nsub):
        b = k.bank()
        sl = slice(s * 512, (s + 1) * 512)
        for kc in range(8):
            mm(k, ps[:, b, :], ones_bf[:, :], sqbuf[:, kc, sl], kc == 0, kc == 7,
               [(sqkey, kc, s), ("ones",)], [("ps", b)])
        P.act(lambda e, b=b, sl=sl: e.activation(rstd[:, sl], ps[:, b, :], AF.Ln,
                                                 bias=k.eps_col[:, 0:1], scale=1.0 / D),
              [("ps", b), ("eps",)], [("rstd", s)])
        P.act(lambda e, sl=sl: e.activation(rstd[:, sl], rstd[:, sl], AF.Exp, scale=-0.5), [("rstd", s)], [("rstd", s)])
        for kc in range(8):
            P.dve(lambda e, kc=kc, sl=sl: e.scalar_tensor_tensor(
                xn[:, kc, sl], x_sb[:, kc, sl], vecs[:, gain_col + kc:gain_col + kc + 1], rstd[:, sl],
                op0=ALU.mult, op1=ALU.mult),
                [("x", kc), ("rstd", s), ("vecs",)], [("xn", kc, s)])


def emit_ffn(k, li, fi, x_src, x_dst, wgu_d, wd_d, gain_col, final_gain_col=None, out_dst=None, pre=None):
    P = k.P
    S = k.S
    T = min(1024, S)
    nsub = T // 512
    B = k.bufs
    x_sb, xn, h, rstd, stmp, ps, vecs, ones_bf = (B["x"], B["xn"], B["h"], B["rstd"], B["stmp"],
                                                  B["ps"], B["vecs"], B["ones"])
    wgs, wgb, wds, wdb = B["wgs"], B["wgb"], B["wds"], B["wdb"]
    NG = S // T

    def load_wg(n):
        if n >= NG * NF:
            return
        f_, ss = n % NF, n % 2
        P.dma("sp", ("wgs", ss), lambda e, f_=f_, ss=ss: e.dma_start(out=wgs[:, ss, :], in_=wgu_d[f_]),
              [], [("wgs", ss)])

    def load_wd(m):
        if m >= NG * 8:
            return
        dm_, ss = m % 8, m % 2
        P.dma("sp", ("wds", ss), lambda e, dm_=dm_, ss=ss: e.dma_start(out=wds[:, ss, :], in_=wd_d[dm_]),
              [], [("wds", ss)])

    def load_xk(g_, kc):
        if g_ >= NG:
            return
        src = x_src[kc * 128:(kc + 1) * 128, g_ * T:(g_ + 1) * T]
        P.dma("sp", ("xld", kc), lambda e, src=src, kc=kc: e.dma_start(out=x_sb[:, kc, :T], in_=src),
              [("xdram", id(x_src), g_)], [("x", kc)])

    for kc in range(8):
        load_xk(0, kc)
    sqb_ = B["sq"]
    for g in range(S // T):
        tsl = slice(g * T, (g + 1) * T)
        if pre is not None:
            ocatT, woutb = pre
            oc_v = ocatT[:, :, tsl].rearrange("kc p t -> p kc t")
            P.dma("sp", "ocld", lambda e, oc_v=oc_v: e.dma_start(out=xn[:, :, :T], in_=oc_v),
                  [("ocdram", g)], [("xn", kc, s) for kc in range(8) for s in range(nsub)])
            for dm in range(8):
                for s in range(nsub):
                    sl = slice(s * 512, (s + 1) * 512)
                    b = k.bank()
                    for kc in range(8):
                        mm(k, ps[:, b, :], woutb[:, kc, dm * 128:(dm + 1) * 128], xn[:, kc, sl], kc == 0, kc == 7,
                           [("woutb",), ("xn", kc, s)], [("ps", b)])
                    P.dve(lambda e, b=b, dm=dm, sl=sl: e.tensor_tensor(
                        x_sb[:, dm, sl], x_sb[:, dm, sl], ps[:, b, :], op=ALU.add),
                        [("ps", b), ("x", dm)], [("x", dm)])
        emit_norm(k, x_sb, xn, T, gain_col, vecs, sqb_, "sq", rstd, ones_bf, ps)
        for f in range(NF):
            n = g * NF + f
            slot_s = n % 2
            slot_b = n % 3
            if n == 0:
                load_wg(0)
                load_wg(1)
                load_wd(0)
                load_wd(1)
            P.pool(lambda e, slot_s=slot_s, slot_b=slot_b: e.tensor_copy(wgb[:, slot_b, :], wgs[:, slot_s, :]),
                   [("wgs", slot_s)], [("wgb", slot_b)])
            load_wg(n + 2)
            for s in range(nsub):
                sl = slice(s * 512, (s + 1) * 512)
                bg = k.bank()
                bu = k.bank()
                for gu, b in ((0, bg), (1, bu)):
                    for kc in range(8):
                        o = (gu * 8 + kc) * 128
                        mm(k, ps[:, b, :], wgb[:, slot_b, o:o + 128], xn[:, kc, sl], kc == 0, kc == 7,
                           [("wgb", slot_b), ("xn", kc, s)], [("ps", b)])
                st = k.uid % 2
                k.uid += 1
                P.act(lambda e, bg=bg, st=st: e.activation(stmp[:, st, :], ps[:, bg, :], AF.Silu),
                      [("ps", bg)], [("stmp", st)])
                P.dve(lambda e, bu=bu, st=st, f=f, sl=sl: e.tensor_tensor(
                    h[:, f, sl], stmp[:, st, :], ps[:, bu, :], op=ALU.mult),
                    [("ps", bu), ("stmp", st)], [("h", f, s)])
        for dm in range(8):
            m = g * 8 + dm
            slot = m % 2
            P.act(lambda e, slot=slot: e.copy(wdb[:, slot, :], wds[:, slot, :]),
                  [("wds", slot)], [("wdb", slot)])
            load_wd(m + 2)
            for s in range(nsub):
                sl = slice(s * 512, (s + 1) * 512)
                b = k.bank()
                for f in range(NF):
                    mm(k, ps[:, b, :], wdb[:, slot, f * 128:(f + 1) * 128], h[:, f, sl], f == 0, f == NF - 1,
                       [("wdb", slot), ("h", f, s)], [("ps", b)])
                P.dve(lambda e, b=b, dm=dm, sl=sl: e.scalar_tensor_tensor(
                    x_sb[:, dm, sl], ps[:, b, :], 0.5, x_sb[:, dm, sl], op0=ALU.mult, op1=ALU.add),
                    [("ps", b), ("x", dm)], [("x", dm)])
            if final_gain_col is None:
                dst = x_dst[dm * 128:(dm + 1) * 128, tsl]
                P.dma("sp", ("xst", dm), lambda e, dst=dst, dm=dm: e.dma_start(out=dst, in_=x_sb[:, dm, :T]),
                      [("x", dm)], [("xdram", id(x_dst), g)])
                load_xk(g + 1, dm)
        if final_gain_col is None:
            pass
        else:
            P2 = k.P
            for kc in range(8):
                P2.act(lambda e, kc=kc: e.activation(sqb_[:, kc, :T], x_sb[:, kc, :T], AF.Square),
                       [("x", kc)], [("sq", kc, s) for s in range(nsub)])
            for s in range(nsub):
                b = k.bank()
                sl = slice(s * 512, (s + 1) * 512)
                for kc in range(8):
                    mm(k, ps[:, b, :], ones_bf[:, :], sqb_[:, kc, sl], kc == 0, kc == 7, [("sq", kc, s), ("ones",)], [("ps", b)])
                P2.act(lambda e, b=b, sl=sl: e.activation(rstd[:, sl], ps[:, b, :], AF.Ln,
                                                          bias=k.eps_col[:, 0:1], scale=1.0 / D),
                       [("ps", b), ("eps",)], [("rstd", s)])
                P2.act(lambda e, sl=sl: e.activation(rstd[:, sl], rstd[:, sl], AF.Exp, scale=-0.5),
                       [("rstd", s)], [("rstd", s)])
            for s in range(nsub):
                sl = slice(s * 512, (s + 1) * 512)
                for kc in range(8):
                    P2.dve(lambda e, kc=kc, sl=sl: e.scalar_tensor_tensor(
                        x_sb[:, kc, sl], x_sb[:, kc, sl], vecs[:, final_gain_col + kc:final_gain_col + kc + 1],
                        rstd[:, sl], op0=ALU.mult, op1=ALU.mult),
                        [("x", kc), ("rstd", s), ("vecs",)], [("x", kc)])
            od_v = out_dst[:, tsl].rearrange("(kc p) t -> p kc t", p=128)
            P2.dma("sp", "xst", lambda e, od_v=od_v: e.dma_start(out=od_v, in_=x_sb[:, :, :T]),
                   [("x", kc) for kc in range(8)], [("odram", g)])
            for kc in range(8):
                load_xk(g + 1, kc)


def load_wout(k, wout_d):
    B = k.bufs
    woutb = k.sb("woutb", [128, 8, 1024], BF16)
    for kc in range(8):
        slot = k.wd_ctr % 2
        k.wd_ctr += 1
        k.P.dma("sp", ("wds", slot), lambda e, kc=kc, slot=slot: e.dma_start(
            out=B["wds"][:, slot, 0:1024], in_=wout_d[:, kc, :]), [], [("wds", slot)])
        k.P.act(lambda e, kc=kc, slot=slot: e.copy(woutb[:, kc, :], B["wds"][:, slot, 0:1024]),
                [("wds", slot)], [("woutb",)])
    return woutb


def alloc_common(k, vecs_d, NV):
    S = k.S
    T = min(1024, S)
    B = {}
    B["x"] = k.sb("x_sb", [128, 8, T], F32)
    B["xn"] = k.sb("xn", [128, 8, T], BF16)
    B["h"] = k.sb("h", [128, NF, T], BF16)
    B["sq"] = k.sb("sq", [128, 8, T], BF16)
    B["rstd"] = k.sb("rstd", [128, T], F32)
    B["stmp"] = k.sb("stmp", [128, 2, 512], F32)
    B["wgs"] = k.sb("wgs", [128, 2, 2048], F32)
    B["wgb"] = k.sb("wgb", [128, 3, 2048], BF16)
    B["wds"] = k.sb("wds", [128, 2, NF * 128], F32)
    B["wdb"] = k.sb("wdb", [128, 2, NF * 128], BF16)
    B["vecs"] = k.sb("vecs", [128, NV], F32)
    B["ones"] = k.sb("ones", [128, 128], BF16)
    B["ps"] = k.psum("ps", [128, 8, 512], F32)
    k.eps_col = k.sb("epsc", [128, 1], F32)
    k.bufs = B
    k.wg_ctr = 0
    k.wd_ctr = 0
    P = k.P
    P.dma("sp", "const", lambda e: e.dma_start(out=B["vecs"][:, :], in_=vecs_d), [], [("vecs",)])
    P.dve(lambda e: e.memset(B["ones"][:, :], 1.0), [], [("ones",)])
    P.dve(lambda e: e.memset(k.eps_col[:, :], EPS), [], [("eps",)])
    return B


TWO_PI = 2.0 * math.pi
CW1 = 6.28125
CW2 = float(np.float32(TWO_PI - 6.28125))

C_INVA, C_SGNA, C_INVR, C_SGNR, C_127MP, C_P = 0, 1, 2, 3, 4, 5
C_DPOS, C_DNEG, C_MGE, C_MLE, C_IOTA1, C_IOTAR, C_IDENT = 8, 136, 264, 392, 520, 648, 776
C_TRI_INC, C_TRI_SUF, C_MASKLO, C_MASKUP, C_TRI_SSUF, C_TRI_SPRE = 904, 1032, 1160, 1288, 1416, 1544
NCST = 1672


def emit_rope_tables(k, posf, T, cst, inv_col, sgn_col, Ctab, Stab, tmp, key):
    P = k.P
    ang, yi, kf, th, w = tmp["ang"], tmp["yi"], tmp["kf"], tmp["th"], tmp["w"]
    R = [("posf",), ("cst",)]
    P.dve(lambda e: e.tensor_scalar(ang[:, :T], posf[:, :T], cst[:, inv_col:inv_col + 1], None, op0=ALU.mult),
           R, [("t_ang",)])
    P.dve(lambda e: e.tensor_scalar(yi[:, :T], ang[:, :T], 1.0 / TWO_PI, None, op0=ALU.mult),
           [("t_ang",)], [("t_yi",)])
    P.dve(lambda e: e.tensor_copy(kf[:, :T], yi[:, :T]), [("t_yi",)], [("t_kf",)])
    P.dve(lambda e: e.tensor_scalar(th[:, :T], kf[:, :T], -CW1, None, op0=ALU.mult), [("t_kf",)], [("t_th",)])
    P.dve(lambda e: e.tensor_tensor(th[:, :T], th[:, :T], ang[:, :T], op=ALU.add), [("t_th",), ("t_ang",)], [("t_th",)])
    P.dve(lambda e: e.tensor_scalar(kf[:, :T], kf[:, :T], -CW2, None, op0=ALU.mult), [("t_kf",)], [("t_kf",)])
    P.dve(lambda e: e.tensor_tensor(th[:, :T], th[:, :T], kf[:, :T], op=ALU.add), [("t_th",), ("t_kf",)], [("t_th",)])
    P.dve(lambda e: e.tensor_scalar(w[:, :T], th[:, :T], math.pi, TWO_PI, op0=ALU.is_gt, op1=ALU.mult),
           [("t_th",)], [("t_w",)])
    P.dve(lambda e: e.tensor_tensor(w[:, :T], th[:, :T], w[:, :T], op=ALU.subtract), [("t_th",), ("t_w",)], [("t_w",)])
    P.act(lambda e: e.activation(Stab[:, :T], w[:, :T], AF.Sin, scale=cst[:, sgn_col:sgn_col + 1]),
          [("t_w",), ("cst",)], [(key, "S")])
    P.dve(lambda e: e.tensor_scalar(w[:, :T], th[:, :T], math.pi / 2, TWO_PI, op0=ALU.is_gt, op1=ALU.mult),
           [("t_th",)], [("t_w",)])
    P.dve(lambda e: e.tensor_tensor(w[:, :T], th[:, :T], w[:, :T], op=ALU.subtract), [("t_th",), ("t_w",)], [("t_w",)])
    P.act(lambda e: e.activation(Ctab[:, :T], w[:, :T], AF.Sin, bias=k.halfpi_col[:, 0:1]),
          [("t_w",), ("hpi",)], [(key, "C")])


def emit_inproj(k, x_src, gain_col, win_d, pairs, wtok_d, ntokc, tok_dsts, pos_d, cst_d, vecs_d, NV):
    P = k.P
    S = k.S
    T = min(1024, S)
    nsub = T // 512
    x_sb = k.sb("x", [128, 8, T], F32)
    xn = k.sb("xn", [128, 8, T], BF16)
    sq = k.sb("sq", [128, 8, T], BF16)
    rstd = k.sb("rstd", [128, T], F32)
    wgs = k.sb("wgs", [128, 2, 2048], F32)
    wgb = k.sb("wgb", [128, 3, 2048], BF16)
    wts = k.sb("wts", [128, 2, ntokc], F32)
    wtok = k.sb("wtok", [128, 8, ntokc], BF16)
    vecs = k.sb("vecs", [128, NV], F32)
    cst = k.sb("cst", [128, NCST], F32)
    ones = k.sb("ones", [128, 128], BF16)
    ost = k.sb("ost", [128, 8, 512], BF16)
    vst = k.sb("vst", [128, 8, 512], BF16)
    t12 = k.sb("t12", [128, 4, 512], F32)
    ps = k.psum("ps", [128, 8, 512], F32)
    k.eps_col = k.sb("epsc", [128, 1], F32)
    k.halfpi_col = k.sb("hpic", [128, 1], F32)
    rope = any(p[0].startswith("rope") for p in pairs)
    P.dma("sp", "const", lambda e: e.dma_start(out=vecs[:, :], in_=vecs_d), [], [("vecs",)])
    P.dma("sp", "const", lambda e: e.dma_start(out=cst[:, :], in_=cst_d), [], [("cst",)])
    P.dve(lambda e: e.memset(ones[:, :], 1.0), [], [("ones",)])
    P.dve(lambda e: e.memset(k.eps_col[:, :], EPS), [], [("eps",)])
    P.dve(lambda e: e.memset(k.halfpi_col[:, :], math.pi / 2), [], [("hpi",)])
    if rope:
        posi = k.sb("posi", [128, T], I32)
        posf = k.sb("posf", [128, T], F32)
        tabs = {n: k.sb("tab" + n, [128, T], F32) for n in ("CA", "SA", "CR", "SR")}
        tmp = {"ang": k.sb("ang", [128, T], F32), "yi": k.sb("yi", [128, T], I32),
               "kf": k.sb("kf", [128, T], F32), "th": k.sb("th", [128, T], F32), "w": k.sb("w", [128, T], F32)}
    for kc in range(8):
        sl = kc % 2
        P.dma("sp", ("wts", sl), lambda e, kc=kc, sl=sl: e.dma_start(out=wts[:, sl, :], in_=wtok_d[:, kc, :]),
              [], [("wts", sl)])
        P.act(lambda e, kc=kc, sl=sl: e.copy(wtok[:, kc, :], wts[:, sl, :]), [("wts", sl)], [("wtok", kc)])
    wg_ctr = 0
    n_w = (S // T) * len(pairs)

    def load_w(n):
        if n >= n_w:
            return
        pi_, ss = n % len(pairs), n % 2
        P.dma("sp", ("wgs", ss), lambda e, pi_=pi_, ss=ss: e.dma_start(out=wgs[:, ss, :], in_=win_d[pi_]),
              [], [("wgs", ss)])

    def load_x(g_):
        if g_ >= S // T:
            return
        tsl_ = slice(g_ * T, (g_ + 1) * T)
        xs_v = x_src[:, tsl_].rearrange("(kc p) t -> p kc t", p=128)
        P.dma("sp", "xld", lambda e, xs_v=xs_v: e.dma_start(out=x_sb[:, :, :], in_=xs_v),
              [("xdram", g_)], [("x", kc) for kc in range(8)])
        if rope:
            P.dma("sp", "pos", lambda e, tsl_=tsl_: e.dma_start(out=posi[:, :], in_=pos_d[tsl_].partition_broadcast(128)),
                  [], [("posi",)])

    load_x(0)
    for g in range(S // T):
        tsl = slice(g * T, (g + 1) * T)
        if rope:
            P.dve(lambda e: e.tensor_copy(posf[:, :], posi[:, :]), [("posi",)], [("posf",)])
            emit_rope_tables(k, posf, T, cst, C_INVA, C_SGNA, tabs["CA"], tabs["SA"], tmp, "tabA")
            emit_rope_tables(k, posf, T, cst, C_INVR, C_SGNR, tabs["CR"], tabs["SR"], tmp, "tabR")
        emit_norm(k, x_sb, xn, T, gain_col, vecs, sq, "sq", rstd, ones, ps)
        load_x(g + 1)
        for pi, (kind, d0, d1) in enumerate(pairs):
            slot_s = wg_ctr % 2
            slot_b = wg_ctr % 3
            wg_ctr += 1
            if wg_ctr == 1:
                load_w(0)
                load_w(1)
            P.act(lambda e, slot_s=slot_s, slot_b=slot_b: e.copy(wgb[:, slot_b, :], wgs[:, slot_s, :]),
                  [("wgs", slot_s)], [("wgb", slot_b)])
            load_w(wg_ctr + 1)
            for s in range(nsub):
                sl = slice(s * 512, (s + 1) * 512)
                dsl = slice(g * T + s * 512, g * T + (s + 1) * 512)
                b0 = k.bank()
                b1 = k.bank()
                for half, b in ((0, b0), (1, b1)):
                    for kc in range(8):
                        o = (half * 8 + kc) * 128
                        mm(k, ps[:, b, :], wgb[:, slot_b, o:o + 128], xn[:, kc, sl], kc == 0, kc == 7,
                           [("wgb", slot_b), ("xn", kc, s)], [("ps", b)])
                if kind.startswith("rope"):
                    tk = "tab" + kind[-1]
                    Ct, St = tabs["C" + kind[-1]], tabs["S" + kind[-1]]
                    u = k.uid % 2
                    k.uid += 1
                    os_ = k.uid % 8
                    P.dve(lambda e, b0=b0, u=u, sl=sl, Ct=Ct: e.tensor_tensor(
                        t12[:, 2 * u, :], ps[:, b0, :], Ct[:, sl], op=ALU.mult),
                        [("ps", b0), (tk, "C")], [("t12", 2 * u)])
                    P.dve(lambda e, b1=b1, u=u, sl=sl, St=St: e.tensor_tensor(
                        t12[:, 2 * u + 1, :], ps[:, b1, :], St[:, sl], op=ALU.mult),
                        [("ps", b1), (tk, "S")], [("t12", 2 * u + 1)])
                    P.pool(lambda e, u=u, os_=os_: e.tensor_tensor(
                        ost[:, os_, :], t12[:, 2 * u, :], t12[:, 2 * u + 1, :], op=ALU.add),
                        [("t12", 2 * u), ("t12", 2 * u + 1)], [("ost", os_)])
                    P.dma("sp", ("ost", os_), lambda e, os_=os_, d0=d0, dsl=dsl: e.dma_start(
                        out=d0[:, dsl], in_=ost[:, os_, :]), [("ost", os_)], [("sc_fm", id(d0), g)])
                else:
                    fn = AF.Silu if kind == "silu" else AF.Copy
                    for b, dd in ((b0, d0), (b1, d1)):
                        if dd is None:
                            continue
                        k.uid += 1
                        os_ = k.uid % 8
                        P.act(lambda e, b=b, os_=os_, fn=fn: e.activation(ost[:, os_, :], ps[:, b, :], fn),
                              [("ps", b)], [("ost", os_)])
                        P.dma("sp", ("ost", os_), lambda e, os_=os_, dd=dd, dsl=dsl: e.dma_start(
                            out=dd[:, dsl], in_=ost[:, os_, :]), [("ost", os_)], [("sc_fm", id(dd), g)])
        for tt in range(T // 128):
            t0 = g * T + tt * 128
            for (c0, ncol, dd) in tok_dsts:
                b = k.bank()
                for kc in range(8):
                    mm(k, ps[:, b, :ncol], xn[:, kc, tt * 128:(tt + 1) * 128], wtok[:, kc, c0:c0 + ncol],
                       kc == 0, kc == 7, [("wtok", kc), ("xn", kc, tt // 4)], [("ps", b)])
                k.uid += 1
                vs = k.uid % 8
                P.act(lambda e, b=b, vs=vs, ncol=ncol: e.copy(vst[:, vs, :ncol], ps[:, b, :ncol]),
                      [("ps", b)], [("vst", vs)])
                P.dma("sp", ("vst", vs), lambda e, vs=vs, dd=dd, t0=t0, ncol=ncol: e.dma_start(
                    out=dd[t0:t0 + 128, :], in_=vst[:, vs, :ncol]), [("vst", vs)], [("sc_tm", id(dd), g)])


def dma_tiles(k, grp, dst, src, nt, key):
    for t0 in range(0, nt, 8):
        t1 = min(nt, t0 + 8)
        sv = src[t0 * 128:t1 * 128, :].rearrange("(t p) c -> p t c", p=128)
        k.P.dma("sp", grp, lambda e, sv=sv, t0=t0, t1=t1: e.dma_start(out=dst[:, t0:t1, :], in_=sv), [], [key])


def emit_pnorm(k, O, T, ps, ones, sqb, rs, ncp, n_feat, eps_col, pk=None):
    P = k.P
    b = k.bank(4, 0)
    pkeys = [("ps", b)] if pk is None else pk(b)
    for i, (Oap, Okey) in enumerate(O):
        P.act(lambda e, Oap=Oap, i=i: e.activation(sqb[:, i, :T], Oap, AF.Square), [Okey], [("pn_sq", i)])
        mm(k, ps[:, b, :T], ones[:, :], sqb[:, i, :T], i == 0, i == len(O) - 1, [("pn_sq", i), ("ones",)], pkeys)
    P.act(lambda e, b=b: e.activation(rs[:, :T], ps[:, b, :T], AF.Ln, bias=eps_col[:, 0:1], scale=1.0 / n_feat),
          pkeys + [("eps",)], [("pn_rs",)])
    P.act(lambda e: e.activation(rs[:, :T], rs[:, :T], AF.Exp, scale=-0.5), [("pn_rs",)], [("pn_rs",)])


PE_L = (1,)


def emit_diffattn(k, qkT, vtok_a, ocatT, vecs_d, NV, V_LQ, V_DANORM, lam_init):
    P = k.P
    S = k.S
    NQ = S // 512
    NK = S // 128
    qT = k.sb("qT", [128, 2, S], BF16)
    kT = k.sb("kT", [128, 2, S], BF16)
    V = k.sb("V", [128, 2, NK, 128], BF16)
    pb = k.sb("pb", [128, 4, 512], BF16)
    accL = k.sb("accL", [128, 2, 512], F32)
    Oc = k.sb("Oc", [128, 2, 512], F32)
    onesf = k.sb("onesf", [128, 128], F32)
    negone = k.sb("negone", [128, 512], F32)
    vecs = k.sb("vecs", [128, NV], F32)
    ones = k.sb("ones", [128, 128], BF16)
    eps_col = k.sb("epsc", [128, 1], F32)
    sc = k.sb("scal", [128, 8], F32)
    junk = k.sb("junk", [128, 2, 64], F32)
    r12 = k.sb("r12", [128, 2, 512], F32)
    t12 = k.sb("t12", [128, 2, 512], F32)
    Of = k.sb("Of", [128, 512], F32)
    sqb = k.sb("sqb", [128, 1, 512], BF16)
    rs = k.sb("rs", [128, 512], F32)
    ob = k.sb("ob", [128, 2, 512], BF16)
    ps = k.psum("ps", [128, 8, 512], F32)
    P.dma("sp", "const", lambda e: e.dma_start(out=vecs[:, :], in_=vecs_d), [], [("vecs",)])
    P.dve(lambda e: e.memset(ones[:, :], 1.0), [], [("ones",)])
    P.dve(lambda e: e.memset(onesf[:, :], 1.0), [], [("onesf",)])
    P.dve(lambda e: e.memset(negone[:, :], -1.0), [], [("negone",)])
    P.dve(lambda e: e.memset(eps_col[:, :], EPS), [], [("eps",)])
    for i in range(2):
        a = V_LQ + i * 128
        P.dve(lambda e, a=a, i=i: e.scalar_tensor_tensor(junk[:, i, :], vecs[:, a:a + 64], 1.0, vecs[:, a + 64:a + 128],
                                                         op0=ALU.mult, op1=ALU.mult, accum_out=sc[:, i:i + 1]),
              [("vecs",)], [("sc", i), ("junk", i)])
        P.act(lambda e, i=i: e.activation(sc[:, i:i + 1], sc[:, i:i + 1], AF.Exp), [("sc", i)], [("sc", i)])
    P.dve(lambda e: e.tensor_tensor(sc[:, 2:3], sc[:, 1:2], sc[:, 0:1], op=ALU.subtract), [("sc", 0), ("sc", 1)], [("sc", 2)])
    P.dve(lambda e: e.tensor_scalar(sc[:, 2:3], sc[:, 2:3], -lam_init, None, op0=ALU.add), [("sc", 2)], [("sc", 2)])
    P.dve(lambda e: e.tensor_scalar(sc[:, 3:4], vecs[:, V_DANORM:V_DANORM + 1], 1.0 - lam_init, None, op0=ALU.mult),
          [("vecs",)], [("sc", 3)])
    def load_head(h_):
        if h_ >= 4:
            return
        sl_ = h_ % 2
        P.dma("sp", ("hq", sl_), lambda e: e.dma_start(out=qT[:, sl_, :], in_=qkT[h_]), [], [("qT", sl_)])
        P.dma("sp", ("hk", sl_), lambda e: e.dma_start(out=kT[:, sl_, :], in_=qkT[4 + h_]), [], [("kT", sl_)])
        dma_tiles(k, ("hv", sl_), V[:, sl_, :, :], vtok_a[:, h_ * 128:(h_ + 1) * 128], NK, ("V", sl_))

    load_head(0)
    for h in range(4):
        sl = h % 2
        load_head(h + 1)
        for qg in range(NQ):
            qs = slice(qg * 512, (qg + 1) * 512)
            bO = (4, 5)
            bL = (6, 7)
            sbanks = {}

            def scores(kc):
                ks = slice(kc * 128, (kc + 1) * 128)
                bb = []
                for c in range(2):
                    b = k.bank(4, 0)
                    bb.append(b)
                    pr = slice(c * 64, (c + 1) * 64)
                    mm(k, ps[:, b, :], kT[pr, sl, ks], qT[pr, sl, qs], True, True,
                       [("kT", sl), ("qT", sl)], [("ps", b)])
                sbanks[kc] = bb

            scores(0)
            for kc in range(NK):
                if kc + 1 < NK:
                    scores(kc + 1)
                for c in range(2):
                    b = sbanks[kc][c]
                    k.uid += 1
                    pp = k.uid % 4
                    P.act(lambda e, b=b, pp=pp: e.activation(pb[:, pp, :], ps[:, b, :], AF.Exp, scale=0.125),
                          [("ps", b)], [("pb", pp)])
                    mm(k, ps[:, bO[c], :], V[:, sl, kc, :], pb[:, pp, :], kc == 0, kc == NK - 1,
                       [("V", sl), ("pb", pp)], [("ps", bO[c])])
                    if c in PE_L:
                        mm(k, ps[:, bL[c], :], ones[:, :], pb[:, pp, :], kc == 0, kc == NK - 1,
                           [("ones",), ("pb", pp)], [("ps", bL[c])])
                    elif kc == 0:
                        P.dve(lambda e, c=c, pp=pp: e.tensor_copy(accL[:, c, :], pb[:, pp, :]),
                              [("pb", pp)], [("accL", c)])
                    else:
                        P.dve(lambda e, c=c, pp=pp: e.tensor_tensor(accL[:, c, :], accL[:, c, :], pb[:, pp, :], op=ALU.add),
                              [("pb", pp), ("accL", c)], [("accL", c)])
            for c in range(2):
                P.dve(lambda e, c=c: e.tensor_copy(Oc[:, c, :], ps[:, bO[c], :]), [("ps", bO[c])], [("Oc", c)])
            for c in range(2):
                if c in PE_L:
                    continue
                mm(k, ps[:, bL[c], :], onesf[:, :], accL[:, c, :], True, True, [("onesf",), ("accL", c)], [("ps", bL[c])])
            for c in range(2):
                P.act(lambda e, c=c: e.activation(r12[:, c, :], ps[:, bL[c], :], AF.Ln), [("ps", bL[c])], [("r12", c)])
                P.act(lambda e, c=c: e.activation(r12[:, c, :], r12[:, c, :], AF.Exp, scale=-1.0),
                      [("r12", c)], [("r12", c)])
                P.dve(lambda e, c=c: e.tensor_tensor(t12[:, c, :], Oc[:, c, :], r12[:, c, :], op=ALU.mult),
                      [("Oc", c), ("r12", c)], [("t12", c)])
            P.dve(lambda e: e.scalar_tensor_tensor(Of[:, :], t12[:, 1, :], sc[:, 2:3], t12[:, 0, :],
                                                   op0=ALU.mult, op1=ALU.add),
                  [("t12", 0), ("t12", 1), ("sc", 2)], [("Of",)])
            emit_pnorm(k, [(Of[:, :], ("Of",))], 512, ps, ones, sqb, rs, 1, 128.0, eps_col)
            k.uid += 1
            oo = k.uid % 2
            P.dve(lambda e, oo=oo: e.scalar_tensor_tensor(ob[:, oo, :], Of[:, :], sc[:, 3:4], rs[:, :],
                                                          op0=ALU.mult, op1=ALU.mult),
                  [("Of",), ("sc", 3), ("pn_rs",)], [("ob", oo)])
            P.dma("sp", ("ob", oo), lambda e, oo=oo, h=h, qs=qs: e.dma_start(out=ocatT[h][:, qs], in_=ob[:, oo, :]),
                  [("ob", oo)], [("ocat", h, qg)])


def emit_retention(k, qkT, vtok_r, sgT, ocatT, vecs_d, NV, cst_d, V_RLF, V_RLB, V_RETNORM):
    P = k.P
    S = k.S
    NK = S // 128
    NQ = S // 512
    q64 = k.sb("q64", [64, S], BF16)
    k64 = k.sb("k64", [64, S], BF16)
    qf = k.sb("qf", [64, S], BF16)
    qb = k.sb("qb", [64, S], BF16)
    V = k.sb("V", [128, NK, 128], BF16)
    sg = k.sb("sg", [128, S], BF16)
    AT = k.sb("AT", [128, NK, 128], BF16)
    kfb = k.sb("kfb", [128, 4, 128], BF16)
    kvs = k.sb("kvs", [64, NK, 256], F32)
    SF = k.sb("SF", [64, NK, 128], F32)
    SB = k.sb("SB", [64, NK, 128], F32)
    SFb = k.sb("SFb", [64, NK, 128], BF16)
    SBb = k.sb("SBb", [64, NK, 128], BF16)
    vecs = k.sb("vecs", [128, NV], F32)
    cst = k.sb("cst", [128, NCST], F32)
    identb = k.sb("identb", [128, 128], BF16)
    ones = k.sb("ones", [128, 128], BF16)
    eps_col = k.sb("epsc", [128, 1], F32)
    one_col = k.sb("onec", [128, 1], F32)
    sc = k.sb("scal", [128, 8], F32)
    E12 = k.sb("E12", [128, 2, 128], F32)
    Dc = k.sb("Dc", [128, 128], F32)
    qd = k.sb("qd", [128, 2, 512], F32)
    Of = k.sb("Of", [128, 512], F32)
    sqb = k.sb("sqb", [128, 1, 512], BF16)
    rs = k.sb("rs", [128, 512], F32)
    tt = k.sb("tt", [128, 512], F32)
    ob = k.sb("ob", [128, 2, 512], BF16)
    ps = k.psum("ps", [128, 7, 512], F32)
    P.dma("sp", "const", lambda e: e.dma_start(out=vecs[:, :], in_=vecs_d), [], [("vecs",)])
    P.dma("sp", "const", lambda e: e.dma_start(out=cst[:, :], in_=cst_d), [], [("cst",)])
    P.dve(lambda e: e.memset(ones[:, :], 1.0), [], [("ones",)])
    P.dve(lambda e: e.memset(eps_col[:, :], EPS), [], [("eps",)])
    P.dve(lambda e: e.memset(one_col[:, :], 1.0), [], [("onec",)])
    P.dve(lambda e: e.tensor_copy(identb[:, :], cst[:, C_IDENT:C_IDENT + 128]), [("cst",)], [("identb",)])
    for h in range(4):
        pr = slice((h % 2) * 64, (h % 2) * 64 + 64)
        P.dma("sp", "hq", lambda e, h=h, pr=pr: e.dma_start(out=q64[:, :], in_=qkT[8 + h // 2][pr, :]), [], [("q64",)])
        P.dma("sp", "hk", lambda e, h=h, pr=pr: e.dma_start(out=k64[:, :], in_=qkT[10 + h // 2][pr, :]), [], [("k64",)])
        dma_tiles(k, "hv", V, vtok_r[:, h * 128:(h + 1) * 128], NK, ("V",))
        P.dma("sp", "hg", lambda e, h=h: e.dma_start(out=sg[:, :], in_=sgT[h]), [], [("sg",)])
        for d, col in ((0, V_RLF + h), (1, V_RLB + h)):
            P.act(lambda e, d=d, col=col: e.activation(sc[:, d:d + 1], vecs[:, col:col + 1], AF.Exp, scale=-1.0),
                  [("vecs",)], [("sc", d)])
            P.act(lambda e, d=d: e.activation(sc[:, d:d + 1], sc[:, d:d + 1], AF.Ln, bias=one_col[:, 0:1]),
                  [("sc", d), ("onec",)], [("sc", d)])
            P.dve(lambda e, d=d: e.tensor_scalar(sc[:, d:d + 1], sc[:, d:d + 1], -1.0, None, op0=ALU.mult),
                  [("sc", d)], [("sc", d)])
            ccol = C_127MP if d == 0 else C_P
            P.act(lambda e, d=d, ccol=ccol: e.activation(sc[:, 2 + d:3 + d], cst[:, ccol:ccol + 1], AF.Exp,
                                                         scale=sc[:, d:d + 1]),
                  [("sc", d), ("cst",)], [("sc", 2 + d)])
            P.act(lambda e, d=d: e.activation(sc[:, 4 + d:5 + d], sc[:, d:d + 1], AF.Exp, scale=128.0),
                  [("sc", d)], [("sc", 4 + d)])
            dcol, mcol = (C_DPOS, C_MGE) if d == 0 else (C_DNEG, C_MLE)
            P.act(lambda e, d=d, dcol=dcol: e.activation(E12[:, d, :], cst[:, dcol:dcol + 128], AF.Exp,
                                                         scale=sc[:, d:d + 1]),
                  [("sc", d), ("cst",)], [("E12", d)])
            P.dve(lambda e, d=d, mcol=mcol: e.tensor_tensor(E12[:, d, :], E12[:, d, :], cst[:, mcol:mcol + 128],
                                                            op=ALU.mult), [("E12", d), ("cst",)], [("E12", d)])
            icol = C_IOTA1 if d == 0 else C_IOTAR
            P.act(lambda e, d=d, icol=icol: e.activation(qd[:, d, 0:128], cst[:, icol:icol + 128], AF.Exp,
                                                         scale=sc[:, d:d + 1]),
                  [("sc", d), ("cst",)], [("qd", d)])
            P.dve(lambda e, d=d: e.tensor_scalar(qd[:, d, 0:128], qd[:, d, 0:128], 0.125, None, op0=ALU.mult),
                  [("qd", d)], [("qd", d)])
            for j in range(1, 4):
                P.dve(lambda e, d=d, j=j: e.tensor_copy(qd[:, d, j * 128:(j + 1) * 128], qd[:, d, 0:128]),
                      [("qd", d)], [("qd", d)])
        P.dve(lambda e: e.tensor_tensor(Dc[:, :], E12[:, 0, :], E12[:, 1, :], op=ALU.add),
              [("E12", 0), ("E12", 1)], [("Dc",)])
        P.dve(lambda e: e.tensor_scalar(Dc[:, :], Dc[:, :], 0.125, None, op0=ALU.mult), [("Dc",)], [("Dc",)])
        for g in range(NQ):
            gs = slice(g * 512, (g + 1) * 512)
            P.dve(lambda e, gs=gs: e.tensor_tensor(qf[:, gs], q64[:, gs], qd[0:64, 0, :], op=ALU.mult),
                  [("q64",), ("qd", 0)], [("qf", g)])
            P.pool(lambda e, gs=gs: e.tensor_tensor(qb[:, gs], q64[:, gs], qd[0:64, 1, :], op=ALU.mult),
                   [("q64",), ("qd", 1)], [("qb", g)])
        P.dve(lambda e: e.memset(SF[:, 0, :], 0.0), [], [("SF", 0)])
        P.dve(lambda e: e.memset(SB[:, NK - 1, :], 0.0), [], [("SB", NK - 1)])
        import os
        rstop = int(os.environ.get("RET_STOP", "9"))
        if rstop < 1:
            continue
        def retA(c):
            cs = slice(c * 128, (c + 1) * 128)
            ks = c % 4
            b = k.bank(7, 0)
            mm(k, ps[:, b, 0:128], k64[:, cs], q64[:, cs], True, True, [("k64",), ("q64",)], [("ps", b)])
            P.dve(lambda e, b=b, c=c: e.tensor_tensor(AT[:, c, :], ps[:, b, 0:128], Dc[:, :], op=ALU.mult),
                  [("ps", b), ("Dc",)], [("AT", c)])
            bt = k.bank(7, 0)
            mm(k, ps[:, bt, 0:64], k64[:, cs], identb[0:64, 0:64], True, True, [("k64",), ("identb",)], [("ps", bt)])
            P.dve(lambda e, bt=bt, ks=ks: e.tensor_scalar(kfb[:, ks, 0:64], ps[:, bt, 0:64], sc[:, 2:3], None,
                                                          op0=ALU.mult),
                  [("ps", bt), ("sc", 2)], [("kfb", ks, 0)])
            P.dve(lambda e, bt=bt, ks=ks: e.tensor_scalar(kfb[:, ks, 64:128], ps[:, bt, 0:64], sc[:, 3:4], None,
                                                          op0=ALU.mult),
                  [("ps", bt), ("sc", 3)], [("kfb", ks, 1)])

        def retB(c):
            ks = c % 4
            b2 = k.bank(7, 0)
            mm(k, ps[0:64, b2, 0:128], kfb[:, ks, 0:64], V[:, c, :], True, True, [("kfb", ks, 0), ("V",)], [("ps", b2)])
            mm(k, ps[0:64, b2, 128:256], kfb[:, ks, 64:128], V[:, c, :], True, True, [("kfb", ks, 1), ("V",)], [("ps", b2)])
            P.act(lambda e, b2=b2, c=c: e.copy(kvs[:, c, :], ps[0:64, b2, 0:256]), [("ps", b2)], [("kvs", c)])

        for step in range(NK + 2):
            if step < NK:
                retA(step)
            if step >= 2:
                retB(step - 2)
        if rstop < 2:
            continue
        for c in range(NK - 1):
            P.dve(lambda e, c=c: e.scalar_tensor_tensor(SF[:, c + 1, :], SF[:, c, :], sc[0:64, 4:5], kvs[:, c, 0:128],
                                                        op0=ALU.mult, op1=ALU.add),
                  [("SF", c), ("kvs", c), ("sc", 4)], [("SF", c + 1)])
        for c in range(NK - 1, 0, -1):
            P.dve(lambda e, c=c: e.scalar_tensor_tensor(SB[:, c - 1, :], SB[:, c, :], sc[0:64, 5:6], kvs[:, c, 128:256],
                                                        op0=ALU.mult, op1=ALU.add),
                  [("SB", c), ("kvs", c), ("sc", 5)], [("SB", c - 1)])
        P.act(lambda e: e.copy(SFb[:, :, :], SF[:, :, :]), [("SF", c) for c in range(NK)], [("SFb",)])
        P.act(lambda e: e.copy(SBb[:, :, :], SB[:, :, :]), [("SB", c) for c in range(NK)], [("SBb",)])
        if rstop < 3:
            continue
        for g in range(NQ):
            gs = slice(g * 512, (g + 1) * 512)
            b = k.bank(7, 0)
            for j in range(4):
                c = g * 4 + j
                cs = slice(c * 128, (c + 1) * 128)
                js = slice(j * 128, (j + 1) * 128)
                mm(k, ps[:, b, js], V[:, c, :], AT[:, c, :], True, False, [("V",), ("AT", c)], [("ps", b)])
                mm(k, ps[:, b, js], SFb[:, c, :], qf[:, cs], False, False, [("SFb",), ("qf", g)], [("ps", b)])
                mm(k, ps[:, b, js], SBb[:, c, :], qb[:, cs], False, True, [("SBb",), ("qb", g)], [("ps", b)])
            P.dve(lambda e, b=b: e.tensor_copy(Of[:, :], ps[:, b, :]), [("ps", b)], [("Of",)])
            emit_pnorm(k, [(Of[:, :], ("Of",))], 512, ps, ones, sqb, rs, 1, 128.0, eps_col)
            P.dve(lambda e: e.scalar_tensor_tensor(tt[:, :], Of[:, :], vecs[:, V_RETNORM:V_RETNORM + 1], rs[:, :],
                                                   op0=ALU.mult, op1=ALU.mult),
                  [("Of",), ("vecs",), ("pn_rs",)], [("tt",)])
            k.uid += 1
            oo = k.uid % 2
            P.dve(lambda e, oo=oo, gs=gs: e.tensor_tensor(ob[:, oo, :], tt[:, :], sg[:, gs], op=ALU.mult),
                  [("tt",), ("sg",)], [("ob", oo)])
            P.dma("sp", ("ob", oo), lambda e, oo=oo, h=h, gs=gs: e.dma_start(out=ocatT[4 + h][:, gs], in_=ob[:, oo, :]),
                  [("ob", oo)], [("ocat", h, g)])


V_FFN1 = (0, 16)
V_FFN2 = (8, 24)
V_AB, V_C, V_FINAL, V_DANORM, V_RETNORM, V_GLANORM = 32, 40, 48, 56, 57, 58
V_RLF, V_RLB, V_LQ = 60, 64, 68
NV = 68 + 256


def lay_vec(v):
    return np.ascontiguousarray(np.asarray(v, np.float32).reshape(-1, 128).T)


def lay_wgu(Wg, Wu):
    a = np.stack([Wg, Wu], 0).reshape(2, 8, 128, -1, 128)
    nf = a.shape[3]
    return np.ascontiguousarray(a.transpose(3, 2, 0, 1, 4)).reshape(nf, 128, 2048)


def lay_wd(Wd):
    a = Wd.reshape(NF, 128, 8, 128)
    return np.ascontiguousarray(a.transpose(2, 1, 0, 3)).reshape(8, 128, NF * 128)


def lay_kmajor(W):
    return np.ascontiguousarray(W.reshape(8, 128, -1).transpose(1, 0, 2))


def build_cst():
    c = np.zeros((128, NCST), np.float32)
    p = np.arange(128)
    d = p % 64
    inva = np.where(d < 16, 500000.0 ** (-(2.0 * (d % 8)) / 16.0), 0.0)
    ia = (1.0 / (np.float32(500000.0) ** (np.arange(0, 16, 2, dtype=np.float32) / np.float32(16)))).astype(np.float32)
    ir = (1.0 / (np.float32(10000.0) ** (np.arange(0, 64, 2, dtype=np.float32) / np.float32(64)))).astype(np.float32)
    c[:, C_INVA] = np.where(d < 16, ia[d % 8], 0.0)
    c[:, C_SGNA] = np.where(d < 8, -1.0, 1.0)
    c[:, C_INVR] = ir[d % 32]
    c[:, C_SGNR] = np.where(d < 32, -1.0, 1.0)
    c[:, C_127MP] = 127 - p
    c[:, C_P] = p
    diff = (p[None, :] - p[:, None]).astype(np.float32)
    c[:, C_DPOS:C_DPOS + 128] = np.maximum(diff, 0)
    c[:, C_DNEG:C_DNEG + 128] = np.maximum(-diff, 0)
    c[:, C_MGE:C_MGE + 128] = (diff >= 0)
    c[:, C_MLE:C_MLE + 128] = (diff <= 0)
    c[:, C_IOTA1:C_IOTA1 + 128] = (p + 1)[None, :]
    c[:, C_IOTAR:C_IOTAR + 128] = (128 - p)[None, :]
    c[:, C_IDENT:C_IDENT + 128] = np.eye(128)
    same = (p[:, None] // 64) == (p[None, :] // 64)
    sd = p[:, None]
    td = p[None, :]
    c[:, C_TRI_INC:C_TRI_INC + 128] = same & (sd <= td)
    c[:, C_TRI_SUF:C_TRI_SUF + 128] = same & (sd >= td)
    c[:, C_MASKLO:C_MASKLO + 128] = same & (sd <= td)
    c[:, C_MASKUP:C_MASKUP + 128] = same & (sd >= td)
    c[:, C_TRI_SSUF:C_TRI_SSUF + 128] = same & (sd > td)
    c[:, C_TRI_SPRE:C_TRI_SPRE + 128] = same & (sd < td)
    return c


def swap_perm_da():
    idx = np.arange(512)
    d = idx % 64
    return np.where(d < 8, idx + 8, np.where(d < 16, idx - 8, idx))


def swap_perm_ret():
    idx = np.arange(256)
    d = idx % 64
    return np.where(d < 32, idx + 32, idx - 32)


def lay_win_ab(W):
    qa, ka, va, qr, kr, vr, gr = (W[:, 0:512], W[:, 512:1024], W[:, 1024:1536], W[:, 1536:1792],
                                  W[:, 1792:2048], W[:, 2048:2560], W[:, 2560:3072])
    pa, prr = swap_perm_da(), swap_perm_ret()
    pairs = []
    for X in (qa, ka):
        Xs = X[:, pa]
        for j in range(4):
            pairs.append((X[:, j * 128:(j + 1) * 128], Xs[:, j * 128:(j + 1) * 128]))
    for X in (qr, kr):
        Xs = X[:, prr]
        for j in range(2):
            pairs.append((X[:, j * 128:(j + 1) * 128], Xs[:, j * 128:(j + 1) * 128]))
    pairs.append((gr[:, 0:128], gr[:, 128:256]))
    pairs.append((gr[:, 256:384], gr[:, 384:512]))
    win = np.concatenate([lay_wgu(a, b) for a, b in pairs], 0)
    wtok = lay_kmajor(np.concatenate([va, vr], 1))
    return np.ascontiguousarray(win), wtok


def build_program(S, stages):
    k = K(S)
    nc = k.nc
    inp = lambda n, shp, dt=F32: nc.dram_tensor(n, list(shp), dt, kind="ExternalInput").ap()
    xT = inp("xT", [D, S])
    pos = inp("pos", [S], I32)
    vecs_d = inp("vecs", [128, NV])
    cst_d = inp("cst", [128, NCST])
    wgu = {}
    wd = {}
    for l in range(2):
        for f in range(2):
            if ("ffn%d_%d" % (f + 1, l)) in stages:
                wgu[(l, f)] = inp("wgu_%d_%d" % (l, f), [NF, 128, 2048])
                wd[(l, f)] = inp("wd_%d_%d" % (l, f), [8, 128, NF * 128])
    out = nc.dram_tensor("outT", [D, S], F32, kind="ExternalOutput").ap()
    xs = out
    k.arena = k.dram("arena", [40 * 128 * S], BF16)
    k.arena_off = 0
    ocatT = k.carve([8, 128, S])
    k.arena_base = k.arena_off
    if "ab" in stages:
        win_ab = inp("win_ab", [14, 128, 2048])
        wtok_ab = inp("wtok_ab", [128, 8, 1024])
        wout_ab = inp("wout_ab", [128, 8, 1024])
        k.arena_off = k.arena_base
        qkT = k.carve([12, 128, S])
        sgT = k.carve([4, 128, S])
        vtok = k.carve([S, 1024])
    if "c" in stages:
        win_c = inp("win_c", [NPAIR_C, 128, 2048])
        wtok_c = inp("wtok_c", [128, 8, 1536])
        wout_c = inp("wout_c", [128, 8, 1024])
        w2aug = inp("w2aug", [17, 1024])
    last = stages[-1]
    cur = xT
    for st in stages:
        dst = out if st == last else xs
        if st.startswith("ffn"):
            f = int(st[3]) - 1
            l = int(st[5])
            with k.phase():
                alloc_common(k, vecs_d, NV)
                pre = None
                if f == 1 and l == 0 and "ab" in stages:
                    pre = (ocatT, load_wout(k, wout_ab))
                if f == 1 and l == 1 and "c" in stages:
                    pre = (ocatT, load_wout(k, wout_c))
                gcol = (V_FFN1, V_FFN2)[f][l]
                if st == "ffn2_1":
                    emit_ffn(k, l, f, cur, None, wgu[(l, f)], wd[(l, f)], gcol, final_gain_col=V_FINAL,
                             out_dst=out, pre=pre)
                else:
                    emit_ffn(k, l, f, cur, dst, wgu[(l, f)], wd[(l, f)], gcol, pre=pre)
            cur = dst
        elif st == "ab":
            with k.phase():
                pairs = [("ropeA", qkT[j], None) for j in range(8)] + [("ropeR", qkT[8 + j], None) for j in range(4)]
                pairs += [("silu", sgT[0], sgT[1]), ("silu", sgT[2], sgT[3])]
                emit_inproj(k, cur, V_AB, win_ab, pairs, wtok_ab, 1024,
                            [(0, 512, vtok[:, 0:512]), (512, 512, vtok[:, 512:1024])], pos, cst_d, vecs_d, NV)
            import os
            sub = os.environ.get("AB_SUB", "da,ret")
            if "da" in sub:
              with k.phase():
                emit_diffattn(k, qkT, vtok[:, 0:512], ocatT, vecs_d, NV, V_LQ, V_DANORM, 0.2)
            if "ret" in sub:
              with k.phase():
                emit_retention(k, qkT, vtok[:, 512:1024], sgT, ocatT, vecs_d, NV, cst_d, V_RLF, V_RLB, V_RETNORM)
            if st == last:
                raise ValueError("mixer cannot be last stage")
        elif st == "c":
            emit_gla_all(k, cur, ocatT, win_c, wtok_c, w2aug, pos, cst_d, vecs_d)
    k.gstack.close()
    return k


NPAIR_C = 9


def host_inputs(inputs, b, stages, S):
    g = lambda n: np.asarray(inputs[n], np.float32)
    m = {}
    m["xT"] = np.ascontiguousarray(g("x")[b, :S].T)
    m["pos"] = np.ascontiguousarray(np.asarray(inputs["positions"], np.int32)[:S])
    return m


def host_shared(inputs, stages):
    g = lambda n: np.asarray(inputs[n], np.float32)
    m = {}
    vecs = np.zeros((128, NV), np.float32)
    for l in range(2):
        vecs[:, V_FFN1[l]:V_FFN1[l] + 8] = lay_vec(g("ffn1_norm")[l])
        vecs[:, V_FFN2[l]:V_FFN2[l] + 8] = lay_vec(g("ffn2_norm")[l])
    vecs[:, V_AB:V_AB + 8] = lay_vec(g("ab_norm")[0])
    vecs[:, V_C:V_C + 8] = lay_vec(g("c_norm")[0])
    vecs[:, V_FINAL:V_FINAL + 8] = lay_vec(g("final_norm"))
    vecs[:, V_DANORM] = g("da_norm")[0]
    vecs[:, V_RETNORM] = g("ret_norm")[0]
    vecs[:, V_GLANORM:V_GLANORM + 2] = lay_vec(g("gla_norm")[0])
    vecs[:, V_RLF:V_RLF + 4] = g("ret_logit_f")[0][None, :]
    vecs[:, V_RLB:V_RLB + 4] = g("ret_logit_b")[0][None, :]
    for i, n in enumerate(("da_lq1", "da_lk1", "da_lq2", "da_lk2")):
        vecs[:, V_LQ + 64 * i:V_LQ + 64 * (i + 1)] = g(n)[0][None, :]
    m["vecs"] = vecs
    m["cst"] = build_cst()
    for l in range(2):
        for f in range(2):
            if ("ffn%d_%d" % (f + 1, l)) in stages:
                names = (("ffn1_w_gate", "ffn1_w_up", "ffn1_w_down"), ("ffn2_w_gate", "ffn2_w_up", "ffn2_w_down"))[f]
                m["wgu_%d_%d" % (l, f)] = lay_wgu(g(names[0])[l], g(names[1])[l])
                m["wd_%d_%d" % (l, f)] = lay_wd(g(names[2])[l])
    if "ab" in stages:
        m["win_ab"], m["wtok_ab"] = lay_win_ab(g("ab_w_in")[0])
        m["wout_ab"] = lay_kmajor(g("ab_w_out")[0])
    if "c" in stages:
        m.update(lay_gla(inputs))
    return m


ALL_STAGES = ["ffn1_0", "ab", "ffn2_0", "ffn1_1", "c", "ffn2_1"]


def run(inputs, stages, S, ncores, trace=False):
    k = build_program(S, stages)
    shared = host_shared(inputs, stages)
    in_maps = []
    for b in range(ncores):
        m = dict(shared)
        m.update(host_inputs(inputs, b, stages, S))
        in_maps.append(m)
    res = run_bass_kernel_spmd(k.nc, in_maps, core_ids=list(range(ncores)), trace=trace)
    outs = [np.ascontiguousarray(r["outT"].T) for r in res.results]
    return np.stack(outs, 0), res


def lay_gla(inputs):
    g = lambda n: np.asarray(inputs[n], np.float32)
    W = g("c_w_in")[0]
    q, kk, v, gg = W[:, 0:512], W[:, 512:1024], W[:, 1024:2048], W[:, 2048:3072]
    lr = np.zeros((1024, 128), np.float32)
    lr[:, 0:32] = W[:, 3072:3104]
    zero = np.zeros((1024, 128), np.float32)
    pairs = [(q[:, 0:128], q[:, 128:256]), (q[:, 256:384], q[:, 384:512]),
             (kk[:, 0:128], kk[:, 128:256]), (kk[:, 256:384], kk[:, 384:512])]
    for j in range(4):
        pairs.append((gg[:, j * 256:j * 256 + 128], gg[:, j * 256 + 128:j * 256 + 256]))
    pairs.append((lr, zero))
    win = np.concatenate([lay_wgu(a, b) for a, b in pairs], 0)
    wtok = lay_kmajor(np.concatenate([v, kk], 1))
    w2aug = np.zeros((17, 1024), np.float32)
    w2aug[0:16, 0:512] = g("gla_w2_f")[0]
    w2aug[16, 0:512] = g("gla_b_f")[0]
    w2aug[0:16, 512:1024] = g("gla_w2_b")[0]
    w2aug[16, 512:1024] = g("gla_b_b")[0]
    return {"win_c": np.ascontiguousarray(win), "wtok_c": wtok, "w2aug": w2aug,
            "wout_c": lay_kmajor(g("c_w_out")[0])}


def emit_gla_all(k, x_src, ocatT, win_c, wtok_c, w2aug_d, pos, cst_d, vecs_d):
    S = k.S
    k.arena_off = k.arena_base
    gqT = k.carve([4, 128, S])
    gkT = k.carve([4, 128, S])
    gsgT = k.carve([8, 128, S])
    lrT = k.carve([128, S])
    vtokc = k.carve([S, 1024])
    ktokc = k.carve([S, 512])
    with k.phase():
        pairs = [("copy", gqT[0], gqT[1]), ("copy", gqT[2], gqT[3]), ("copy", gkT[0], gkT[1]), ("copy", gkT[2], gkT[3])]
        pairs += [("silu", gsgT[2 * j], gsgT[2 * j + 1]) for j in range(4)]
        pairs += [("copy", lrT, None)]
        emit_inproj(k, x_src, V_C, win_c, pairs, wtok_c, 1536,
                    [(0, 512, vtokc[:, 0:512]), (512, 512, vtokc[:, 512:1024]), (1024, 512, ktokc)],
                    pos, cst_d, vecs_d, NV)
    import os
    if os.environ.get("C_SUB", "gla") == "gla":
      with k.phase():
        emit_gla(k, gqT, gkT, gsgT, lrT, vtokc, ktokc, ocatT, w2aug_d, cst_d, vecs_d)


def emit_gla(k, gqT, gkT, gsgT, lrT, vtokc, ktokc, ocatT, w2aug_d, cst_d, vecs_d):
    P = k.P
    S = k.S
    NT = S // 128
    NC = S // 64
    NQ = S // 512
    QS = 128 ** -0.5
    qT = k.sb("qT", [128, S], BF16)
    kT = k.sb("kT", [128, S], BF16)
    V = k.sb("V", [128, NT, 256], BF16)
    ktok = k.sb("ktok", [128, NT, 128], BF16)
    sg = k.sb("sg", [128, 2, S], BF16)
    lrA = k.sb("lrA", [17, 2, S], BF16)
    w2s = k.sb("w2s", [17, 1024], F32)
    w2b = k.sb("w2b", [17, 1024], BF16)
    qg = k.sb("qg", [128, S], BF16)
    AT = k.sb("AT", [128, NT, 128], BF16)
    Sb = k.sb("Sb", [128, NC, 256], BF16)
    Sf = k.sb("Sf", [128, 2, 256], F32)
    OF = k.sb("OF", [128, 2, S], F32)
    vecs = k.sb("vecs", [128, NV], F32)
    cst = k.sb("cst", [128, NCST], F32)
    cb = k.sb("cb", [128, 6, 128], BF16)
    ones = k.sb("ones", [128, 128], BF16)
    eps_col = k.sb("epsc", [128, 1], F32)
    one_col = k.sb("onec", [128, 1], F32)
    ez = k.sb("ez", [128, 8, 128], F32)
    la = k.sb("la", [128, 8, 128], BF16)
    EcEw = k.sb("EcEw", [128, 8, 256], F32)
    Ec = EcEw[:, :, 0:128]
    En = k.sb("En", [128, 8, 128], F32)
    Ew = EcEw[:, :, 128:256]
    kneg = k.sb("kneg", [128, 8, 128], BF16)
    klast = k.sb("klast", [128, 8, 128], BF16)
    Oe = k.sb("Oe", [128, 2, 512], F32)
    sqb = k.sb("sqb", [128, 2, 512], BF16)
    rs = k.sb("rs", [128, 512], F32)
    tt_ = k.sb("tt", [128, 512], F32)
    ob = k.sb("ob", [128, 2, 512], BF16)
    ps = k.psum("ps", [128, 8, 512], F32)
    P.dma("sp", "const", lambda e: e.dma_start(out=vecs[:, :], in_=vecs_d), [], [("vecs",)])
    P.dma("sp", "const", lambda e: e.dma_start(out=cst[:, :], in_=cst_d), [], [("cst",)])
    P.dma("sp", "const", lambda e: e.dma_start(out=w2s[:, :], in_=w2aug_d), [], [("w2s",)])
    P.dve(lambda e: e.tensor_copy(w2b[:, :], w2s[:, :]), [("w2s",)], [("w2b",)])
    P.dve(lambda e: e.memset(ones[:, :], 1.0), [], [("ones",)])
    P.dve(lambda e: e.memset(eps_col[:, :], EPS), [], [("eps",)])
    P.dve(lambda e: e.memset(one_col[:, :], 1.0), [], [("onec",)])
    for i, col in enumerate((C_TRI_INC, C_TRI_SUF, C_TRI_SSUF, C_TRI_SPRE)):
        P.dve(lambda e, i=i, col=col: e.tensor_copy(cb[:, i, :], cst[:, col:col + 128]), [("cst",)], [("cb",)])
    P.dve(lambda e: e.memset(lrA[:, :, :], 1.0), [], [("lrA",)])
    for d in range(2):
        P.dma("sp", "const", lambda e, d=d: e.dma_start(out=lrA[0:16, d, :], in_=lrT[d * 16:(d + 1) * 16, :]),
              [], [("lrA",)])
    qrr = [0]

    def qalloc(n):
        i = ((qrr[0] + n - 1) // n) * n % 32
        qrr[0] = i + n
        return i // 4, (i % 4) * 128, [("psq", i + j) for j in range(n)]

    def pk(b):
        return [("psq", b * 4 + q) for q in range(4)]

    for h in range(4):
        P.dma("sp", "hq", lambda e, h=h: e.dma_start(out=qT[:, :], in_=gqT[h]), [], [("qT",)])
        P.dma("sp", "hk", lambda e, h=h: e.dma_start(out=kT[:, :], in_=gkT[h]), [], [("kT",)])
        dma_tiles(k, "hv", V, vtokc[:, h * 256:(h + 1) * 256], NT, ("V",))
        dma_tiles(k, "hkt", ktok, ktokc[:, h * 128:(h + 1) * 128], NT, ("ktok",))
        for e2 in range(2):
            P.dma("sp", "hg", lambda e, h=h, e2=e2: e.dma_start(out=sg[:, e2, :], in_=gsgT[2 * h + e2]), [], [("sg", e2)])
        for d in range(2):
            tri_c, tri_w = (0, 2) if d == 0 else (1, 3)
            mcol = C_MASKLO if d == 0 else C_MASKUP
            tiles = list(range(NT)) if d == 0 else list(range(NT - 1, -1, -1))
            c_init = 0 if d == 0 else NC - 1
            P.dve(lambda e, c_init=c_init: e.memset(Sf[:, 0, :], 0.0), [], [("Sf", 0)])
            P.pool(lambda e, c_init=c_init: e.memset(Sb[:, c_init, :], 0.0), [], [("Sb", c_init)])
            sslot = 0
            def stageA(i, tt):
                u = i % 8
                ts = slice(tt * 128, (tt + 1) * 128)
                bz, cz, kz = qalloc(4)
                mm(k, ps[:, bz, cz:cz + 128], lrA[:, d, ts], w2b[:, d * 512 + h * 128:d * 512 + (h + 1) * 128], True, True,
                   [("lrA",), ("w2b",)], kz)
                P.act(lambda e, bz=bz, cz=cz, u=u: e.activation(ez[:, u, :], ps[:, bz, cz:cz + 128], AF.Exp, scale=-1.0),
                      kz, [("ez", u)])
                P.act(lambda e, u=u: e.activation(ez[:, u, :], ez[:, u, :], AF.Ln, bias=one_col[:, 0:1]),
                      [("ez", u), ("onec",)], [("ez", u)])
                P.dve(lambda e, u=u: e.tensor_scalar(la[:, u, :], ez[:, u, :], -1.0 / 16.0, None, op0=ALU.mult),
                      [("ez", u)], [("la", u)])

            def stageB(i, tt):
                u = i % 8
                ts = slice(tt * 128, (tt + 1) * 128)
                bc, cc_, kc_ = qalloc(4)
                mm(k, ps[:, bc, cc_:cc_ + 128], la[:, u, :], cb[:, tri_c, :], True, True, [("la", u), ("cb",)], kc_)
                mm(k, ps[:, bc, cc_ + 128:cc_ + 256], cb[:, tri_w, :], la[:, u, :], True, True, [("la", u), ("cb",)], kc_)
                P.act(lambda e, bc=bc, cc_=cc_, u=u: e.activation(EcEw[:, u, :], ps[:, bc, cc_:cc_ + 256], AF.Exp),
                      kc_, [("Ec", u), ("Ew", u)])
                P.act(lambda e, bc=bc, cc_=cc_, u=u: e.activation(En[:, u, :], ps[:, bc, cc_:cc_ + 128], AF.Exp, scale=-1.0),
                      kc_, [("En", u)])
                P.dve(lambda e, u=u, ts=ts: e.scalar_tensor_tensor(qg[:, ts], Ec[:, u, :], QS, qT[:, ts],
                                                                   op0=ALU.mult, op1=ALU.mult),
                      [("Ec", u), ("qT",)], [("qg", tt)])
                P.pool(lambda e, u=u, ts=ts: e.tensor_tensor(kneg[:, u, :], En[:, u, :], kT[:, ts], op=ALU.mult),
                       [("En", u), ("kT",)], [("kneg", u)])
                P.pool(lambda e, u=u, tt=tt: e.tensor_tensor(klast[:, u, :], Ew[:, u, :], ktok[:, tt, :], op=ALU.mult),
                       [("Ew", u), ("ktok",)], [("klast", u)])

            def stageC(i, tt):
                nonlocal sslot
                u = i % 8
                ts = slice(tt * 128, (tt + 1) * 128)
                bs, cs_, ks_ = qalloc(4)
                mm(k, ps[:, bs, cs_:cs_ + 128], kneg[:, u, :], qg[:, ts], True, True, [("kneg", u), ("qg", tt)], ks_)
                P.dve(lambda e, bs=bs, cs_=cs_, tt=tt, mcol=mcol: e.tensor_tensor(AT[:, tt, :], ps[:, bs, cs_:cs_ + 128],
                                                                                  cst[:, mcol:mcol + 128], op=ALU.mult),
                      ks_ + [("cst",)], [("AT", tt)])
                chunks = (0, 1) if d == 0 else (1, 0)
                for cc in chunks:
                    c = tt * 2 + cc
                    cn = c + 1 if d == 0 else c - 1
                    if cn < 0 or cn >= NC:
                        continue
                    pr = slice(cc * 64, cc * 64 + 64)
                    bk, ck_, kk_ = qalloc(4)
                    mm(k, ps[:, bk, ck_:ck_ + 256], klast[pr, u, :], V[pr, tt, :], True, True, [("klast", u), ("V",)], kk_)
                    dcol = (cc * 64 + 63) if d == 0 else (cc * 64)
                    so, sn = sslot % 2, (sslot + 1) % 2
                    sslot += 1
                    P.dve(lambda e, bk=bk, ck_=ck_, u=u, dcol=dcol, so=so, sn=sn: e.scalar_tensor_tensor(
                        Sf[:, sn, :], Sf[:, so, :], Ec[:, u, dcol:dcol + 1], ps[:, bk, ck_:ck_ + 256],
                        op0=ALU.mult, op1=ALU.add),
                        [("Sf", so), ("Ec", u)] + kk_, [("Sf", sn)])
                    if cc == 0:
                        P.act(lambda e, sn=sn, cn=cn: e.copy(Sb[:, cn, :], Sf[:, sn, :]), [("Sf", sn)], [("Sb", cn)])
                    else:
                        P.dve(lambda e, sn=sn, cn=cn: e.tensor_copy(Sb[:, cn, :], Sf[:, sn, :]), [("Sf", sn)], [("Sb", cn)])

            nt_ = len(tiles)
            for step in range(nt_ + 4):
                if step < nt_:
                    stageA(step, tiles[step])
                if 2 <= step < nt_ + 2:
                    stageB(step - 2, tiles[step - 2])
                if step >= 4:
                    stageC(step - 4, tiles[step - 4])
            for g in range(NQ):
                gs = slice(g * 512, (g + 1) * 512)
                be = (k.bank(), k.bank())
                for j in range(4):
                    tt = g * 4 + j
                    for e2 in range(2):
                        js = slice(j * 128, (j + 1) * 128)
                        mm(k, ps[:, be[e2], js], V[:, tt, e2 * 128:(e2 + 1) * 128], AT[:, tt, :], True, False,
                           [("V",), ("AT", tt)], pk(be[e2]))
                        for cc in range(2):
                            c = tt * 2 + cc
                            cs = slice(c * 64, (c + 1) * 64)
                            jc = slice(j * 128 + cc * 64, j * 128 + cc * 64 + 64)
                            mm(k, ps[:, be[e2], jc], Sb[:, c, e2 * 128:(e2 + 1) * 128], qg[:, cs], False, cc == 1,
                               [("Sb", c), ("qg", tt)], pk(be[e2]))
                if d == 0:
                    for e2 in range(2):
                        P.act(lambda e, e2=e2, gs=gs, b=be[e2]: e.copy(OF[:, e2, gs], ps[:, b, :]),
                              pk(be[e2]), [("OF", e2, g)])
                else:
                    for e2 in range(2):
                        P.dve(lambda e, e2=e2, gs=gs, b=be[e2]: e.tensor_tensor(Oe[:, e2, :], OF[:, e2, gs], ps[:, b, :],
                                                                                op=ALU.add),
                              pk(be[e2]) + [("OF", e2, g)], [("Oe", e2)])
                    emit_pnorm(k, [(Oe[:, 0, :], ("Oe", 0)), (Oe[:, 1, :], ("Oe", 1))], 512, ps, ones, sqb, rs, 2,
                               256.0, eps_col, pk=pk)
                    for e2 in range(2):
                        P.dve(lambda e, e2=e2: e.scalar_tensor_tensor(
                            tt_[:, :], Oe[:, e2, :], vecs[:, V_GLANORM + e2:V_GLANORM + e2 + 1], rs[:, :],
                            op0=ALU.mult, op1=ALU.mult), [("Oe", e2), ("vecs",), ("pn_rs",)], [("tt",)])
                        k.uid += 1
                        oo = k.uid % 2
                        P.dve(lambda e, oo=oo, e2=e2, gs=gs: e.tensor_tensor(ob[:, oo, :], tt_[:, :], sg[:, e2, gs],
                                                                            op=ALU.mult),
                              [("tt",), ("sg", e2)], [("ob", oo)])
                        P.dma("sp", ("ob", oo), lambda e, oo=oo, h=h, e2=e2, gs=gs: e.dma_start(
                            out=ocatT[2 * h + e2][:, gs], in_=ob[:, oo, :]), [("ob", oo)], [("ocat", h, e2, g)])


def run_core_inputs(inputs, stages, S, ncores, x_override=None, trace=False):
    k = build_program(S, stages)
    shared = host_shared(inputs, stages)
    in_maps = []
    for b in range(ncores):
        m = dict(shared)
        m.update(host_inputs(inputs, b, stages, S))
        if x_override is not None:
            m["xT"] = x_override[b]
        in_maps.append(m)
    res = run_bass_kernel_spmd(k.nc, in_maps, core_ids=list(range(ncores)), trace=trace)
    return [r["outT"] for r in res.results], res


LAUNCHES = [ALL_STAGES]


def kernel(**inputs):
    xo = None
    for stages in LAUNCHES:
        xo, _ = run_core_inputs(inputs, stages, 4096, 8, x_override=xo)
    return np.stack([np.ascontiguousarray(o.T) for o in xo], 0).astype(np.float32)
```
